# Optimizing a Trainium2 kernel written in Bass

```python
import jax, jax.numpy as jnp
from jax import lax
import numpy as np

D_MODEL = 1024
BATCH = 8
SEQ = 8192
DEPTH = 1

CHUNK = 64
Q_BLOCK = 128
D_MIX = D_MODEL
MLA_HEADS = 4
MLA_NOPE = 128
MLA_ROPE = 64
MLA_V = 128
MLA_WIDTH = MLA_HEADS * MLA_V
Q_LORA = 384
KV_LORA = 256
ROPE_BASE = 10000.0
RWKV_HEAD = 64
RWKV_WIDTH = D_MIX - MLA_WIDTH
RWKV_HEADS = RWKV_WIDTH // RWKV_HEAD
DECAY_LORA = 64
AAA_LORA = 64
GATE_LORA = 128
MLA_COLS = Q_LORA + KV_LORA + MLA_ROPE
RWKV_COLS = 3 * RWKV_WIDTH + DECAY_LORA + AAA_LORA + GATE_LORA
IN_COLS = MLA_COLS + RWKV_COLS
D_FF = 2816
NORM_EPS = 1e-6
GN_EPS = 64e-5
N_MOD = 9

kernel_name = "hybrid_mla_rwkv7_macaron_adaln"


def rmsnorm(x, g, eps=NORM_EPS):
    x32 = x.astype(jnp.float32)
    y = x32 * lax.rsqrt(jnp.mean(x32 * x32, axis=-1, keepdims=True) + eps)
    return (y * g.astype(jnp.float32)).astype(x.dtype)


def modulate(u, shift, scale):
    return u * (1.0 + scale[:, None, :]) + shift[:, None, :]


def swiglu(u, w_gate, w_up, w_down):
    return (jax.nn.silu(u @ w_gate) * (u @ w_up)) @ w_down


def apply_rot(x, cos, sin):
    half = x.shape[-1] // 2
    x32 = x.astype(jnp.float32)
    x1, x2 = x32[..., :half], x32[..., half:]
    return jnp.concatenate([x1 * cos - x2 * sin, x2 * cos + x1 * sin], axis=-1).astype(x.dtype)


def mla_group(p_q, p_kv, p_kr, cos, sin, q_norm_g, w_uq, kv_norm_g, w_ukv):
    B, S, _ = p_q.shape
    q = (rmsnorm(p_q, q_norm_g) @ w_uq).reshape(B, S, MLA_HEADS, MLA_NOPE + MLA_ROPE)
    q_nope = q[..., :MLA_NOPE]
    q_rope = apply_rot(q[..., MLA_NOPE:], cos[:, :, None, :], sin[:, :, None, :])
    kv = (rmsnorm(p_kv, kv_norm_g) @ w_ukv).reshape(B, S, MLA_HEADS, MLA_NOPE + MLA_V)
    k_nope, v = kv[..., :MLA_NOPE], kv[..., MLA_NOPE:]
    k_rope = apply_rot(p_kr, cos, sin)
    nb = S // Q_BLOCK
    qn_b = q_nope.reshape(B, nb, Q_BLOCK, MLA_HEADS, MLA_NOPE).transpose(1, 0, 2, 3, 4)
    qr_b = q_rope.reshape(B, nb, Q_BLOCK, MLA_HEADS, MLA_ROPE).transpose(1, 0, 2, 3, 4)
    key_chunk = jnp.arange(S) // CHUNK
    scale = (MLA_NOPE + MLA_ROPE) ** -0.5

    def block(args):
        qn, qr, i = args
        s = (jnp.einsum('bqhd,bkhd->bhqk', qn, k_nope)
             + jnp.einsum('bqhd,bkd->bhqk', qr, k_rope)).astype(jnp.float32) * scale
        q_chunk = (i * Q_BLOCK + jnp.arange(Q_BLOCK)) // CHUNK
        mask = key_chunk[None, :] <= q_chunk[:, None]
        s = jnp.where(mask[None, None], s, -jnp.inf)
        pr = jax.nn.softmax(s, axis=-1).astype(v.dtype)
        return jnp.einsum('bhqk,bkhd->bqhd', pr, v)

    o = lax.map(block, (qn_b, qr_b, jnp.arange(nb)))
    return o.transpose(1, 0, 2, 3, 4).reshape(B, S, MLA_WIDTH)


def rwkv7_group(p, shift_mix, w0, w2, a0, a2, g2, k_k, k_a, r_k, ln_w, ln_b):
    B, S, _ = p.shape
    C, H, N = RWKV_WIDTH, RWKV_HEADS, RWKV_HEAD
    f32 = jnp.float32
    p_prev = jnp.pad(p, ((0, 0), (1, 0), (0, 0)))[:, :-1]
    p = p + (p_prev - p) * shift_mix
    r, k, v = p[..., :C], p[..., C:2 * C], p[..., 2 * C:3 * C]
    o = 3 * C
    xw = p[..., o:o + DECAY_LORA]
    xa = p[..., o + DECAY_LORA:o + DECAY_LORA + AAA_LORA]
    xg = p[..., o + DECAY_LORA + AAA_LORA:]
    w_log = -jax.nn.softplus(-(w0 + jnp.tanh(xw) @ w2)) - 0.5
    decay = jnp.exp(-jnp.exp(w_log.astype(f32)))
    a = jax.nn.sigmoid(a0 + xa @ a2)
    g = jax.nn.sigmoid(xg) @ g2
    kk = (k * k_k).astype(f32).reshape(B, S, H, N)
    kk = kk / jnp.maximum(jnp.sqrt(jnp.sum(kk * kk, axis=-1, keepdims=True)), 1e-12)
    k = k * (1.0 + (a - 1.0) * k_a)
    heads = lambda t: t.astype(f32).reshape(B, S, H, N)
    r_h, k_h, v_h, a_h = heads(r), heads(k), heads(v), heads(a)
    tm = lambda t: jnp.moveaxis(t.reshape(B, S, H, N), 1, 0)

    def step(state, inp):
        r_t, w_t, k_t, v_t, aa_t, bb_t = inp
        sa = jnp.einsum('bhvk,bhk->bhv', state, aa_t)
        state = (state * w_t[:, :, None, :] + sa[..., None] * bb_t[:, :, None, :]
                 + v_t[..., None] * k_t[:, :, None, :])
        return state, jnp.einsum('bhvk,bhk->bhv', state, r_t)

    s0 = jnp.zeros((B, H, N, N), f32)
    _, y = lax.scan(step, s0, (tm(r_h), tm(decay), tm(k_h), tm(v_h), tm(-kk), tm(kk * a_h)))
    y = jnp.moveaxis(y, 0, 1)
    mu = jnp.mean(y, axis=-1, keepdims=True)
    var = jnp.mean(jnp.square(y - mu), axis=-1, keepdims=True)
    y = ((y - mu) * lax.rsqrt(var + GN_EPS)).reshape(B, S, C) * ln_w.astype(f32) + ln_b.astype(f32)
    bonus = jnp.sum(r_h * k_h * r_k.astype(f32), axis=-1, keepdims=True) * v_h
    out = (y + bonus.reshape(B, S, C)) * g.astype(f32)
    return out.astype(p.dtype)


def setup_inputs(seed: int = 0) -> dict:
    key = jax.random.key(seed)
    ks = iter(jax.random.split(key, 48))
    L, D, C = DEPTH, D_MODEL, RWKV_WIDTH

    def nrm(shape, scale):
        return scale * jax.random.normal(next(ks), shape, jnp.float32)

    def gain(shape):
        return 1.0 + nrm(shape, 0.02)

    def unif(shape, lo, hi):
        return jax.random.uniform(next(ks), shape, jnp.float32, lo, hi)

    x = nrm((BATCH, SEQ, D), 1.0)
    c = nrm((BATCH, D), 1.0)
    offset = jax.random.randint(next(ks), (BATCH, 1), 0, 4096, dtype=jnp.int32)
    positions = offset + jnp.arange(SEQ, dtype=jnp.int32)[None, :]
    return {
        "x": x,
        "c": c,
        "positions": positions,
        "w_mod": nrm((L, D, N_MOD * D), 0.5 * D ** -0.5),
        "b_mod": nrm((L, N_MOD * D), 0.02),
        "ffn1_norm_g": gain((L, D)),
        "ffn1_w_gate": nrm((L, D, D_FF), D ** -0.5),
        "ffn1_w_up": nrm((L, D, D_FF), D ** -0.5),
        "ffn1_w_down": nrm((L, D_FF, D), D_FF ** -0.5),
        "mix_norm_g": gain((L, D)),
        "w_in": nrm((L, D, IN_COLS), D ** -0.5),
        "q_norm_g": gain((L, Q_LORA)),
        "w_uq": nrm((L, Q_LORA, MLA_HEADS * (MLA_NOPE + MLA_ROPE)), Q_LORA ** -0.5),
        "kv_norm_g": gain((L, KV_LORA)),
        "w_ukv": nrm((L, KV_LORA, MLA_HEADS * (MLA_NOPE + MLA_V)), KV_LORA ** -0.5),
        "attn_out_norm_g": gain((L, MLA_WIDTH)),
        "rwkv_shift_mix": unif((L, RWKV_COLS), 0.0, 1.0),
        "rwkv_w0": unif((L, C), -5.5, 0.5),
        "rwkv_w2": nrm((L, DECAY_LORA, C), 0.1 * DECAY_LORA ** -0.5),
        "rwkv_a0": nrm((L, C), 0.1),
        "rwkv_a2": nrm((L, AAA_LORA, C), 0.5 * AAA_LORA ** -0.5),
        "rwkv_g2": nrm((L, GATE_LORA, C), GATE_LORA ** -0.5),
        "rwkv_k_k": 0.85 + nrm((L, C), 0.05),
        "rwkv_k_a": 1.0 + nrm((L, C), 0.05),
        "rwkv_r_k": nrm((L, RWKV_HEADS, RWKV_HEAD), 0.1),
        "rwkv_ln_w": gain((L, C)),
        "rwkv_ln_b": nrm((L, C), 0.02),
        "w_out": nrm((L, D_MIX, D), D_MIX ** -0.5),
        "ffn2_norm_g": gain((L, D)),
        "ffn2_w_gate": nrm((L, D, D_FF), D ** -0.5),
        "ffn2_w_up": nrm((L, D, D_FF), D ** -0.5),
        "ffn2_w_down": nrm((L, D_FF, D), D_FF ** -0.5),
        "final_norm_g": gain((D,)),
    }


def reference(x, c, positions, w_mod, b_mod, ffn1_norm_g, ffn1_w_gate, ffn1_w_up, ffn1_w_down,
              mix_norm_g, w_in, q_norm_g, w_uq, kv_norm_g, w_ukv, attn_out_norm_g,
              rwkv_shift_mix, rwkv_w0, rwkv_w2, rwkv_a0, rwkv_a2, rwkv_g2, rwkv_k_k, rwkv_k_a,
              rwkv_r_k, rwkv_ln_w, rwkv_ln_b, w_out, ffn2_norm_g, ffn2_w_gate, ffn2_w_up,
              ffn2_w_down, final_norm_g):
    half = MLA_ROPE // 2
    inv_freq = ROPE_BASE ** (-jnp.arange(half, dtype=jnp.float32) / half)
    ang = positions.astype(jnp.float32)[..., None] * inv_freq
    cos, sin = jnp.cos(ang), jnp.sin(ang)
    c_act = jax.nn.silu(c)
    h = x
    for l in range(DEPTH):
        mod = c_act @ w_mod[l] + b_mod[l]
        sh1, sc1, gt1, sh2, sc2, gt2, sh3, sc3, gt3 = jnp.split(mod, N_MOD, axis=-1)
        u = modulate(rmsnorm(h, ffn1_norm_g[l]), sh1, sc1)
        h = h + 0.5 * gt1[:, None, :] * swiglu(u, ffn1_w_gate[l], ffn1_w_up[l], ffn1_w_down[l])
        u = modulate(rmsnorm(h, mix_norm_g[l]), sh2, sc2)
        proj = u @ w_in[l]
        p_q = proj[..., :Q_LORA]
        p_kv = proj[..., Q_LORA:Q_LORA + KV_LORA]
        p_kr = proj[..., Q_LORA + KV_LORA:MLA_COLS]
        p_rw = proj[..., MLA_COLS:]
        y_a = mla_group(p_q, p_kv, p_kr, cos, sin, q_norm_g[l], w_uq[l], kv_norm_g[l], w_ukv[l])
        y_a = rmsnorm(y_a, attn_out_norm_g[l])
        y_b = rwkv7_group(p_rw, rwkv_shift_mix[l], rwkv_w0[l], rwkv_w2[l], rwkv_a0[l], rwkv_a2[l],
                          rwkv_g2[l], rwkv_k_k[l], rwkv_k_a[l], rwkv_r_k[l], rwkv_ln_w[l], rwkv_ln_b[l])
        y = jnp.concatenate([y_a, y_b], axis=-1) @ w_out[l]
        h = h + gt2[:, None, :] * y
        u = modulate(rmsnorm(h, ffn2_norm_g[l]), sh3, sc3)
        h = h + 0.5 * gt3[:, None, :] * swiglu(u, ffn2_w_gate[l], ffn2_w_up[l], ffn2_w_down[l])
    return rmsnorm(h, final_norm_g)
```

```python
import math
from contextlib import ExitStack

import numpy as np
import concourse.bass as bass
import concourse.mybir as mybir
from concourse.bass_utils import run_bass_kernel_spmd

F32 = mybir.dt.float32
BF16 = mybir.dt.bfloat16
I32 = mybir.dt.int32
AF = mybir.ActivationFunctionType
ALU = mybir.AluOpType
AX = mybir.AxisListType

D = 1024
DFF = 2816
T = 256
NTB = T // 128
NJ = DFF // 128
KAPPA = math.exp(-0.5)
SCALE = 192.0 ** -0.5
PI = math.pi

V_CT, V_BM, V_G1, V_G2, V_G3, V_GF, V_GQ, V_GKV = 0, 8, 80, 88, 96, 104, 112, 115
V_MIXR, V_MIXK, V_MIXV, V_MIXWA, V_MIXG = 117, 121, 125, 129, 130
V_W0, V_A0, V_KK, V_KA, V_RK, V_LNW, V_LNB = 131, 135, 139, 143, 147, 151, 155
NV = 159
C_ID, C_ML, C_MU, C_MUI, C_BO, C_INVF, C_SS, C_RST = 0, 128, 256, 384, 512, 640, 641, 642
NCST = 642 + 512


import os
POOL2DVE = bool(os.environ.get('POOL2DVE'))


class Buf:
    __slots__ = ("name", "w", "wd", "r", "rd")

    def __init__(self, name):
        self.name = name
        self.w = {}
        self.wd = []
        self.r = {}
        self.rd = []


class Op:
    __slots__ = ("idx", "eng", "fn", "deps", "is_dma", "dkey", "dval", "signal", "sigval", "raw")

    def __init__(self, idx, eng, fn, is_dma, dkey, dval):
        self.idx = idx
        self.eng = eng
        self.fn = fn
        self.deps = []
        self.is_dma = is_dma
        self.dkey = dkey
        self.dval = dval
        self.signal = False
        self.sigval = 0


class Prog:
    ENGS = ("pe", "act", "dve", "pool", "sp")

    def __init__(self, nc, es):
        self.nc = nc
        self.es = es
        self.ops = []
        self.eng_ops = {e: [] for e in self.ENGS}
        self.dcount = {}
        self.nbuf = 0

    def buf(self, name=None):
        self.nbuf += 1
        return Buf(name or f"b{self.nbuf}")

    def bufs(self, n, name=None):
        return [self.buf(f"{name}{i}") for i in range(n)]

    def add(self, eng, fn, reads=(), writes=(), dkey=None):
        is_dma = dkey is not None
        if eng == "pool" and not is_dma and POOL2DVE:
            eng = "dve"
        dval = 0
        if is_dma:
            self.dcount[dkey] = self.dcount.get(dkey, 0) + 1
            dval = 16 * self.dcount[dkey]
        op = Op(len(self.ops), eng, fn, is_dma, dkey, dval)
        deps = {}

        def dep(o, raw):
            if o is op:
                return
            if (not o.is_dma) and (not is_dma) and o.eng == eng:
                if eng == "pe":
                    return
            deps[o.idx] = o

        for b in reads:
            for o in b.w.values():
                dep(o, True)
            for o in b.wd:
                dep(o, True)
        for b in writes:
            for o in b.r.values():
                dep(o, False)
            for o in b.rd:
                dep(o, False)
            for o in b.w.values():
                dep(o, False)
            for o in b.wd:
                dep(o, False)
        for b in reads:
            if is_dma:
                b.rd.append(op)
            else:
                b.r[eng] = op
        for b in writes:
            if b.r or b.rd:
                keep_r = None
                b.w = {}
                b.wd = []
                b.r = {}
                b.rd = []
            if is_dma:
                b.wd.append(op)
            else:
                b.w[eng] = op
        op.deps = list(deps.values())
        self.ops.append(op)
        self.eng_ops[eng].append(op)
        return op

    def emit(self):
        nc = self.nc
        es = self.es
        for op in self.ops:
            for d in op.deps:
                if not d.is_dma:
                    d.signal = True
        cnt = {e: 0 for e in self.ENGS}
        for op in self.ops:
            if (not op.is_dma) and op.signal:
                cnt[op.eng] += 1
                op.sigval = cnt[op.eng]
        self.stats = dict(cnt=dict(cnt), nops={e: len(v) for e, v in self.eng_ops.items()}, dmax=max(self.dcount.values()) * 16)
        esem = {e: es.enter_context(nc.semaphore("s_" + e)) for e in self.ENGS}
        dsem = {k: es.enter_context(nc.semaphore("d_" + k)) for k in self.dcount}
        block = es.enter_context(nc.Block())

        def gen(engname):
            def body(e):
                known = {}
                for op in self.eng_ops[engname]:
                    for d in op.deps:
                        if d.is_dma:
                            key, sem, val = "d_" + d.dkey, dsem[d.dkey], d.dval
                        else:
                            key, sem, val = "e_" + d.eng, esem[d.eng], d.sigval
                        if known.get(key, 0) >= val:
                            continue
                        e.wait_ge(sem, val)
                        known[key] = val
                    ins = op.fn(e)
                    if op.is_dma:
                        ins.then_inc(dsem[op.dkey], 16)
                    elif op.signal:
                        ins.then_inc(esem[op.eng], 1)
                if engname == "sp":
                    for k, n in self.dcount.items():
                        e.wait_ge(dsem[k], 16 * n)

            return body

        block.tensor(gen("pe"))
        block.vector(gen("dve"))
        block.scalar(gen("act"))
        block.gpsimd(gen("pool"))
        block.sync(gen("sp"))


def build_nc(S, stages=("ffn1", "mix", "ffn2"), dbg=()):
    NT = S // T
    nc = bass.Bass("TRN2", target_bir_lowering=False)
    es = ExitStack()
    P = Prog(nc, es)

    def din(name, shape, dt=F32):
        return nc.dram_tensor(name, list(shape), dt, kind="ExternalInput").ap()

    x_d = din("x", [S, D])
    pos_d = din("pos", [64, S], I32)
    vt_d = din("vt", [128, NV])
    cst_d = din("cst", [128, NCST])
    gat_d = din("gat", [128, 512])
    wmod_d = din("w_mod", [D, 9 * D])
    wsrc = {
        "w1g": din("w1g", [D, DFF]), "w1u": din("w1u", [D, DFF]), "w1d": din("w1d", [DFF, D]),
        "win": din("win", [D, 2496]), "wuq": din("wuq", [384, 768]), "wukv": din("wukv", [256, 1024]),
        "w2": din("w2", [64, 512]), "a2": din("a2", [64, 512]), "g2": din("g2", [128, 512]),
        "wout": din("wout", [D, D]),
        "w3g": din("w3g", [D, DFF]), "w3u": din("w3u", [D, DFF]), "w3d": din("w3d", [DFF, D]),
    }
    out_d = nc.dram_tensor("out", [S, D], F32, kind="ExternalOutput").ap()
    dbg_d = {}
    for name, shape in dbg:
        dbg_d[name] = nc.dram_tensor("dbg_" + name, list(shape), F32, kind="ExternalOutput").ap()

    wb = {k: nc.dram_tensor(k + "_bf", list(v.shape), BF16).ap() for k, v in wsrc.items()}
    wb_buf = {k: P.buf("wb_" + k) for k in wsrc}
    Kc_d = nc.dram_tensor("Kc", [4, 128, S], BF16).ap()
    Vc_d = nc.dram_tensor("Vc", [4, NT, 128, NTB * 129], BF16).ap()
    Kc_buf = [[P.buf(f"Kc{h}_{i}") for i in range(NT)] for h in range(4)]
    Vc_buf = [[P.buf(f"Vc{h}_{i}") for i in range(NT)] for h in range(4)]

    def sb(name, shape, dt=F32):
        return es.enter_context(nc.sbuf_tensor("sb_" + name, list(shape), dt))

    vt = sb("vt", [128, NV]); vt_b = P.buf("vt")
    cst = sb("cst", [128, NCST]); cst_b = P.buf("cst")
    drv = sb("drv", [128, 96]); drv_b = P.buf("drv")
    modT = sb("modT", [128, 72]); mod_b = P.buf("modT")
    cact = sb("cact", [128, 8]); cact_b = P.buf("cact")
    ident_bf = sb("ident_bf", [128, 128], BF16)
    ones_bf = sb("ones_bf", [128, 128], BF16)
    bones_bf = sb("bones_bf", [128, 128], BF16)
    bones_f = sb("bones_f", [128, 128])
    mL4 = sb("mL4", [128, 4, 128], BF16); mU4 = sb("mU4", [128, 4, 128], BF16); mUI4 = sb("mUI4", [128, 4, 128], BF16)
    id4 = sb("id4", [128, 4, 128], BF16)
    gat = sb("gat", [128, 512])
    kconst_b = P.buf("kconst")

    hT = sb("hT", [128, 8, T]); hT_b = P.bufs(8, "hT")
    uT = sb("uT", [128, 8, T], BF16); uT_b = P.bufs(8, "uT")
    sqb = sb("sqb", [128, 8, T], BF16); sqb_b = P.bufs(8, "sqb")
    tmpA = sb("tmpA", [128, T]); tmpA_b = P.buf("tmpA")
    tmpB = [sb(f"tmpB{i}", [128, T]) for i in range(2)]; tmpB_b = P.bufs(2, "tmpB")
    arena = sb("arena", [128, NJ * T], BF16)
    actT = arena[:].rearrange("p (j t) -> p j t", j=NJ); actT_b = P.bufs(NJ, "actT")
    sgt = [sb(f"sgt{i}", [128, T]) for i in range(2)]; sgt_b = P.bufs(2, "sgt")
    xin = [sb(f"xin{i}", [128, D]) for i in range(2)]; xin_b = P.bufs(2, "xin")
    ost = xin; ost_b = xin_b
    wgs = [sb(f"wgs{i}", [128, 8, 128], BF16) for i in range(2)]; wgs_b = P.bufs(2, "wgs")
    wus = [sb(f"wus{i}", [128, 8, 128], BF16) for i in range(2)]; wus_b = P.bufs(2, "wus")
    wds = [sb(f"wds{i}", [128, 4, 512], BF16) for i in range(2)]; wds_b = P.bufs(2, "wds")
    wmods = [sb(f"wmods{i}", [128, 8, 128]) for i in range(1)]; wmods_b = P.bufs(1, "wmods")


    Krc_d = nc.dram_tensor("Krc", [64, S], BF16).ap()
    Krc_buf = [P.buf(f"Krc{i}") for i in range(NT)]
    wuq_sb = sb("wuq_sb", [128, 3, 768], BF16)
    wuq_rs = sb("wuq_rs", [128, 3, 4, 64], BF16)
    wukv_sb = sb("wukv_sb", [128, 2, 1024], BF16)
    wuv_sb = sb("wuv_sb", [128, 2, 4, 128], BF16)
    wkr_sb = sb("wkr_sb", [128, 8, 128], BF16)
    wa_sb = sb("wa_sb", [128, 512], BF16)
    g2_sb = sb("g2_sb", [128, 512], BF16)
    wres_b = P.buf("wres")
    wins = [sb(f"wins{i}", [128, 8, 256], BF16) for i in range(2)]; wins_b = P.bufs(2, "wins")
    w128 = [sb(f"w128_{i}", [128, 8, 128], BF16) for i in range(2)]; w128_b = P.bufs(2, "w128")
    pq = sb("pq", [128, 3, T]); pq_b = P.bufs(3, "pq")
    qn = sb("qn", [128, 3, T], BF16); qn_b = P.bufs(3, "qn")
    pkv = sb("pkv", [128, 2, T]); pkv_b = P.bufs(2, "pkv")
    kvn = sb("kvn", [128, 2, T], BF16); kvn_b = P.bufs(2, "kvn")
    QnT = sb("QnT", [128, 4, T], BF16); QnT_b = P.bufs(4, "QnT")
    QrT = sb("QrT", [128, 4, T], BF16); QrT_b = P.bufs(4, "QrT")
    Kst = sb("Kst", [128, 4, T], BF16); Kst_b = P.buf("Kst")
    Krst = sb("Krst", [128, T], BF16); Krst_b = P.buf("Krst")
    Vst = sb("Vst", [128, 4, NTB, 129], BF16); Vst_b = P.buf("Vst")
    Kt = [sb(f"Kt{i}", [128, T], BF16) for i in range(2)]; Kt_b = P.bufs(2, "Kt")
    Krt = [sb(f"Krt{i}", [128, T], BF16) for i in range(2)]; Krt_b = P.bufs(2, "Krt")
    Vt = [sb(f"Vt{i}", [128, NTB * 129], BF16) for i in range(2)]; Vt_b = P.bufs(2, "Vt")
    PT = [sb(f"PT{i}", [128, T], BF16) for i in range(3)]; PT_b = P.bufs(3, "PT")
    posi = sb("posi", [128, T], I32); posi_b = P.buf("posi")
    ang = sb("ang", [128, T]); ang_b = P.buf("ang")
    ang2 = sb("ang2", [128, T]); ang2_b = P.buf("ang2")
    cosT = sb("cosT", [128, T]); sinS = sb("sinS", [128, T]); rope_b = P.buf("rope")
    rtmp = [sb(f"rtmp{i}", [128, T]) for i in range(4)]; rtmp_b = P.bufs(4, "rtmp")
    ya = sb("ya", [128, NTB, 512]); ya_b = P.bufs(NTB, "ya")
    yab = sb("yab", [128, NTB, 512], BF16); yab_b = P.bufs(NTB, "yab")
    yT = sb("yT", [128, 8, T], BF16); yT_b = P.bufs(8, "yT")
    ssq = sb("ssq", [128, 8]); ssq_b = P.buf("ssq")
    rec = sb("rec", [128, 8]); rec_b = P.buf("rec")


    NCH = T // 64
    Hst = sb("Hst", [128, 4, 64]); Hst_b = P.buf("Hst")
    Hb = sb("Hb", [128, 4, 64], BF16); Hb_b = P.buf("Hb")
    carry = sb("carry", [128, 14]); carry_b = P.bufs(14, "carry")
    Bt = [sb(f"Bt{i}", [128, T + 8]) for i in range(2)]; Bt_b = P.bufs(2, "Bt")
    xwxa = sb("xwxa", [128, T]); xwxa_b = P.buf("xwxa")
    xg = sb("xg", [128, T]); xg_b = P.buf("xg")
    thb = sb("thb", [128, T], BF16); xab = sb("xab", [128, T], BF16); sgx = sb("sgx", [128, T], BF16)
    lor_b = P.buf("lor")
    RT = {}
    for _n in ("rS", "kS", "vS", "sg", "Lc", "E1", "E3", "aG", "kk", "rn", "kkn", "bb", "E2", "tk", "kmod", "E4"):
        RT[_n] = (sb("r_" + _n, [128, T]), P.buf("r_" + _n))
    sqk = sb("sqk", [128, T], BF16); sqk_b = P.buf("sqk")
    Ab = sb("Ab", [128, 8, T], BF16); Ab_b = P.bufs(4, "Ab")
    Btb = sb("Btb", [128, 4, T], BF16); Btb_b = P.bufs(4, "Btb")
    Ktb = sb("Ktb", [128, 4, T], BF16); Ktb_b = P.bufs(4, "Ktb")
    Rb = sb("Rb", [128, 8, T], BF16); Rb_b = P.bufs(4, "Rb")
    vb = sqb[:, 0:4, :]; vb_b = sqb_b[0:4]
    Bhb = sqb[:, 4:8, :]; Bhb_b = sqb_b[4:8]
    Khb = sb("Khb", [128, 4, T], BF16); Khb_b = P.bufs(4, "Khb")
    Vtm = sb("Vtm", [128, NTB, 512], BF16); Vtm_b = P.bufs(NTB, "Vtm")
    Bhtm = sb("Bhtm", [128, NTB, 512], BF16); Bhtm_b = P.bufs(NTB, "Bhtm")
    Khtmc = [sb(f"Khtm{i}", [128, NTB, 512], BF16) for i in range(2)]; Khtm_b = P.bufs(NTB, "Khtm")
    bon = arena[:, 0:8 * T].bitcast(F32).rearrange("p (c t) -> p c t", c=4)
    gG = arena[:, 8 * T:16 * T].bitcast(F32).rearrange("p (c t) -> p c t", c=4)
    bon_b = [(actT_b[2 * c], actT_b[2 * c + 1]) for c in range(4)]
    gG_b = [(actT_b[8 + 2 * c], actT_b[8 + 2 * c + 1]) for c in range(4)]
    yR = sb("yR", [128, 4, T]); yR_b = P.buf("yR")
    gL = sb("gL", [128, 4, NCH]); gL_b = P.buf("gL")
    LL = []
    for _i in range(1):
        LL.append({n: (sb(f"{n}{_i}", [128, 8, 128], BF16), P.bufs(2, f"{n}{_i}")) for n in ("TtA", "AkT", "ArbT", "ArkT")})
    MM = [(sb(f"Mm{i}", [128, 4, 128], BF16), P.buf(f"Mm{i}")) for i in range(4)]
    Xb = sb("Xb", [128, 512], BF16); Xb_b = P.buf("Xb")
    Ubc = [sb(f"Ub{i}", [128, 512], BF16) for i in range(2)]; Ub_b = P.buf("Ub")

    pbank = [es.enter_context(nc.psum_tensor(f"pb{i}", [128, 512], F32)) for i in range(8)]
    pbank_b = P.bufs(8, "pb")
    rrA = [0]

    def bankA():
        k = rrA[0] % 4
        rrA[0] += 1
        return pbank[k], pbank_b[k]

    def bankB(i):
        return pbank[4 + i], pbank_b[4 + i]

    def dma(q, out, in_, reads, writes, key):
        return P.add(q, lambda e: e.dma_start(out=out, in_=in_), reads, writes, dkey=key)

    def mm(out, lhsT, rhs, start, stop, reads, writes):
        return P.add("pe", lambda e: e.matmul(out, lhsT=lhsT, rhs=rhs, start=start, stop=stop), reads, writes)

    def tr(out, in_, ident, reads, writes):
        return P.add("pe", lambda e: e.transpose(out, in_, ident), reads, writes)

    def act(out, in_, func, reads, writes, bias=0.0, scale=1.0, accum_out=None):
        if accum_out is None:
            return P.add("act", lambda e: e.activation(out=out, in_=in_, func=func, bias=bias, scale=scale), reads, writes)
        return P.add("act", lambda e: e.activation(out=out, in_=in_, func=func, bias=bias, scale=scale, accum_out=accum_out), reads, writes)

    def tt(eng, out, in0, in1, op, reads, writes):
        return P.add(eng, lambda e: e.tensor_tensor(out=out, in0=in0, in1=in1, op=op), reads, writes)

    def ts(eng, out, in0, s1, s2, op0, op1, reads, writes):
        if op1 is None:
            return P.add(eng, lambda e: e.tensor_scalar(out=out, in0=in0, scalar1=s1, scalar2=None, op0=op0), reads, writes)
        return P.add(eng, lambda e: e.tensor_scalar(out=out, in0=in0, scalar1=s1, scalar2=s2, op0=op0, op1=op1), reads, writes)

    def stt(eng, out, in0, scalar, in1, op0, op1, reads, writes):
        return P.add(eng, lambda e: e.scalar_tensor_tensor(out=out, in0=in0, scalar=scalar, in1=in1, op0=op0, op1=op1), reads, writes)

    def cp(eng, out, in_, reads, writes):
        if eng == "act":
            return P.add(eng, lambda e: e.activation(out=out, in_=in_, func=AF.Identity), reads, writes)
        return P.add(eng, lambda e: e.tensor_copy(out=out, in_=in_), reads, writes)

    def memset(eng, ap, val, writes):
        return P.add(eng, lambda e: e.memset(ap, val), (), writes)

    for k, src in wsrc.items():
        rows = src.shape[0]
        step = 256 if rows >= 256 else rows
        for r0 in range(0, rows, step):
            r1 = min(rows, r0 + step)
            dma("pool", wb[k][r0:r1, :], src[r0:r1, :], (), (wb_buf[k],), "wc_" + k)

    dma("sp", vt[:], vt_d, (), (vt_b,), "c_vt")
    dma("sp", cst[:], cst_d, (), (cst_b,), "c_cst")
    dma("sp", gat[:], gat_d, (), (kconst_b,), "c_gat")

    cp("dve", ident_bf[:], cst[:, C_ID:C_ID + 128], (cst_b,), (kconst_b,))
    memset("dve", ones_bf[:], 1.0, (kconst_b,))
    cp("dve", bones_bf[:], cst[:, C_BO:C_BO + 128], (cst_b,), (kconst_b,))
    ts("dve", bones_f[:], cst[:, C_BO:C_BO + 128], 1.0 / 64.0, None, ALU.mult, None, (cst_b,), (kconst_b,))
    for i in range(4):
        cp("dve", mL4[:, i, :], cst[:, C_ML:C_ML + 128], (cst_b,), (kconst_b,))
        cp("dve", mU4[:, i, :], cst[:, C_MU:C_MU + 128], (cst_b,), (kconst_b,))
        cp("dve", mUI4[:, i, :], cst[:, C_MUI:C_MUI + 128], (cst_b,), (kconst_b,))
        cp("dve", id4[:, i, :], cst[:, C_ID:C_ID + 128], (cst_b,), (kconst_b,))
    ident_f = cst[:, C_ID:C_ID + 128]

    act(cact[:], vt[:, V_CT:V_CT + 8], AF.Silu, (vt_b,), (cact_b,))
    mps, mps_b = bankA()
    for j in range(72):
        s = 0
        dma("sp", wmods[s][:], wmod_d[:, j * 128:(j + 1) * 128].rearrange("(kc p) n -> p kc n", p=128),
            (), (wmods_b[s],), f"wmod{s}")
        for kc in range(8):
            mm(mps[:, j:j + 1], wmods[s][:, kc, :], cact[:, kc:kc + 1],
               kc == 0, kc == 7, (wmods_b[s], cact_b), (mps_b,))
    tt("dve", modT[:], mps[:, 0:72], vt[:, V_BM:V_BM + 72], ALU.add, (mps_b, vt_b), (mod_b,))

    DR_GM1, DR_GM2, DR_GM3, DR_CG1, DR_CG3, DR_GF, DR_GQ, DR_GKV = 0, 8, 16, 24, 32, 40, 48, 51
    DR_OMR, DR_OMK, DR_OMV, DR_OMWA, DR_OMG, DR_OMA, DR_LNW8 = 53, 57, 61, 65, 66, 67, 71
    SH1, SC1, GT1, SH2, SC2, GT2, SH3, SC3, GT3 = [8 * i for i in range(9)]

    def dts(out_c, n, in_ap, s1, s2, op0, op1):
        ts("dve", drv[:, out_c:out_c + n], in_ap, s1, s2, op0, op1, (mod_b, vt_b, drv_b), (drv_b,))

    for (dst, sc, g) in ((DR_GM1, SC1, V_G1), (DR_GM2, SC2, V_G2), (DR_GM3, SC3, V_G3)):
        dts(dst, 8, modT[:, sc:sc + 8], 1.0, 32.0, ALU.add, ALU.mult)
        tt("dve", drv[:, dst:dst + 8], drv[:, dst:dst + 8], vt[:, g:g + 8], ALU.mult, (drv_b, vt_b), (drv_b,))
    dts(DR_CG1, 8, modT[:, GT1:GT1 + 8], 0.5, None, ALU.mult, None)
    dts(DR_CG3, 8, modT[:, GT3:GT3 + 8], 0.5, None, ALU.mult, None)
    dts(DR_GF, 8, vt[:, V_GF:V_GF + 8], 32.0, None, ALU.mult, None)
    dts(DR_GQ, 3, vt[:, V_GQ:V_GQ + 3], math.sqrt(384.0), None, ALU.mult, None)
    dts(DR_GKV, 2, vt[:, V_GKV:V_GKV + 2], 16.0, None, ALU.mult, None)
    dts(DR_OMR, 14, vt[:, V_MIXR:V_MIXR + 14], -1.0, 1.0, ALU.mult, ALU.add)
    dts(DR_OMA, 4, vt[:, V_KA:V_KA + 4], -1.0, 1.0, ALU.mult, ALU.add)
    dts(DR_LNW8, 4, vt[:, V_LNW:V_LNW + 4], 8.0, None, ALU.mult, None)
    CONST = (drv_b, mod_b, vt_b, kconst_b, cst_b)
    epst = sb("epst", [128, 16])
    eps_cols = {}

    def eps_ap(val, np_=128):
        if val not in eps_cols:
            k = len(eps_cols)
            eps_cols[val] = k
            memset("dve", epst[:, k:k + 1], float(val), (kconst_b,))
        k = eps_cols[val]
        return epst[0:np_, k:k + 1]

    for _v in (D * 1e-6, 384 * 1e-6, 256 * 1e-6, 1e-30, 64 * 64e-5, -PI, 0.0):
        eps_ap(_v)


    def ldw(out, in_, k):
        dma("sp", out, in_, (wb_buf[k],), (wres_b,), "wres")
    ldw(wuq_sb[:], wb["wuq"].rearrange("(kc p) n -> p kc n", p=128), "wuq")
    wuq4 = wb["wuq"].rearrange("(kc p) (h d) -> p kc h d", p=128, d=192)
    for kc in range(3):
        ldw(wuq_rs[:, kc, :, 0:32], wuq4[:, kc, :, 160:192], "wuq")
        ldw(wuq_rs[:, kc, :, 32:64], wuq4[:, kc, :, 128:160], "wuq")
    ldw(wukv_sb[:], wb["wukv"].rearrange("(kc p) n -> p kc n", p=128), "wukv")
    wukv5 = wb["wukv"].rearrange("(kc p) (h two d) -> p kc h two d", p=128, two=2, d=128)
    for kc in range(2):
        ldw(wuv_sb[:, kc, :, :], wukv5[:, kc, :, 1, :], "wukv")
    win3 = wb["win"].rearrange("(kc p) n -> p kc n", p=128)
    ldw(wkr_sb[:, :, 0:64], win3[:, :, 640:704], "win")
    ldw(wkr_sb[:, :, 64:96], win3[:, :, 672:704], "win")
    ldw(wkr_sb[:, :, 96:128], win3[:, :, 640:672], "win")
    ldw(wa_sb[0:64, :], wb["w2"], "w2")
    ldw(wa_sb[64:128, :], wb["a2"], "a2")
    ldw(g2_sb[:], wb["g2"], "g2")
    memset("pool", Vst[:, :, :, 128:129], 1.0, (Vst_b,))

    dbg_n = [0]

    def dump(name, ap, bufs, n):
        if name not in dbg_d:
            return
        k = dbg_n[0]; dbg_n[0] += 1
        tmp = sb(f"dbgtmp{k}", [128, n])
        tb_ = P.buf()
        cp("dve", tmp[:], ap, tuple(bufs), (tb_,))
        dma("pool", dbg_d[name], tmp[:], (tb_,), (), f"dbg{k}")

    memset("pool", Ab[:], 0.0, tuple(Ab_b))
    memset("pool", Rb[:], 0.0, tuple(Rb_b))
    for _i in range(2):
        memset("pool", Khtmc[_i][:], 0.0, tuple(Khtm_b))
        memset("pool", Ubc[_i][:], 0.0, (Ub_b,))
    memset("pool", Hst[:], 0.0, (Hst_b,))
    memset("pool", Hb[:], 0.0, (Hb_b,))
    memset("pool", carry[:], 0.0, tuple(carry_b))

    tmpS = sb("tmpS", [128, T]); tmpS_b = P.buf("tmpS")

    def rsqrt_ps(ps, ps_b, addc, np_=128, n=T, dst=None, dst_b=None):
        dst = tmpA if dst is None else dst
        dst_b = tmpA_b if dst_b is None else dst_b
        act(tmpS[0:np_, 0:n], ps[0:np_, 0:n], AF.Sqrt, (ps_b,), (tmpS_b,), bias=eps_ap(addc, np_))
        P.add("dve", lambda e: e.reciprocal(out=dst[0:np_, 0:n], in_=tmpS[0:np_, 0:n]), (tmpS_b,), (dst_b,))

    def norm_mod(gm_ap_fn, sh_ap_fn, dst, dst_b, Dn=D, src=None, src_b=None, nchunk=8, eps=1e-6):
        src = hT if src is None else src
        src_b = hT_b if src_b is None else src_b
        for c in range(nchunk):
            act(sqb[:, c, :], src[:, c, :], AF.Square, (src_b[c],), (sqb_b[c],))
        ps, ps_b = bankA()
        for c in range(nchunk):
            mm(ps[:, 0:T], ones_bf[:], sqb[:, c, :], c == 0, c == nchunk - 1, (sqb_b[c], kconst_b), (ps_b,))
        rsqrt_ps(ps, ps_b, Dn * eps)
        for c in range(nchunk):
            if sh_ap_fn is None:
                stt("dve", dst[:, c, :], src[:, c, :], gm_ap_fn(c), tmpA[:], ALU.mult, ALU.mult,
                    (src_b[c], tmpA_b) + CONST, (dst_b[c],))
            else:
                k = c % 2
                tt("dve", tmpB[k][:], src[:, c, :], tmpA[:], ALU.mult, (src_b[c], tmpA_b), (tmpB_b[k],))
                act(dst[:, c, :], tmpB[k][:], AF.Identity, (tmpB_b[k],) + CONST, (dst_b[c],),
                    bias=sh_ap_fn(c), scale=gm_ap_fn(c))

    wslot = {"g": 0, "u": 0, "d": 0}

    def ffn(kg, ku, kd, cg_col):
        for j in range(NJ):
            sg_ = wslot["g"] % 2; wslot["g"] += 1
            su_ = wslot["u"] % 2; wslot["u"] += 1
            dma("sp", wgs[sg_][:], wb[kg][:, j * 128:(j + 1) * 128].rearrange("(kc p) n -> p kc n", p=128),
                (wb_buf[kg],), (wgs_b[sg_],), f"wgs{sg_}")
            dma("sp", wus[su_][:], wb[ku][:, j * 128:(j + 1) * 128].rearrange("(kc p) n -> p kc n", p=128),
                (wb_buf[ku],), (wus_b[su_],), f"wus{su_}")
            gps, gps_b = bankA()
            ups, ups_b = bankA()
            for kc in range(8):
                mm(gps[:, 0:T], wgs[sg_][:, kc, :], uT[:, kc, :], kc == 0, kc == 7,
                   (wgs_b[sg_], uT_b[kc]), (gps_b,))
            for kc in range(8):
                mm(ups[:, 0:T], wus[su_][:, kc, :], uT[:, kc, :], kc == 0, kc == 7,
                   (wus_b[su_], uT_b[kc]), (ups_b,))
            k = j % 2
            act(sgt[k][:], gps[:, 0:T], AF.Silu, (gps_b,), (sgt_b[k],))
            tt("dve", actT[:, j, :], sgt[k][:], ups[:, 0:T], ALU.mult, (sgt_b[k], ups_b), (actT_b[j],))
        for half in range(2):
            for jg in range(6):
                nj = 4 if jg < 5 else 2
                sd_ = wslot["d"] % 2; wslot["d"] += 1
                dma("sp", wds[sd_][:, 0:nj, :],
                    wb[kd][jg * 512:jg * 512 + nj * 128, half * 512:(half + 1) * 512].rearrange("(j p) n -> p j n", p=128),
                    (wb_buf[kd],), (wds_b[sd_],), f"wds{sd_}")
                for jj in range(nj):
                    j = 4 * jg + jj
                    for c4 in range(4):
                        bk, bk_b = bankB(c4)
                        mm(bk[:, 0:T], wds[sd_][:, jj, c4 * 128:(c4 + 1) * 128], actT[:, j, :], j == 0, j == NJ - 1,
                           (wds_b[sd_], actT_b[j]), (bk_b,))
            for c4 in range(4):
                c = half * 4 + c4
                bk, bk_b = bankB(c4)
                stt("dve", hT[:, c, :], bk[:, 0:T], drv[:, cg_col + c:cg_col + c + 1], hT[:, c, :], ALU.mult, ALU.add,
                    (bk_b, hT_b[c]) + CONST, (hT_b[c],))


    winslot = [0]
    w128slot = [0]
    kvslot = [0]
    ptslot = [0]
    rtslot = [0]

    def load_win(key, c0, ncols):
        k = winslot[0] % 2; winslot[0] += 1
        dma("sp", wins[k][:, :, 0:ncols], wb[key][:, c0:c0 + ncols].rearrange("(kc p) n -> p kc n", p=128),
            (wb_buf[key],), (wins_b[k],), f"wins{k}")
        return wins[k], wins_b[k]

    def load_w128(key, c0):
        k = w128slot[0] % 2; w128slot[0] += 1
        dma("sp", w128[k][:], wb[key][:, c0:c0 + 128].rearrange("(kc p) n -> p kc n", p=128),
            (wb_buf[key],), (w128_b[k],), f"w128_{k}")
        return w128[k], w128_b[k]

    def proj(lhs_fn, w_b, M=128):
        ps, ps_b = bankA()
        for kc in range(8):
            mm(ps[0:M, 0:T], lhs_fn(kc), uT[:, kc, :], kc == 0, kc == 7, (w_b, uT_b[kc]), (ps_b,))
        return ps, ps_b

    def rt():
        k = rtslot[0] % 4; rtslot[0] += 1
        return rtmp[k], rtmp_b[k]

    def rope_combine(psr, psr_b, psrs, psrs_b, dst_ap, dst_bufs):
        t1, t1_b = rt()
        t2, t2_b = rt()
        tt("dve", t1[0:64, :], psr[0:64, 0:T], cosT[0:64, :], ALU.mult, (psr_b, rope_b), (t1_b,))
        tt("dve", t2[0:64, :], psrs[0:64, 0:T], sinS[0:64, :], ALU.mult, (psrs_b, rope_b), (t2_b,))
        tt("pool", dst_ap, t1[0:64, :], t2[0:64, :], ALU.add, (t1_b, t2_b), dst_bufs)

    def mla(it, t0):
        dma("sp", posi[0:64, :], pos_d[:, t0:t0 + T], (), (posi_b,), "posi")
        cp("dve", ang[0:64, :], posi[0:64, :], (posi_b,), (ang_b,))
        ts("dve", ang[0:64, :], ang[0:64, :], cst[0:64, C_INVF:C_INVF + 1], None, ALU.mult, None, (ang_b, cst_b), (ang_b,))
        C1 = 6.28125
        C2 = 2 * PI - C1

        def sin_of(dst, shift):
            ts("dve", ang2[0:64, :], ang[0:64, :], shift, 1.0 / (2 * PI), ALU.add, ALU.mult, (ang_b, rope_b), (ang2_b,))
            cp("dve", posi[0:64, :], ang2[0:64, :], (ang2_b,), (posi_b,))
            cp("dve", ang2[0:64, :], posi[0:64, :], (posi_b,), (ang2_b,))
            t1, t1_b = rt()
            ts("dve", t1[0:64, :], ang[0:64, :], shift, None, ALU.add, None, (ang_b,), (t1_b,))
            stt("dve", t1[0:64, :], ang2[0:64, :], -C1, t1[0:64, :], ALU.mult, ALU.add, (ang2_b, t1_b), (t1_b,))
            stt("dve", t1[0:64, :], ang2[0:64, :], -C2, t1[0:64, :], ALU.mult, ALU.add, (ang2_b, t1_b), (t1_b,))
            ts("dve", ang2[0:64, :], t1[0:64, :], PI, None, ALU.is_gt, None, (t1_b,), (ang2_b,))
            stt("dve", t1[0:64, :], ang2[0:64, :], -2 * PI, t1[0:64, :], ALU.mult, ALU.add, (ang2_b, t1_b), (t1_b,))
            ts("dve", ang2[0:64, :], t1[0:64, :], -PI, None, ALU.is_lt, None, (t1_b,), (ang2_b,))
            stt("dve", t1[0:64, :], ang2[0:64, :], 2 * PI, t1[0:64, :], ALU.mult, ALU.add, (ang2_b, t1_b), (t1_b,))
            ts("dve", t1[0:64, :], t1[0:64, :], PI, -PI, ALU.min, ALU.max, (t1_b,), (t1_b,))
            act(dst, t1[0:64, :], AF.Sin, (t1_b,), (rope_b,))

        sin_of(sinS[0:64, :], 0.0)
        ts("dve", sinS[0:64, :], sinS[0:64, :], cst[0:64, C_SS:C_SS + 1], None, ALU.mult, None, (rope_b, cst_b), (rope_b,))
        sin_of(cosT[0:64, :], 0.5 * PI)

        w, w_b = load_win("win", 0, 256)
        for jj in range(2):
            ps, ps_b = proj(lambda kc, jj=jj, w=w: w[:, kc, jj * 128:(jj + 1) * 128], w_b)
            cp("act", pq[:, jj, :], ps[:, 0:T], (ps_b,), (pq_b[jj],))
        w, w_b = load_win("win", 256, 256)
        ps, ps_b = proj(lambda kc, w=w: w[:, kc, 0:128], w_b)
        cp("act", pq[:, 2, :], ps[:, 0:T], (ps_b,), (pq_b[2],))
        ps, ps_b = proj(lambda kc, w=w: w[:, kc, 128:256], w_b)
        cp("act", pkv[:, 0, :], ps[:, 0:T], (ps_b,), (pkv_b[0],))
        w, w_b = load_win("win", 512, 128)
        ps, ps_b = proj(lambda kc, w=w: w[:, kc, 0:128], w_b)
        cp("act", pkv[:, 1, :], ps[:, 0:T], (ps_b,), (pkv_b[1],))
        psr, psr_b = proj(lambda kc: wkr_sb[:, kc, 0:64], wres_b, M=64)
        psrs, psrs_b = proj(lambda kc: wkr_sb[:, kc, 64:128], wres_b, M=64)
        rope_combine(psr, psr_b, psrs, psrs_b, Krst[0:64, :], (Krst_b,))
        dma("pool", Krc_d[:, t0:t0 + T], Krst[0:64, :], (Krst_b,), (Krc_buf[it],), "krst")

        norm_mod(lambda c: drv[:, DR_GQ + c:DR_GQ + c + 1], None, qn, qn_b, Dn=384, src=pq, src_b=pq_b, nchunk=3)
        norm_mod(lambda c: drv[:, DR_GKV + c:DR_GKV + c + 1], None, kvn, kvn_b, Dn=256, src=pkv, src_b=pkv_b, nchunk=2)

        for h in range(4):
            ps, ps_b = bankA()
            for kc in range(3):
                mm(ps[:, 0:T], wuq_sb[:, kc, 192 * h:192 * h + 128], qn[:, kc, :], kc == 0, kc == 2,
                   (wres_b, qn_b[kc]), (ps_b,))
            cp("act", QnT[:, h, :], ps[:, 0:T], (ps_b,), (QnT_b[h],))
            psr, psr_b = bankA()
            for kc in range(3):
                mm(psr[0:64, 0:T], wuq_sb[:, kc, 192 * h + 128:192 * h + 192], qn[:, kc, :], kc == 0, kc == 2,
                   (wres_b, qn_b[kc]), (psr_b,))
            psrs, psrs_b = bankA()
            for kc in range(3):
                mm(psrs[0:64, 0:T], wuq_rs[:, kc, h, :], qn[:, kc, :], kc == 0, kc == 2,
                   (wres_b, qn_b[kc]), (psrs_b,))
            rope_combine(psr, psr_b, psrs, psrs_b, QrT[0:64, h, :], (QrT_b[h],))
        for h in range(4):
            ps, ps_b = bankA()
            for kc in range(2):
                mm(ps[:, 0:T], wukv_sb[:, kc, 256 * h:256 * h + 128], kvn[:, kc, :], kc == 0, kc == 1,
                   (wres_b, kvn_b[kc]), (ps_b,))
            cp("act", Kst[:, h, :], ps[:, 0:T], (ps_b,), (Kst_b,))
        dma("pool", Kc_d[:, :, t0:t0 + T].rearrange("h p t -> p h t"), Kst[:], (Kst_b,),
            tuple(Kc_buf[h][it] for h in range(4)), "kst")
        for tb in range(NTB):
            ps, ps_b = bankA()
            for kc in range(2):
                mm(ps[:, 0:512], kvn[:, kc, tb * 128:(tb + 1) * 128], wuv_sb[:, kc, :, :].rearrange("p h d -> p (h d)"),
                   kc == 0, kc == 1, (wres_b, kvn_b[kc]), (ps_b,))
            cp("dve", Vst[:, :, tb, 0:128], ps[:, 0:512].rearrange("p (h d) -> p h d", h=4), (ps_b,), (Vst_b,))
        for h in range(4):
            dma("pool", Vc_d[h, it], Vst[:, h, :, :].rearrange("p t d -> p (t d)"), (Vst_b,), (Vc_buf[h][it],), f"vst{h}")

        for h in range(4):
            opsl = [bankB(qs) for qs in range(NTB)]
            for jt in range(it + 1):
                k = kvslot[0] % 2; kvslot[0] += 1
                dma("sp", Kt[k][:], Kc_d[h][:, jt * T:(jt + 1) * T], (Kc_buf[h][jt],), (Kt_b[k],), f"kt{k}")
                dma("sp", Krt[k][0:64, :], Krc_d[:, jt * T:(jt + 1) * T], (Krc_buf[jt],), (Krt_b[k],), f"krt{k}")
                dma("sp", Vt[k][:], Vc_d[h, jt], (Vc_buf[h][jt],), (Vt_b[k],), f"vt{k}")
                for kb in range(NTB):
                    diag = jt == it
                    q0 = kb * 128 if diag else 0
                    sps, sps_b = bankA()
                    mm(sps[:, q0:T], Kt[k][:, kb * 128:(kb + 1) * 128], QnT[:, h, q0:T], True, False,
                       (Kt_b[k], QnT_b[h]), (sps_b,))
                    mm(sps[:, q0:T], Krt[k][0:64, kb * 128:(kb + 1) * 128], QrT[0:64, h, q0:T], False, True,
                       (Krt_b[k], QrT_b[h]), (sps_b,))
                    p = ptslot[0] % 3; ptslot[0] += 1
                    act(PT[p][:, q0:T], sps[:, q0:T], AF.Exp, (sps_b,), (PT_b[p],), scale=SCALE)
                    if diag:
                        memset("pool", PT[p][64:128, q0:q0 + 64], 0.0, (PT_b[p],))
                    for qs in range(q0 // 128, NTB):
                        first = (jt == 0 and kb == 0)
                        last = (diag and kb == qs)
                        ops, ops_b = opsl[qs]
                        mm(ops[:, 0:129], PT[p][:, qs * 128:(qs + 1) * 128],
                           Vt[k][:, kb * 129:(kb + 1) * 129], first, last, (PT_b[p], Vt_b[k]), (ops_b,))
            for qs in range(NTB):
                ops, ops_b = opsl[qs]
                P.add("dve", lambda e, qs=qs, h=h, ops=ops: e.reciprocal(out=rec[:, qs:qs + 1], in_=ops[:, 128:129]),
                      (ops_b,), (rec_b,))
                ts("dve", ya[:, qs, h * 128:(h + 1) * 128], ops[:, 0:128], rec[:, qs:qs + 1], None,
                   ALU.mult, None, (ops_b, rec_b), (ya_b[qs],))

        for qs in range(NTB):
            t1, t1_b = rt()
            t2, t2_b = rt()
            act(t1[:], ya[:, qs, 0:T], AF.Square, (ya_b[qs],), (t1_b,))
            act(t2[:], ya[:, qs, T:2 * T], AF.Square, (ya_b[qs],), (t2_b,))
            tt("dve", t1[:], t1[:], t2[:], ALU.add, (t1_b, t2_b), (t1_b,))
            P.add("dve", lambda e, qs=qs, t1=t1: e.reduce_sum(out=ssq[:, qs:qs + 1], in_=t1[:], axis=AX.X), (t1_b,), (ssq_b,))
        if it == NT - 1:
            dump("cos", cosT[:], (rope_b,), T)
            dump("sin", sinS[:], (rope_b,), T)
            dump("qn0", QnT[:, 0, :], (QnT_b[0],), T)
            dump("qr0", QrT[:, 0, :], (QrT_b[0],), T)
            dump("ya0", ya[:, 0, :], (ya_b[0],), 512)
            dump("kst0", Kst[:, 0, :], (Kst_b,), T)
            dump("krst", Krst[:], (Krst_b,), T)
        ts("dve", ssq[:, 0:NTB], ssq[:, 0:NTB], 1.0 / 512.0, 1e-6, ALU.mult, ALU.add, (ssq_b,), (ssq_b,))
        act(ssq[:, 0:NTB], ssq[:, 0:NTB], AF.Sqrt, (ssq_b,), (ssq_b,))
        P.add("dve", lambda e: e.reciprocal(out=ssq[:, 0:NTB], in_=ssq[:, 0:NTB]), (ssq_b,), (ssq_b,))
        for qs in range(NTB):
            stt("dve", yab[:, qs, :], ya[:, qs, :], ssq[:, qs:qs + 1], gat[:], ALU.mult, ALU.mult,
                (ya_b[qs], ssq_b, kconst_b), (yab_b[qs],))
        for cch in range(4):
            ps, ps_b = bankA()
            psbf = ps[:].bitcast(BF16)
            for qs in range(NTB):
                tr(psbf[:, qs * 128:(qs + 1) * 128], yab[:, qs, cch * 128:(cch + 1) * 128], ident_bf[:],
                   (yab_b[qs], kconst_b), (ps_b,))
            cp("act", yT[:, cch, :], psbf[:, 0:T], (ps_b,), (yT_b[cch],))


    btslot = [0]

    SSUB = int(os.environ.get("SSUB", "9"))
    RSUB = int(os.environ.get("RSUB", "9"))

    def shift_evac(ps, ps_b, ci, mixcol, omcol, dst, dst_b):
        k = btslot[0] % 2; btslot[0] += 1
        ts("dve", Bt[k][:, 8:T + 8], ps[:, 0:T], vt[:, mixcol:mixcol + 1], None, ALU.mult, None, (ps_b, vt_b), (Bt_b[k],))
        if SSUB <= 1:
            return
        cp("pool", Bt[k][:, 7:8], carry[:, ci:ci + 1], (carry_b[ci],), (Bt_b[k],))
        if SSUB <= 2:
            return
        ts("dve", dst, ps[:, 0:T], drv[:, omcol:omcol + 1], None, ALU.mult, None, (ps_b,) + CONST, (dst_b,))
        if SSUB <= 3:
            return
        tt("pool", dst, dst, Bt[k][:, 7:T + 7], ALU.add, (dst_b, Bt_b[k]), (dst_b,))
        if SSUB <= 4:
            return
        cp("pool", carry[:, ci:ci + 1], Bt[k][:, T + 7:T + 8], (Bt_b[k],), (carry_b[ci],))

    def R_(n):
        return RT[n][0][:], RT[n][1]


    RCUT = int(os.environ.get("RCUT", "9"))

    def rwkv_fill():
        for c in range(4, 8):
            memset("pool", yT[:, c, :], 0.0, (yT_b[c],))

    def rwkv(it, t0):
        w, w_b = load_win("win", 2240, 256)
        ps, ps_b = proj(lambda kc, w=w: w[:, kc, 0:128], w_b)
        if RSUB <= 1:
            cp("act", xwxa[:], ps[:, 0:T], (ps_b,), (xwxa_b,))
            return rwkv_fill()
        shift_evac(ps, ps_b, 12, V_MIXWA, DR_OMWA, xwxa[:], xwxa_b)
        if RSUB <= 2:
            return rwkv_fill()
        ps, ps_b = proj(lambda kc, w=w: w[:, kc, 128:256], w_b)
        shift_evac(ps, ps_b, 13, V_MIXG, DR_OMG, xg[:], xg_b)
        if RSUB <= 3:
            return rwkv_fill()
        act(thb[0:64, :], xwxa[0:64, :], AF.Tanh, (xwxa_b,), (lor_b,))
        if RSUB <= 4:
            return rwkv_fill()
        cp("dve", xab[64:128, :], xwxa[64:128, :], (xwxa_b,), (lor_b,))
        if RSUB <= 5:
            return rwkv_fill()
        act(sgx[:], xg[:], AF.Sigmoid, (xg_b,), (lor_b,))
        if RCUT <= 1:
            return rwkv_fill()
        rS, rS_b = R_("rS"); kS, kS_b = R_("kS"); vS, vS_b = R_("vS"); sg, sg_b = R_("sg"); Lc, Lc_b = R_("Lc")
        E1, E1_b = R_("E1"); E3, E3_b = R_("E3"); aG, aG_b = R_("aG"); kk, kk_b = R_("kk"); rn, rn_b = R_("rn")
        kkn, kkn_b = R_("kkn"); bb, bb_b = R_("bb"); E2, E2_b = R_("E2"); tk, tk_b = R_("tk"); kmod, kmod_b = R_("kmod")
        E4, E4_b = R_("E4"); rk, rk_b = R_("bb"); yc, yc_b = R_("kkn"); yn, yn_b = R_("kk")
        for cc in range(4):
            sl = slice(cc * 128, (cc + 1) * 128)
            for (dst, dst_b, c0, ci, mixc, omc) in ((rS, rS_b, 704, cc, V_MIXR + cc, DR_OMR + cc),
                                                   (kS, kS_b, 1216, 4 + cc, V_MIXK + cc, DR_OMK + cc),
                                                   (vS, vS_b, 1728, 8 + cc, V_MIXV + cc, DR_OMV + cc)):
                w, w_b = load_w128("win", c0 + 128 * cc)
                ps, ps_b = proj(lambda kc, w=w: w[:, kc, :], w_b)
                shift_evac(ps, ps_b, ci, mixc, omc, dst, dst_b)
            ps, ps_b = bankA()
            mm(ps[:, 0:T], wa_sb[0:64, sl], thb[0:64, :], True, True, (wres_b, lor_b), (ps_b,))
            act(sg, ps[:, 0:T], AF.Sigmoid, (ps_b,) + CONST, (sg_b,), bias=vt[:, V_W0 + cc:V_W0 + cc + 1])
            if os.environ.get("NOSCAN"):
                cp("dve", Lc, sg, (sg_b,), (Lc_b,))
            else:
                P.add("dve", lambda e: e.tensor_tensor_scan(out=Lc, data0=cst[:, C_RST:C_RST + T], data1=sg, initial=0.0,
                                                            op0=ALU.mult, op1=ALU.add), (sg_b, cst_b), (Lc_b,))
            act(E1, Lc, AF.Exp, (Lc_b,), (E1_b,), scale=-KAPPA)
            cp("pool", gL[:, cc, :], E1.rearrange("p (c t) -> p c t", t=64)[:, :, 63], (E1_b,), (gL_b,))
            tt("dve", Rb[0:64, 2 * cc, :], rS[0:64, :], E1[0:64, :], ALU.mult, (rS_b, E1_b), (Rb_b[cc],))
            tt("dve", Rb[64:128, 2 * cc + 1, :], rS[64:128, :], E1[64:128, :], ALU.mult, (rS_b, E1_b), (Rb_b[cc],))
            tt("pool", E3, Lc, sg, ALU.subtract, (Lc_b, sg_b), (E3_b,))
            act(E3, E3, AF.Exp, (E3_b,), (E3_b,), scale=-KAPPA)
            ps, ps_b = bankA()
            mm(ps[:, 0:T], wa_sb[64:128, sl], xab[64:128, :], True, True, (wres_b, lor_b), (ps_b,))
            act(aG, ps[:, 0:T], AF.Sigmoid, (ps_b,) + CONST, (aG_b,), bias=vt[:, V_A0 + cc:V_A0 + cc + 1])
            ts("dve", kk, kS, vt[:, V_KK + cc:V_KK + cc + 1], None, ALU.mult, None, (kS_b, vt_b), (kk_b,))
            act(sqk[:], kk, AF.Square, (kk_b,), (sqk_b,))
            ps, ps_b = bankA()
            mm(ps[:, 0:T], bones_bf[:], sqk[:], True, True, (sqk_b, kconst_b), (ps_b,))
            rsqrt_ps(ps, ps_b, 1e-30, dst=RT["rn"][0], dst_b=rn_b)
            tt("dve", kkn, kk, rn, ALU.mult, (kk_b, rn_b), (kkn_b,))
            stt("dve", Ab[0:64, 2 * cc, :], kkn[0:64, :], -1.0, E3[0:64, :], ALU.mult, ALU.mult, (kkn_b, E3_b), (Ab_b[cc],))
            stt("dve", Ab[64:128, 2 * cc + 1, :], kkn[64:128, :], -1.0, E3[64:128, :], ALU.mult, ALU.mult, (kkn_b, E3_b), (Ab_b[cc],))
            tt("pool", bb, kkn, aG, ALU.mult, (kkn_b, aG_b), (bb_b,))
            act(E2, Lc, AF.Exp, (Lc_b,), (E2_b,), scale=KAPPA)
            tt("dve", Btb[:, cc, :], bb, E2, ALU.mult, (bb_b, E2_b), (Btb_b[cc],))
            ts("dve", tk, aG, vt[:, V_KA + cc:V_KA + cc + 1], drv[:, DR_OMA + cc:DR_OMA + cc + 1], ALU.mult, ALU.add,
               (aG_b,) + CONST, (tk_b,))
            tt("pool", kmod, kS, tk, ALU.mult, (kS_b, tk_b), (kmod_b,))
            tt("dve", Ktb[:, cc, :], kmod, E2, ALU.mult, (kmod_b, E2_b), (Ktb_b[cc],))
            Lc3 = Lc.rearrange("p (c t) -> p c t", t=64)
            if os.environ.get("NOBC"):
                tt("pool", E4, Lc, Lc, ALU.subtract, (Lc_b,), (E4_b,))
            else:
                tt("pool", E4.rearrange("p (c t) -> p c t", t=64), Lc3[:, :, 63:64].broadcast_to([128, NCH, 64]), Lc3,
                   ALU.subtract, (Lc_b,), (E4_b,))
            act(E4, E4, AF.Exp, (E4_b,), (E4_b,), scale=-KAPPA)
            tt("dve", Bhb[:, cc, :], bb, E4, ALU.mult, (bb_b, E4_b), (Bhb_b[cc],))
            tt("pool", Khb[:, cc, :], kmod, E4, ALU.mult, (kmod_b, E4_b), (Khb_b[cc],))
            cp("act", vb[:, cc, :], vS, (vS_b,), (vb_b[cc],))
            tt("pool", rk, rS, kmod, ALU.mult, (rS_b, kmod_b), (rk_b,))
            ts("dve", sqk[:], rk, vt[:, V_RK + cc:V_RK + cc + 1], None, ALU.mult, None, (rk_b, vt_b), (sqk_b,))
            ps, ps_b = bankA()
            mm(ps[:, 0:T], bones_bf[:], sqk[:], True, True, (sqk_b, kconst_b), (ps_b,))
            tt("dve", bon[:, cc, :], ps[:, 0:T], vS, ALU.mult, (ps_b, vS_b), bon_b[cc])
            ps, ps_b = bankA()
            mm(ps[:, 0:T], g2_sb[:, sl], sgx[:], True, True, (wres_b, lor_b), (ps_b,))
            cp("act", gG[:, cc, :], ps[:, 0:T], (ps_b,), gG_b[cc])

        if RCUT <= 2:
            return rwkv_fill()
        for tb in range(NTB):
            blk = slice(tb * 128, (tb + 1) * 128)
            for (src, src_b, dst, dst_b) in ((vb, vb_b, Vtm, Vtm_b), (Bhb, Bhb_b, Bhtm, Bhtm_b), (Khb, Khb_b, None, Khtm_b)):
                ps, ps_b = bankA()
                psbf = ps[:].bitcast(BF16)
                for cc in range(4):
                    tr(psbf[:, cc * 128:(cc + 1) * 128], src[:, cc, blk], ident_bf[:], (src_b[cc], kconst_b), (ps_b,))
                if dst is None:
                    cp("act", Khtmc[0][0:64, tb, :], psbf[0:64, 0:512], (ps_b,), (dst_b[tb],))
                    cp("act", Khtmc[1][64:128, tb, :], psbf[64:128, 0:512], (ps_b,), (dst_b[tb],))
                else:
                    cp("act", dst[:, tb, :], psbf[:, 0:512], (ps_b,), (dst_b[tb],))

        if RCUT <= 3:
            return rwkv_fill()
        NSUB = int(os.environ.get("NSUB", "99"))
        for tb in range(NTB):
            blk = slice(tb * 128, (tb + 1) * 128)
            L = LL[0]
            TtA, TtA_b = L["TtA"]; AkT, AkT_b = L["AkT"]; ArbT, ArbT_b = L["ArbT"]; ArkT, ArkT_b = L["ArkT"]
            for bi in range(2):
                hs = [4 * bi + i for i in range(4)]

                def opnd(X, h):
                    if X is Ab or X is Rb:
                        return X[:, h, blk]
                    return X[:, h // 2, blk]

                def batch_mm(lfn, rfn, reads):
                    ps, ps_b = bankA()
                    for i, h in enumerate(hs):
                        mm(ps[:, i * 128:(i + 1) * 128], lfn(i, h), rfn(i, h), True, True, reads, (ps_b,))
                    return ps, ps_b

                ab_r = tuple(Ab_b) + tuple(Btb_b) + tuple(Ktb_b) + tuple(Rb_b)
                (M0, M0_b), (M0t, M0t_b), (M1, M1_b), (M1t, M1t_b) = MM
                ps, ps_b = batch_mm(lambda i, h: opnd(Ab, h), lambda i, h: opnd(Btb, h), ab_r)
                tt("dve", M0[:], ps[:].rearrange("p (i s) -> p i s", i=4), mL4[:], ALU.mult, (ps_b, kconst_b), (M0_b,))
                if NSUB <= 1:
                    continue
                ps, ps_b = batch_mm(lambda i, h: opnd(Btb, h), lambda i, h: opnd(Ab, h), ab_r)
                tt("dve", M0t[:], ps[:].rearrange("p (i s) -> p i s", i=4), mU4[:], ALU.mult, (ps_b, kconst_b), (M0t_b,))
                if NSUB <= 2:
                    continue
                tt("pool", TtA[:, 4 * bi:4 * bi + 4, :], M0t[:], id4[:], ALU.add, (M0t_b, kconst_b), (TtA_b[bi],))
                if NSUB <= 3:
                    continue
                cur, cur_b, curt, curt_b = M0, M0_b, M0t, M0t_b
                nxt, nxt_b, nxtt, nxtt_b = M1, M1_b, M1t, M1t_b
                for lvl in range(5):
                    ps, ps_b = batch_mm(lambda i, h: curt[:, i, :], lambda i, h: cur[:, i, :], (cur_b, curt_b))
                    cp("act", nxt[:], ps[:].rearrange("p (i s) -> p i s", i=4), (ps_b,), (nxt_b,))
                    if lvl < 4:
                        ps, ps_b = batch_mm(lambda i, h: cur[:, i, :], lambda i, h: curt[:, i, :], (cur_b, curt_b))
                        cp("act", nxtt[:], ps[:].rearrange("p (i s) -> p i s", i=4), (ps_b,), (nxtt_b,))
                    ps, ps_b = batch_mm(lambda i, h: nxt[:, i, :], lambda i, h: TtA[:, h, :], (nxt_b, TtA_b[bi]))
                    tt("dve", TtA[:, 4 * bi:4 * bi + 4, :], ps[:].rearrange("p (i s) -> p i s", i=4),
                       TtA[:, 4 * bi:4 * bi + 4, :], ALU.add, (ps_b, TtA_b[bi]), (TtA_b[bi],))
                    cur, cur_b, curt, curt_b, nxt, nxt_b, nxtt, nxtt_b = nxt, nxt_b, nxtt, nxtt_b, cur, cur_b, curt, curt_b
                if NSUB <= 4:
                    continue
                ps, ps_b = batch_mm(lambda i, h: opnd(Ktb, h), lambda i, h: opnd(Ab, h), ab_r)
                tt("dve", AkT[:, 4 * bi:4 * bi + 4, :], ps[:].rearrange("p (i s) -> p i s", i=4), mU4[:], ALU.mult,
                   (ps_b, kconst_b), (AkT_b[bi],))
                ps, ps_b = batch_mm(lambda i, h: opnd(Btb, h), lambda i, h: opnd(Rb, h), ab_r)
                tt("dve", ArbT[:, 4 * bi:4 * bi + 4, :], ps[:].rearrange("p (i s) -> p i s", i=4), mUI4[:], ALU.mult,
                   (ps_b, kconst_b), (ArbT_b[bi],))
                ps, ps_b = batch_mm(lambda i, h: opnd(Ktb, h), lambda i, h: opnd(Rb, h), ab_r)
                tt("dve", ArkT[:, 4 * bi:4 * bi + 4, :], ps[:].rearrange("p (i s) -> p i s", i=4), mUI4[:], ALU.mult,
                   (ps_b, kconst_b), (ArkT_b[bi],))

            if RCUT <= 4:
                continue
            LLr = tuple(TtA_b) + tuple(AkT_b) + tuple(ArbT_b) + tuple(ArkT_b)
            for half in range(2):
                ch = 2 * tb + half
                tp = 64 * half
                tsl = slice(tp, tp + 64)
                csl = slice(tb * 128 + tp, tb * 128 + tp + 64)
                Ub = Ubc[half]
                Khtm = Khtmc[half]
                ps, ps_b = bankA()
                for h in range(8):
                    pb, cc = 64 * (h % 2), h // 2
                    hs_ = slice(h * 64, (h + 1) * 64)
                    mm(ps[:, hs_], Ab[:, h, blk], Hb[:, cc, :], True, False, (Ab_b[cc], Hb_b), (ps_b,))
                    mm(ps[:, hs_], AkT[:, h, :], Vtm[:, tb, hs_], False, True, LLr + (Vtm_b[tb],), (ps_b,))
                cp("act", Xb[:], ps[:], (ps_b,), (Xb_b,))
                ps, ps_b = bankA()
                for h in range(8):
                    hs_ = slice(h * 64, (h + 1) * 64)
                    mm(ps[:, hs_], TtA[:, h, :], Xb[:, hs_], True, True, LLr + (Xb_b,), (ps_b,))
                cp("dve", Ub[tsl, :], ps[tsl, :], (ps_b,), (Ub_b,))
                ps, ps_b = bankA()
                for h in range(8):
                    pb, cc = 64 * (h % 2), h // 2
                    hs_ = slice(h * 64, (h + 1) * 64)
                    o = ps[pb:pb + 64, cc * 64:(cc + 1) * 64]
                    mm(o, Hb[:, cc, :], Rb[:, h, csl], True, False, (Hb_b, Rb_b[cc]), (ps_b,))
                    mm(o, Ub[:, hs_], ArbT[:, h, tsl], False, False, LLr + (Ub_b,), (ps_b,))
                    mm(o, Vtm[:, tb, hs_], ArkT[:, h, tsl], False, True, LLr + (Vtm_b[tb],), (ps_b,))
                cp("act", yR[:, :, csl], ps[:, 0:256].rearrange("p (c t) -> p c t", c=4), (ps_b,), (yR_b,))
                ps, ps_b = bankA()
                for h in range(8):
                    pb, cc = 64 * (h % 2), h // 2
                    hs_ = slice(h * 64, (h + 1) * 64)
                    o = ps[pb:pb + 64, cc * 64:(cc + 1) * 64]
                    mm(o, Bhtm[:, tb, hs_], Ub[:, hs_], True, False, (Bhtm_b[tb], Ub_b), (ps_b,))
                    mm(o, Khtm[:, tb, hs_], Vtm[:, tb, hs_], False, True, (Khtm_b[tb], Vtm_b[tb]), (ps_b,))
                tt("dve", Hst[:], Hst[:], gL[:, :, ch:ch + 1].broadcast_to([128, 4, 64]), ALU.mult, (Hst_b, gL_b), (Hst_b,))
                tt("dve", Hst[:], Hst[:], ps[:, 0:256].rearrange("p (c v) -> p c v", c=4), ALU.add, (Hst_b, ps_b), (Hst_b,))
                cp("act", Hb[:], Hst[:], (Hst_b,), (Hb_b,))

        if RCUT <= 5:
            return rwkv_fill()
        for cc in range(4):
            ps, ps_b = bankA()
            mm(ps[:, 0:T], bones_f[:], yR[:, cc, :], True, True, (yR_b, kconst_b), (ps_b,))
            tt("dve", yc, yR[:, cc, :], ps[:, 0:T], ALU.subtract, (yR_b, ps_b), (yc_b,))
            act(sqk[:], yc, AF.Square, (yc_b,), (sqk_b,))
            ps, ps_b = bankA()
            mm(ps[:, 0:T], bones_bf[:], sqk[:], True, True, (sqk_b, kconst_b), (ps_b,))
            rsqrt_ps(ps, ps_b, 64 * 64e-5, dst=RT["rn"][0], dst_b=rn_b)
            tt("dve", yc, yc, rn, ALU.mult, (yc_b, rn_b), (yc_b,))
            act(yn, yc, AF.Identity, (yc_b,) + CONST, (yn_b,), bias=vt[:, V_LNB + cc:V_LNB + cc + 1],
                scale=drv[:, DR_LNW8 + cc:DR_LNW8 + cc + 1])
            tt("pool", yn, yn, bon[:, cc, :], ALU.add, (yn_b,) + bon_b[cc], (yn_b,))
            tt("dve", yT[:, 4 + cc, :], yn, gG[:, cc, :], ALU.mult, (yn_b,) + gG_b[cc], (yT_b[4 + cc],))

    def outproj():
        for g in range(4):
            w, w_b = load_win("wout", g * 256, 256)
            for jj in range(2):
                c = 2 * g + jj
                ps, ps_b = bankA()
                for kc in range(8):
                    mm(ps[:, 0:T], w[:, kc, jj * 128:(jj + 1) * 128], yT[:, kc, :], kc == 0, kc == 7,
                       (w_b, yT_b[kc]), (ps_b,))
                stt("dve", hT[:, c, :], ps[:, 0:T], modT[:, GT2 + c:GT2 + c + 1], hT[:, c, :], ALU.mult, ALU.add,
                    (ps_b, hT_b[c]) + CONST, (hT_b[c],))

    xslot = [0]
    oslot = [0]
    for it in range(NT):
        t0 = it * T
        for tb in range(NTB):
            s = xslot[0] % 2; xslot[0] += 1
            dma("sp", xin[s][:], x_d[t0 + tb * 128:t0 + (tb + 1) * 128, :], (), (xin_b[s],), f"xin{s}")
            for half in range(2):
                ps, ps_b = bankA()
                for c4 in range(4):
                    c = half * 4 + c4
                    tr(ps[:, c4 * 128:(c4 + 1) * 128], xin[s][:, c * 128:(c + 1) * 128], ident_f,
                       (xin_b[s], cst_b), (ps_b,))
                cp("act", hT[:, half * 4:half * 4 + 4, tb * 128:(tb + 1) * 128],
                   ps[:].rearrange("p (c t) -> p c t", c=4), (ps_b,), tuple(hT_b[half * 4:half * 4 + 4]))

        if "ffn1" in stages:
            norm_mod(lambda c: drv[:, DR_GM1 + c:DR_GM1 + c + 1], lambda c: modT[:, SH1 + c:SH1 + c + 1], uT, uT_b)
            ffn("w1g", "w1u", "w1d", DR_CG1)

        if "mix" in stages:
            norm_mod(lambda c: drv[:, DR_GM2 + c:DR_GM2 + c + 1], lambda c: modT[:, SH2 + c:SH2 + c + 1], uT, uT_b)
            if "nomla" in stages:
                for c in range(4):
                    memset("pool", yT[:, c, :], 0.0, (yT_b[c],))
            else:
                mla(it, t0)
            if "norwkv" in stages:
                for c in range(4, 8):
                    memset("pool", yT[:, c, :], 0.0, (yT_b[c],))
            else:
                rwkv(it, t0)
            outproj()

        if "ffn2" in stages:
            norm_mod(lambda c: drv[:, DR_GM3 + c:DR_GM3 + c + 1], lambda c: modT[:, SH3 + c:SH3 + c + 1], uT, uT_b)
            ffn("w3g", "w3u", "w3d", DR_CG3)

        for c in range(8):
            act(sqb[:, c, :], hT[:, c, :], AF.Square, (hT_b[c],), (sqb_b[c],))
        ps, ps_b = bankA()
        for c in range(8):
            mm(ps[:, 0:T], ones_bf[:], sqb[:, c, :], c == 0, c == 7, (sqb_b[c], kconst_b), (ps_b,))
        rsqrt_ps(ps, ps_b, D * 1e-6)
        for c in range(8):
            stt("dve", hT[:, c, :], hT[:, c, :], drv[:, DR_GF + c:DR_GF + c + 1], tmpA[:], ALU.mult, ALU.mult,
                (hT_b[c], tmpA_b) + CONST, (hT_b[c],))
        for tb in range(NTB):
            s = oslot[0] % 2; oslot[0] += 1
            for half in range(2):
                ps, ps_b = bankA()
                for c4 in range(4):
                    c = half * 4 + c4
                    tr(ps[:, c4 * 128:(c4 + 1) * 128], hT[:, c, tb * 128:(tb + 1) * 128], ident_f,
                       (hT_b[c], cst_b), (ps_b,))
                cp("act" if half == 0 else "dve", ost[s][:, half * 512:(half + 1) * 512], ps[:], (ps_b,), (ost_b[s],))
            dma("pool", out_d[t0 + tb * 128:t0 + (tb + 1) * 128, :], ost[s][:], (ost_b[s],), (), f"ost{s}")

    P.emit()

    if os.environ.get("KDBG"):
        print("STATS", P.stats)
    es.close()
    return nc


def _consts():
    c = np.zeros((128, NCST), np.float32)
    p = np.arange(128)
    c[:, C_ID:C_ID + 128] = np.eye(128, dtype=np.float32)
    same = (p[:, None] // 64) == (p[None, :] // 64)
    c[:, C_ML:C_ML + 128] = (same & (p[None, :] < p[:, None])).astype(np.float32)
    c[:, C_MU:C_MU + 128] = (same & (p[:, None] < p[None, :])).astype(np.float32)
    c[:, C_MUI:C_MUI + 128] = (same & (p[:, None] <= p[None, :])).astype(np.float32)
    c[:, C_BO:C_BO + 128] = same.astype(np.float32)
    c[:, C_INVF] = (10000.0 ** (-(np.arange(128) % 32).astype(np.float32) / 32.0)).astype(np.float32)
    c[:, C_SS] = np.where((p % 64) < 32, -1.0, 1.0)
    c[:, C_RST:C_RST + 512] = ((np.arange(512) % 64) != 0).astype(np.float32)[None, :]
    return c


def _vt(inp, b):
    def ch(v):
        v = np.asarray(v, np.float32).reshape(-1, 128)
        return v.T
    sm = inp["rwkv_shift_mix"][0]
    cols = [
        ch(inp["c"][b]), ch(inp["b_mod"][0]), ch(inp["ffn1_norm_g"][0]), ch(inp["mix_norm_g"][0]),
        ch(inp["ffn2_norm_g"][0]), ch(inp["final_norm_g"]), ch(inp["q_norm_g"][0]), ch(inp["kv_norm_g"][0]),
        ch(sm[0:512]), ch(sm[512:1024]), ch(sm[1024:1536]), ch(sm[1536:1664]), ch(sm[1664:1792]),
        ch(inp["rwkv_w0"][0]), ch(inp["rwkv_a0"][0]), ch(inp["rwkv_k_k"][0]), ch(inp["rwkv_k_a"][0]),
        ch(inp["rwkv_r_k"][0].reshape(-1)), ch(inp["rwkv_ln_w"][0]), ch(inp["rwkv_ln_b"][0]),
    ]
    v = np.concatenate(cols, axis=1)
    assert v.shape == (128, NV), v.shape
    return np.ascontiguousarray(v, np.float32)


def make_in_maps(inp, S, cores):
    f = lambda a: np.ascontiguousarray(np.asarray(a, np.float32))
    shared = {
        "cst": _consts(),
        "gat": np.ascontiguousarray(np.tile(f(inp["attn_out_norm_g"][0])[None, :], (128, 1))),
        "w_mod": f(inp["w_mod"][0]),
        "w1g": f(inp["ffn1_w_gate"][0]), "w1u": f(inp["ffn1_w_up"][0]), "w1d": f(inp["ffn1_w_down"][0]),
        "win": f(inp["w_in"][0]), "wuq": f(inp["w_uq"][0]), "wukv": f(inp["w_ukv"][0]),
        "w2": f(inp["rwkv_w2"][0]), "a2": f(inp["rwkv_a2"][0]), "g2": f(inp["rwkv_g2"][0]),
        "wout": f(inp["w_out"][0]),
        "w3g": f(inp["ffn2_w_gate"][0]), "w3u": f(inp["ffn2_w_up"][0]), "w3d": f(inp["ffn2_w_down"][0]),
    }
    maps = []
    for b in cores:
        m = dict(shared)
        m["x"] = f(inp["x"][b][:S])
        m["pos"] = np.ascontiguousarray(np.tile(np.asarray(inp["positions"][b][:S], np.int32)[None, :], (64, 1)))
        m["vt"] = _vt(inp, b)
        maps.append(m)
    return maps


def kernel(**inputs):
    S = inputs["x"].shape[1]
    B = inputs["x"].shape[0]
    nc = build_nc(S)
    in_maps = make_in_maps(inputs, S, list(range(B)))
    res = run_bass_kernel_spmd(nc, in_maps, core_ids=list(range(B)))
    return np.stack([np.asarray(r["out"], np.float32) for r in res.results], axis=0)
```

```python
import math
from contextlib import ExitStack

import numpy as np
import concourse.bass as bass
import concourse.mybir as mybir
from concourse.bass_utils import run_bass_kernel_spmd

F32 = mybir.dt.float32
BF16 = mybir.dt.bfloat16
I32 = mybir.dt.int32
AF = mybir.ActivationFunctionType
ALU = mybir.AluOpType
AX = mybir.AxisListType

D = 1024
DFF = 2816
T = 256
NTB = T // 128
NJ = DFF // 128
KAPPA = math.exp(-0.5)
SCALE = 192.0 ** -0.5
PI = math.pi

V_CT, V_BM, V_G1, V_G2, V_G3, V_GF, V_GQ, V_GKV = 0, 8, 80, 88, 96, 104, 112, 115
V_MIXR, V_MIXK, V_MIXV, V_MIXWA, V_MIXG = 117, 121, 125, 129, 130
V_W0, V_A0, V_KK, V_KA, V_RK, V_LNW, V_LNB = 131, 135, 139, 143, 147, 151, 155
NV = 159
C_ID, C_ML, C_MU, C_MUI, C_BO, C_INVF, C_SS, C_RST = 0, 128, 256, 384, 512, 640, 641, 642
NCST = 642 + 512


import os
POOL2DVE = bool(os.environ.get('POOL2DVE'))


class Buf:
    __slots__ = ("name", "w", "wd", "r", "rd")

    def __init__(self, name):
        self.name = name
        self.w = {}
        self.wd = []
        self.r = {}
        self.rd = []


class Op:
    __slots__ = ("idx", "eng", "fn", "deps", "is_dma", "dkey", "dval", "signal", "sigval", "raw")

    def __init__(self, idx, eng, fn, is_dma, dkey, dval):
        self.idx = idx
        self.eng = eng
        self.fn = fn
        self.deps = []
        self.is_dma = is_dma
        self.dkey = dkey
        self.dval = dval
        self.signal = False
        self.sigval = 0


class Prog:
    ENGS = ("pe", "act", "dve", "pool", "sp")

    def __init__(self, nc, es):
        self.nc = nc
        self.es = es
        self.ops = []
        self.eng_ops = {e: [] for e in self.ENGS}
        self.dcount = {}
        self.nbuf = 0

    def buf(self, name=None):
        self.nbuf += 1
        return Buf(name or f"b{self.nbuf}")

    def bufs(self, n, name=None):
        return [self.buf(f"{name}{i}") for i in range(n)]

    def add(self, eng, fn, reads=(), writes=(), dkey=None):
        is_dma = dkey is not None
        if eng == "pool" and not is_dma and POOL2DVE:
            eng = "dve"
        dval = 0
        if is_dma:
            self.dcount[dkey] = self.dcount.get(dkey, 0) + 1
            dval = 16 * self.dcount[dkey]
        op = Op(len(self.ops), eng, fn, is_dma, dkey, dval)
        deps = {}

        def dep(o, raw):
            if o is op:
                return
            if (not o.is_dma) and (not is_dma) and o.eng == eng:
                if eng == "pe":
                    return
            deps[o.idx] = o

        for b in reads:
            for o in b.w.values():
                dep(o, True)
            for o in b.wd:
                dep(o, True)
        for b in writes:
            for o in b.r.values():
                dep(o, False)
            for o in b.rd:
                dep(o, False)
            for o in b.w.values():
                dep(o, False)
            for o in b.wd:
                dep(o, False)
        for b in reads:
            if is_dma:
                b.rd.append(op)
            else:
                b.r[eng] = op
        for b in writes:
            if b.r or b.rd:
                keep_r = None
                b.w = {}
                b.wd = []
                b.r = {}
                b.rd = []
            if is_dma:
                b.wd.append(op)
            else:
                b.w[eng] = op
        op.deps = list(deps.values())
        self.ops.append(op)
        self.eng_ops[eng].append(op)
        return op

    def emit(self):
        nc = self.nc
        es = self.es
        for op in self.ops:
            for d in op.deps:
                if not d.is_dma:
                    d.signal = True
        cnt = {e: 0 for e in self.ENGS}
        for op in self.ops:
            if (not op.is_dma) and op.signal:
                cnt[op.eng] += 1
                op.sigval = cnt[op.eng]
        self.stats = dict(cnt=dict(cnt), nops={e: len(v) for e, v in self.eng_ops.items()}, dmax=max(self.dcount.values()) * 16)
        esem = {e: es.enter_context(nc.semaphore("s_" + e)) for e in self.ENGS}
        dsem = {k: es.enter_context(nc.semaphore("d_" + k)) for k in self.dcount}
        block = es.enter_context(nc.Block())

        def gen(engname):
            def body(e):
                known = {}
                for op in self.eng_ops[engname]:
                    for d in op.deps:
                        if d.is_dma:
                            key, sem, val = "d_" + d.dkey, dsem[d.dkey], d.dval
                        else:
                            key, sem, val = "e_" + d.eng, esem[d.eng], d.sigval
                        if known.get(key, 0) >= val:
                            continue
                        e.wait_ge(sem, val)
                        known[key] = val
                    ins = op.fn(e)
                    if op.is_dma:
                        ins.then_inc(dsem[op.dkey], 16)
                    elif op.signal:
                        ins.then_inc(esem[op.eng], 1)
                if engname == "sp":
                    for k, n in self.dcount.items():
                        e.wait_ge(dsem[k], 16 * n)

            return body

        block.tensor(gen("pe"))
        block.vector(gen("dve"))
        block.scalar(gen("act"))
        block.gpsimd(gen("pool"))
        block.sync(gen("sp"))


def build_nc(S, stages=("ffn1", "mix", "ffn2"), dbg=()):
    NT = S // T
    nc = bass.Bass("TRN2", target_bir_lowering=False)
    es = ExitStack()
    P = Prog(nc, es)

    def din(name, shape, dt=F32):
        return nc.dram_tensor(name, list(shape), dt, kind="ExternalInput").ap()

    x_d = din("x", [S, D])
    pos_d = din("pos", [64, S], I32)
    vt_d = din("vt", [128, NV])
    cst_d = din("cst", [128, NCST])
    gat_d = din("gat", [128, 512])
    wmod_d = din("w_mod", [D, 9 * D])
    wsrc = {
        "w1g": din("w1g", [D, DFF]), "w1u": din("w1u", [D, DFF]), "w1d": din("w1d", [DFF, D]),
        "win": din("win", [D, 2496]), "wuq": din("wuq", [384, 768]), "wukv": din("wukv", [256, 1024]),
        "w2": din("w2", [64, 512]), "a2": din("a2", [64, 512]), "g2": din("g2", [128, 512]),
        "wout": din("wout", [D, D]),
        "w3g": din("w3g", [D, DFF]), "w3u": din("w3u", [D, DFF]), "w3d": din("w3d", [DFF, D]),
    }
    out_d = nc.dram_tensor("out", [S, D], F32, kind="ExternalOutput").ap()
    dbg_d = {}
    for name, shape in dbg:
        dbg_d[name] = nc.dram_tensor("dbg_" + name, list(shape), F32, kind="ExternalOutput").ap()

    TILED = ("w1g", "w1u", "w3g", "w3u")
    wb = {k: nc.dram_tensor(k + "_bf", ([NJ, 128, 8 * 128] if k in TILED else list(v.shape)), BF16).ap()
          for k, v in wsrc.items()}
    wb_buf = {k: P.buf("wb_" + k) for k in wsrc}
    Kc_d = nc.dram_tensor("Kc", [4, 128, S], BF16).ap()
    Vc_d = nc.dram_tensor("Vc", [4, NT, 128, NTB * 129], BF16).ap()
    Kc_buf = [[P.buf(f"Kc{h}_{i}") for i in range(NT)] for h in range(4)]
    Vc_buf = [[P.buf(f"Vc{h}_{i}") for i in range(NT)] for h in range(4)]

    def sb(name, shape, dt=F32):
        return es.enter_context(nc.sbuf_tensor("sb_" + name, list(shape), dt))

    vt = sb("vt", [128, NV]); vt_b = P.buf("vt")
    cst = sb("cst", [128, NCST]); cst_b = P.buf("cst")
    drv = sb("drv", [128, 96]); drv_b = P.buf("drv")
    modT = sb("modT", [128, 72]); mod_b = P.buf("modT")
    cact = sb("cact", [128, 8]); cact_b = P.buf("cact")
    ident_bf = sb("ident_bf", [128, 128], BF16)
    ones_bf = sb("ones_bf", [128, 128], BF16)
    bones_bf = sb("bones_bf", [128, 128], BF16)
    bones_f = sb("bones_f", [128, 128])
    mL4 = sb("mL4", [128, 4, 128], BF16); mU4 = sb("mU4", [128, 4, 128], BF16); mUI4 = sb("mUI4", [128, 4, 128], BF16)
    id4 = sb("id4", [128, 4, 128], BF16)
    gat = sb("gat", [128, 512])
    kconst_b = P.buf("kconst")

    hT = sb("hT", [128, 8, T]); hT_b = P.bufs(8, "hT")
    uT = sb("uT", [128, 8, T], BF16); uT_b = P.bufs(8, "uT")
    sqb = sb("sqb", [128, 8, T], BF16); sqb_b = P.bufs(8, "sqb")
    tmpA = sb("tmpA", [128, T]); tmpA_b = P.buf("tmpA")
    tmpB = [sb(f"tmpB{i}", [128, T]) for i in range(2)]; tmpB_b = P.bufs(2, "tmpB")
    arena = sb("arena", [128, NJ * T], BF16)
    actT = arena[:].rearrange("p (j t) -> p j t", j=NJ); actT_b = P.bufs(NJ, "actT")
    sgt = [sb(f"sgt{i}", [128, T]) for i in range(2)]; sgt_b = P.bufs(2, "sgt")
    xin = [sb(f"xin{i}", [128, D]) for i in range(1)]; xin_b = P.bufs(1, "xin")
    ost = xin; ost_b = xin_b
    wgs = [sb(f"wgs{i}", [128, 8, 128], BF16) for i in range(2)]; wgs_b = P.bufs(2, "wgs")
    wus = [sb(f"wus{i}", [128, 8, 128], BF16) for i in range(2)]; wus_b = P.bufs(2, "wus")
    wds = [sb(f"wds{i}", [128, 4, 512], BF16) for i in range(2)]; wds_b = P.bufs(2, "wds")
    wmods = [xin[0][:].rearrange("p (kc n) -> p kc n", kc=8)]; wmods_b = [xin_b[0]]


    Krc_d = nc.dram_tensor("Krc", [64, S], BF16).ap()
    Krc_buf = [P.buf(f"Krc{i}") for i in range(NT)]
    wuq_sb = sb("wuq_sb", [128, 3, 768], BF16)
    wuq_rs = sb("wuq_rs", [128, 3, 4, 64], BF16)
    wukv_sb = sb("wukv_sb", [128, 2, 1024], BF16)
    wuv_sb = sb("wuv_sb", [128, 2, 4, 128], BF16)
    wkr_sb = sb("wkr_sb", [128, 8, 128], BF16)
    wa_sb = sb("wa_sb", [128, 512], BF16)
    g2_sb = sb("g2_sb", [128, 512], BF16)
    wres_b = P.buf("wres")
    wins = [sb(f"wins{i}", [128, 8, 256], BF16) for i in range(2)]; wins_b = P.bufs(2, "wins")
    w128 = [sb(f"w128_{i}", [128, 8, 128], BF16) for i in range(2)]; w128_b = P.bufs(2, "w128")
    pq = sb("pq", [128, 3, T]); pq_b = P.bufs(3, "pq")
    qn = sb("qn", [128, 3, T], BF16); qn_b = P.bufs(3, "qn")
    pkv = sb("pkv", [128, 2, T]); pkv_b = P.bufs(2, "pkv")
    kvn = sb("kvn", [128, 2, T], BF16); kvn_b = P.bufs(2, "kvn")
    QnT = sb("QnT", [128, 4, T], BF16); QnT_b = P.bufs(4, "QnT")
    QrT = sb("QrT", [128, 4, T], BF16); QrT_b = P.bufs(4, "QrT")
    Kst = sb("Kst", [128, 4, T], BF16); Kst_b = P.buf("Kst")
    Krst = sb("Krst", [128, T], BF16); Krst_b = P.buf("Krst")
    Vst = sb("Vst", [128, 4, NTB, 129], BF16); Vst_b = P.buf("Vst")
    KVG = 4
    Kt = [sb(f"Kt{i}", [128, KVG * T], BF16) for i in range(2)]; Kt_b = P.bufs(2, "Kt")
    Krt = [sb(f"Krt{i}", [128, KVG * T], BF16) for i in range(2)]; Krt_b = P.bufs(2, "Krt")
    Vt = [sb(f"Vt{i}", [128, KVG, NTB * 129], BF16) for i in range(2)]; Vt_b = P.bufs(2, "Vt")
    PT = [sb(f"PT{i}", [128, T], BF16) for i in range(3)]; PT_b = P.bufs(3, "PT")
    posi = sb("posi", [128, T], I32); posi_b = P.buf("posi")
    ang = sb("ang", [128, T]); ang_b = P.buf("ang")
    ang2 = sb("ang2", [128, T]); ang2_b = P.buf("ang2")
    cosT = sb("cosT", [128, T]); sinS = sb("sinS", [128, T]); rope_b = P.buf("rope")
    rtmp = [sb(f"rtmp{i}", [128, T]) for i in range(4)]; rtmp_b = P.bufs(4, "rtmp")
    ya = sb("ya", [128, NTB, 512]); ya_b = P.bufs(NTB, "ya")
    yab = sb("yab", [128, NTB, 512], BF16); yab_b = P.bufs(NTB, "yab")
    yT = sb("yT", [128, 8, T], BF16); yT_b = P.bufs(8, "yT")
    ssq = sb("ssq", [128, 8]); ssq_b = P.buf("ssq")
    rec = sb("rec", [128, 8]); rec_b = P.buf("rec")


    NCH = T // 64
    Hst = sb("Hst", [128, 4, 64]); Hst_b = P.buf("Hst")
    Hb = sb("Hb", [128, 4, 64], BF16); Hb_b = P.buf("Hb")
    carry = sb("carry", [128, 14]); carry_b = P.bufs(14, "carry")
    Bt = [sb(f"Bt{i}", [128, T + 8]) for i in range(2)]; Bt_b = P.bufs(2, "Bt")
    xwxa = sb("xwxa", [128, T]); xwxa_b = P.buf("xwxa")
    xg = sb("xg", [128, T]); xg_b = P.buf("xg")
    thb = sb("thb", [128, T], BF16); xab = sb("xab", [128, T], BF16); sgx = sb("sgx", [128, T], BF16)
    lor_b = P.buf("lor")
    RT = {}
    for _n in ("rS", "kS", "vS", "sg", "Lc", "E1", "E3", "aG", "kk", "rn", "kkn", "bb", "E2", "tk", "kmod", "E4"):
        RT[_n] = (sb("r_" + _n, [128, T]), P.buf("r_" + _n))
    sqk = sb("sqk", [128, T], BF16); sqk_b = P.buf("sqk")
    Ab = sb("Ab", [128, 8, T], BF16); Ab_b = P.bufs(4, "Ab")
    Btb = sb("Btb", [128, 4, T], BF16); Btb_b = P.bufs(4, "Btb")
    Ktb = sb("Ktb", [128, 4, T], BF16); Ktb_b = P.bufs(4, "Ktb")
    Rb = sb("Rb", [128, 8, T], BF16); Rb_b = P.bufs(4, "Rb")
    vb = sqb[:, 0:4, :]; vb_b = sqb_b[0:4]
    Bhb = sqb[:, 4:8, :]; Bhb_b = sqb_b[4:8]
    Khb = sb("Khb", [128, 4, T], BF16); Khb_b = P.bufs(4, "Khb")
    Vtm = sb("Vtm", [128, NTB, 512], BF16); Vtm_b = P.bufs(NTB, "Vtm")
    Bhtm = sb("Bhtm", [128, NTB, 512], BF16); Bhtm_b = P.bufs(NTB, "Bhtm")
    Khtmc = [sb(f"Khtm{i}", [128, NTB, 512], BF16) for i in range(2)]; Khtm_b = P.bufs(NTB, "Khtm")
    bon = arena[:, 0:8 * T].bitcast(F32).rearrange("p (c t) -> p c t", c=4)
    gG = arena[:, 8 * T:16 * T].bitcast(F32).rearrange("p (c t) -> p c t", c=4)
    bon_b = [(actT_b[2 * c], actT_b[2 * c + 1]) for c in range(4)]
    gG_b = [(actT_b[8 + 2 * c], actT_b[8 + 2 * c + 1]) for c in range(4)]
    yR = sb("yR", [128, 4, T]); yR_b = P.buf("yR")
    gL = sb("gL", [128, 4, NCH]); gL_b = P.buf("gL")
    LL = []
    for _i in range(1):
        LL.append({n: (sb(f"{n}{_i}", [128, 8, 128], BF16), P.bufs(2, f"{n}{_i}")) for n in ("TtA", "AkT", "ArbT", "ArkT")})
    MM = [(sb(f"Mm{i}", [128, 4, 128], BF16), P.buf(f"Mm{i}")) for i in range(4)]
    Xb = sb("Xb", [128, 512], BF16); Xb_b = P.buf("Xb")
    Ubc = [sb(f"Ub{i}", [128, 512], BF16) for i in range(2)]; Ub_b = P.buf("Ub")

    pbank = [es.enter_context(nc.psum_tensor(f"pb{i}", [128, 512], F32)) for i in range(8)]
    pbank_b = P.bufs(8, "pb")
    rrA = [0]

    def bankA():
        k = rrA[0] % 4
        rrA[0] += 1
        return pbank[k], pbank_b[k]

    def bankB(i):
        return pbank[4 + i], pbank_b[4 + i]

    def dma(q, out, in_, reads, writes, key):
        return P.add(q, lambda e: e.dma_start(out=out, in_=in_), reads, writes, dkey=key)

    def mm(out, lhsT, rhs, start, stop, reads, writes):
        return P.add("pe", lambda e: e.matmul(out, lhsT=lhsT, rhs=rhs, start=start, stop=stop), reads, writes)

    def tr(out, in_, ident, reads, writes):
        return P.add("pe", lambda e: e.transpose(out, in_, ident), reads, writes)

    def act(out, in_, func, reads, writes, bias=0.0, scale=1.0, accum_out=None):
        if accum_out is None:
            return P.add("act", lambda e: e.activation(out=out, in_=in_, func=func, bias=bias, scale=scale), reads, writes)
        return P.add("act", lambda e: e.activation(out=out, in_=in_, func=func, bias=bias, scale=scale, accum_out=accum_out), reads, writes)

    def tt(eng, out, in0, in1, op, reads, writes):
        return P.add(eng, lambda e: e.tensor_tensor(out=out, in0=in0, in1=in1, op=op), reads, writes)

    def ts(eng, out, in0, s1, s2, op0, op1, reads, writes):
        if op1 is None:
            return P.add(eng, lambda e: e.tensor_scalar(out=out, in0=in0, scalar1=s1, scalar2=None, op0=op0), reads, writes)
        return P.add(eng, lambda e: e.tensor_scalar(out=out, in0=in0, scalar1=s1, scalar2=s2, op0=op0, op1=op1), reads, writes)

    def stt(eng, out, in0, scalar, in1, op0, op1, reads, writes):
        return P.add(eng, lambda e: e.scalar_tensor_tensor(out=out, in0=in0, scalar=scalar, in1=in1, op0=op0, op1=op1), reads, writes)

    def cp(eng, out, in_, reads, writes):
        if eng == "act":
            return P.add(eng, lambda e: e.activation(out=out, in_=in_, func=AF.Identity), reads, writes)
        return P.add(eng, lambda e: e.tensor_copy(out=out, in_=in_), reads, writes)

    def memset(eng, ap, val, writes):
        return P.add(eng, lambda e: e.memset(ap, val), (), writes)

    for k, src in wsrc.items():
        if k in TILED:
            for j in range(NJ):
                dma("pool", wb[k][j].rearrange("p (kc n) -> p kc n", kc=8),
                    src[:, j * 128:(j + 1) * 128].rearrange("(kc p) n -> p kc n", p=128), (), (wb_buf[k],), "wc_" + k)
            continue
        rows = src.shape[0]
        step = 256 if rows >= 256 else rows
        for r0 in range(0, rows, step):
            r1 = min(rows, r0 + step)
            dma("pool", wb[k][r0:r1, :], src[r0:r1, :], (), (wb_buf[k],), "wc_" + k)

    dma("sp", vt[:], vt_d, (), (vt_b,), "c_vt")
    dma("sp", cst[:], cst_d, (), (cst_b,), "c_cst")
    dma("sp", gat[:], gat_d, (), (kconst_b,), "c_gat")

    cp("dve", ident_bf[:], cst[:, C_ID:C_ID + 128], (cst_b,), (kconst_b,))
    memset("dve", ones_bf[:], 1.0, (kconst_b,))
    cp("dve", bones_bf[:], cst[:, C_BO:C_BO + 128], (cst_b,), (kconst_b,))
    ts("dve", bones_f[:], cst[:, C_BO:C_BO + 128], 1.0 / 64.0, None, ALU.mult, None, (cst_b,), (kconst_b,))
    for i in range(4):
        cp("dve", mL4[:, i, :], cst[:, C_ML:C_ML + 128], (cst_b,), (kconst_b,))
        cp("dve", mU4[:, i, :], cst[:, C_MU:C_MU + 128], (cst_b,), (kconst_b,))
        cp("dve", mUI4[:, i, :], cst[:, C_MUI:C_MUI + 128], (cst_b,), (kconst_b,))
        cp("dve", id4[:, i, :], cst[:, C_ID:C_ID + 128], (cst_b,), (kconst_b,))
    ident_f = cst[:, C_ID:C_ID + 128]

    act(cact[:], vt[:, V_CT:V_CT + 8], AF.Silu, (vt_b,), (cact_b,))
    mps, mps_b = bankA()
    for j in range(72):
        s = 0
        dma("sp", wmods[s], wmod_d[:, j * 128:(j + 1) * 128].rearrange("(kc p) n -> p kc n", p=128),
            (), (wmods_b[s],), "xin0")
        for kc in range(8):
            mm(mps[:, j:j + 1], wmods[s][:, kc, :], cact[:, kc:kc + 1],
               kc == 0, kc == 7, (wmods_b[s], cact_b), (mps_b,))
    tt("dve", modT[:], mps[:, 0:72], vt[:, V_BM:V_BM + 72], ALU.add, (mps_b, vt_b), (mod_b,))

    DR_GM1, DR_GM2, DR_GM3, DR_CG1, DR_CG3, DR_GF, DR_GQ, DR_GKV = 0, 8, 16, 24, 32, 40, 48, 51
    DR_OMR, DR_OMK, DR_OMV, DR_OMWA, DR_OMG, DR_OMA, DR_LNW8 = 53, 57, 61, 65, 66, 67, 71
    SH1, SC1, GT1, SH2, SC2, GT2, SH3, SC3, GT3 = [8 * i for i in range(9)]

    def dts(out_c, n, in_ap, s1, s2, op0, op1):
        ts("dve", drv[:, out_c:out_c + n], in_ap, s1, s2, op0, op1, (mod_b, vt_b, drv_b), (drv_b,))

    for (dst, sc, g) in ((DR_GM1, SC1, V_G1), (DR_GM2, SC2, V_G2), (DR_GM3, SC3, V_G3)):
        dts(dst, 8, modT[:, sc:sc + 8], 1.0, 32.0, ALU.add, ALU.mult)
        tt("dve", drv[:, dst:dst + 8], drv[:, dst:dst + 8], vt[:, g:g + 8], ALU.mult, (drv_b, vt_b), (drv_b,))
    dts(DR_CG1, 8, modT[:, GT1:GT1 + 8], 0.5, None, ALU.mult, None)
    dts(DR_CG3, 8, modT[:, GT3:GT3 + 8], 0.5, None, ALU.mult, None)
    dts(DR_GF, 8, vt[:, V_GF:V_GF + 8], 32.0, None, ALU.mult, None)
    dts(DR_GQ, 3, vt[:, V_GQ:V_GQ + 3], math.sqrt(384.0), None, ALU.mult, None)
    dts(DR_GKV, 2, vt[:, V_GKV:V_GKV + 2], 16.0, None, ALU.mult, None)
    dts(DR_OMR, 14, vt[:, V_MIXR:V_MIXR + 14], -1.0, 1.0, ALU.mult, ALU.add)
    dts(DR_OMA, 4, vt[:, V_KA:V_KA + 4], -1.0, 1.0, ALU.mult, ALU.add)
    dts(DR_LNW8, 4, vt[:, V_LNW:V_LNW + 4], 8.0, None, ALU.mult, None)
    CONST = (drv_b, mod_b, vt_b, kconst_b, cst_b)
    epst = sb("epst", [128, 16])
    eps_cols = {}

    def eps_ap(val, np_=128):
        if val not in eps_cols:
            k = len(eps_cols)
            eps_cols[val] = k
            memset("dve", epst[:, k:k + 1], float(val), (kconst_b,))
        k = eps_cols[val]
        return epst[0:np_, k:k + 1]

    for _v in (D * 1e-6, 384 * 1e-6, 256 * 1e-6, 1e-30, 64 * 64e-5, -PI, 0.0):
        eps_ap(_v)


    def ldw(out, in_, k):
        dma("sp", out, in_, (wb_buf[k],), (wres_b,), "wres")
    ldw(wuq_sb[:], wb["wuq"].rearrange("(kc p) n -> p kc n", p=128), "wuq")
    wuq4 = wb["wuq"].rearrange("(kc p) (h d) -> p kc h d", p=128, d=192)
    for kc in range(3):
        ldw(wuq_rs[:, kc, :, 0:32], wuq4[:, kc, :, 160:192], "wuq")
        ldw(wuq_rs[:, kc, :, 32:64], wuq4[:, kc, :, 128:160], "wuq")
    ldw(wukv_sb[:], wb["wukv"].rearrange("(kc p) n -> p kc n", p=128), "wukv")
    wukv5 = wb["wukv"].rearrange("(kc p) (h two d) -> p kc h two d", p=128, two=2, d=128)
    for kc in range(2):
        ldw(wuv_sb[:, kc, :, :], wukv5[:, kc, :, 1, :], "wukv")
    win3 = wb["win"].rearrange("(kc p) n -> p kc n", p=128)
    ldw(wkr_sb[:, :, 0:64], win3[:, :, 640:704], "win")
    ldw(wkr_sb[:, :, 64:96], win3[:, :, 672:704], "win")
    ldw(wkr_sb[:, :, 96:128], win3[:, :, 640:672], "win")
    ldw(wa_sb[0:64, :], wb["w2"], "w2")
    ldw(wa_sb[64:128, :], wb["a2"], "a2")
    ldw(g2_sb[:], wb["g2"], "g2")
    memset("pool", Vst[:, :, :, 128:129], 1.0, (Vst_b,))

    dbg_n = [0]

    def dump(name, ap, bufs, n):
        if name not in dbg_d:
            return
        k = dbg_n[0]; dbg_n[0] += 1
        tmp = sb(f"dbgtmp{k}", [128, n])
        tb_ = P.buf()
        cp("dve", tmp[:], ap, tuple(bufs), (tb_,))
        dma("pool", dbg_d[name], tmp[:], (tb_,), (), f"dbg{k}")

    memset("pool", Ab[:], 0.0, tuple(Ab_b))
    memset("pool", Rb[:], 0.0, tuple(Rb_b))
    for _i in range(2):
        memset("pool", Khtmc[_i][:], 0.0, tuple(Khtm_b))
        memset("pool", Ubc[_i][:], 0.0, (Ub_b,))
    memset("pool", Hst[:], 0.0, (Hst_b,))
    memset("pool", Hb[:], 0.0, (Hb_b,))
    memset("pool", carry[:], 0.0, tuple(carry_b))

    tmpS = sb("tmpS", [128, T]); tmpS_b = P.buf("tmpS")

    def rsqrt_ps(ps, ps_b, addc, np_=128, n=T, dst=None, dst_b=None):
        dst = tmpA if dst is None else dst
        dst_b = tmpA_b if dst_b is None else dst_b
        act(tmpS[0:np_, 0:n], ps[0:np_, 0:n], AF.Sqrt, (ps_b,), (tmpS_b,), bias=eps_ap(addc, np_))
        P.add("dve", lambda e: e.reciprocal(out=dst[0:np_, 0:n], in_=tmpS[0:np_, 0:n]), (tmpS_b,), (dst_b,))

    def norm_mod(gm_ap_fn, sh_ap_fn, dst, dst_b, Dn=D, src=None, src_b=None, nchunk=8, eps=1e-6):
        src = hT if src is None else src
        src_b = hT_b if src_b is None else src_b
        for c in range(nchunk):
            act(sqb[:, c, :], src[:, c, :], AF.Square, (src_b[c],), (sqb_b[c],))
        ps, ps_b = bankA()
        for c in range(nchunk):
            mm(ps[:, 0:T], ones_bf[:], sqb[:, c, :], c == 0, c == nchunk - 1, (sqb_b[c], kconst_b), (ps_b,))
        rsqrt_ps(ps, ps_b, Dn * eps)
        for c in range(nchunk):
            if sh_ap_fn is None:
                stt("dve", dst[:, c, :], src[:, c, :], gm_ap_fn(c), tmpA[:], ALU.mult, ALU.mult,
                    (src_b[c], tmpA_b) + CONST, (dst_b[c],))
            else:
                k = c % 2
                tt("dve", tmpB[k][:], src[:, c, :], tmpA[:], ALU.mult, (src_b[c], tmpA_b), (tmpB_b[k],))
                act(dst[:, c, :], tmpB[k][:], AF.Identity, (tmpB_b[k],) + CONST, (dst_b[c],),
                    bias=sh_ap_fn(c), scale=gm_ap_fn(c))

    wslot = {"g": 0, "u": 0, "d": 0}

    def ffn(kg, ku, kd, cg_col):
        for j in range(NJ):
            sg_ = wslot["g"] % 2; wslot["g"] += 1
            su_ = wslot["u"] % 2; wslot["u"] += 1
            dma("sp", wgs[sg_][:], wb[kg][j].rearrange("p (kc n) -> p kc n", kc=8),
                (wb_buf[kg],), (wgs_b[sg_],), f"wgs{sg_}")
            dma("sp", wus[su_][:], wb[ku][j].rearrange("p (kc n) -> p kc n", kc=8),
                (wb_buf[ku],), (wus_b[su_],), f"wus{su_}")
            gps, gps_b = bankA()
            ups, ups_b = bankA()
            for kc in range(8):
                mm(gps[:, 0:T], wgs[sg_][:, kc, :], uT[:, kc, :], kc == 0, kc == 7,
                   (wgs_b[sg_], uT_b[kc]), (gps_b,))
            for kc in range(8):
                mm(ups[:, 0:T], wus[su_][:, kc, :], uT[:, kc, :], kc == 0, kc == 7,
                   (wus_b[su_], uT_b[kc]), (ups_b,))
            k = j % 2
            act(sgt[k][:], gps[:, 0:T], AF.Silu, (gps_b,), (sgt_b[k],))
            tt("dve", actT[:, j, :], sgt[k][:], ups[:, 0:T], ALU.mult, (sgt_b[k], ups_b), (actT_b[j],))
        for half in range(2):
            for jg in range(6):
                nj = 4 if jg < 5 else 2
                sd_ = wslot["d"] % 2; wslot["d"] += 1
                dma("sp", wds[sd_][:, 0:nj, :],
                    wb[kd][jg * 512:jg * 512 + nj * 128, half * 512:(half + 1) * 512].rearrange("(j p) n -> p j n", p=128),
                    (wb_buf[kd],), (wds_b[sd_],), f"wds{sd_}")
                for jj in range(nj):
                    j = 4 * jg + jj
                    for c4 in range(4):
                        bk, bk_b = bankB(c4)
                        mm(bk[:, 0:T], wds[sd_][:, jj, c4 * 128:(c4 + 1) * 128], actT[:, j, :], j == 0, j == NJ - 1,
                           (wds_b[sd_], actT_b[j]), (bk_b,))
            for c4 in range(4):
                c = half * 4 + c4
                bk, bk_b = bankB(c4)
                stt("dve", hT[:, c, :], bk[:, 0:T], drv[:, cg_col + c:cg_col + c + 1], hT[:, c, :], ALU.mult, ALU.add,
                    (bk_b, hT_b[c]) + CONST, (hT_b[c],))


    winslot = [0]
    w128slot = [0]
    kvslot = [0]
    ptslot = [0]
    rtslot = [0]

    def load_win(key, c0, ncols):
        k = winslot[0] % 2; winslot[0] += 1
        dma("sp", wins[k][:, :, 0:ncols], wb[key][:, c0:c0 + ncols].rearrange("(kc p) n -> p kc n", p=128),
            (wb_buf[key],), (wins_b[k],), f"wins{k}")
        return wins[k], wins_b[k]

    def load_w128(key, c0):
        k = w128slot[0] % 2; w128slot[0] += 1
        dma("sp", w128[k][:], wb[key][:, c0:c0 + 128].rearrange("(kc p) n -> p kc n", p=128),
            (wb_buf[key],), (w128_b[k],), f"w128_{k}")
        return w128[k], w128_b[k]

    def proj(lhs_fn, w_b, M=128):
        ps, ps_b = bankA()
        for kc in range(8):
            mm(ps[0:M, 0:T], lhs_fn(kc), uT[:, kc, :], kc == 0, kc == 7, (w_b, uT_b[kc]), (ps_b,))
        return ps, ps_b

    def rt():
        k = rtslot[0] % 4; rtslot[0] += 1
        return rtmp[k], rtmp_b[k]

    def rope_combine(psr, psr_b, psrs, psrs_b, dst_ap, dst_bufs):
        t1, t1_b = rt()
        t2, t2_b = rt()
        tt("dve", t1[0:64, :], psr[0:64, 0:T], cosT[0:64, :], ALU.mult, (psr_b, rope_b), (t1_b,))
        tt("dve", t2[0:64, :], psrs[0:64, 0:T], sinS[0:64, :], ALU.mult, (psrs_b, rope_b), (t2_b,))
        tt("pool", dst_ap, t1[0:64, :], t2[0:64, :], ALU.add, (t1_b, t2_b), dst_bufs)

    def mla(it, t0):
        dma("sp", posi[0:64, :], pos_d[:, t0:t0 + T], (), (posi_b,), "posi")
        cp("dve", ang[0:64, :], posi[0:64, :], (posi_b,), (ang_b,))
        ts("dve", ang[0:64, :], ang[0:64, :], cst[0:64, C_INVF:C_INVF + 1], None, ALU.mult, None, (ang_b, cst_b), (ang_b,))
        C1 = 6.28125
        C2 = 2 * PI - C1

        def sin_of(dst, shift):
            ts("dve", ang2[0:64, :], ang[0:64, :], shift, 1.0 / (2 * PI), ALU.add, ALU.mult, (ang_b, rope_b), (ang2_b,))
            cp("dve", posi[0:64, :], ang2[0:64, :], (ang2_b,), (posi_b,))
            cp("dve", ang2[0:64, :], posi[0:64, :], (posi_b,), (ang2_b,))
            t1, t1_b = rt()
            ts("dve", t1[0:64, :], ang[0:64, :], shift, None, ALU.add, None, (ang_b,), (t1_b,))
            stt("dve", t1[0:64, :], ang2[0:64, :], -C1, t1[0:64, :], ALU.mult, ALU.add, (ang2_b, t1_b), (t1_b,))
            stt("dve", t1[0:64, :], ang2[0:64, :], -C2, t1[0:64, :], ALU.mult, ALU.add, (ang2_b, t1_b), (t1_b,))
            ts("dve", ang2[0:64, :], t1[0:64, :], PI, None, ALU.is_gt, None, (t1_b,), (ang2_b,))
            stt("dve", t1[0:64, :], ang2[0:64, :], -2 * PI, t1[0:64, :], ALU.mult, ALU.add, (ang2_b, t1_b), (t1_b,))
            ts("dve", ang2[0:64, :], t1[0:64, :], -PI, None, ALU.is_lt, None, (t1_b,), (ang2_b,))
            stt("dve", t1[0:64, :], ang2[0:64, :], 2 * PI, t1[0:64, :], ALU.mult, ALU.add, (ang2_b, t1_b), (t1_b,))
            ts("dve", t1[0:64, :], t1[0:64, :], PI, -PI, ALU.min, ALU.max, (t1_b,), (t1_b,))
            act(dst, t1[0:64, :], AF.Sin, (t1_b,), (rope_b,))

        sin_of(sinS[0:64, :], 0.0)
        ts("dve", sinS[0:64, :], sinS[0:64, :], cst[0:64, C_SS:C_SS + 1], None, ALU.mult, None, (rope_b, cst_b), (rope_b,))
        sin_of(cosT[0:64, :], 0.5 * PI)

        w, w_b = load_win("win", 0, 256)
        for jj in range(2):
            ps, ps_b = proj(lambda kc, jj=jj, w=w: w[:, kc, jj * 128:(jj + 1) * 128], w_b)
            cp("act", pq[:, jj, :], ps[:, 0:T], (ps_b,), (pq_b[jj],))
        w, w_b = load_win("win", 256, 256)
        ps, ps_b = proj(lambda kc, w=w: w[:, kc, 0:128], w_b)
        cp("act", pq[:, 2, :], ps[:, 0:T], (ps_b,), (pq_b[2],))
        ps, ps_b = proj(lambda kc, w=w: w[:, kc, 128:256], w_b)
        cp("act", pkv[:, 0, :], ps[:, 0:T], (ps_b,), (pkv_b[0],))
        w, w_b = load_win("win", 512, 128)
        ps, ps_b = proj(lambda kc, w=w: w[:, kc, 0:128], w_b)
        cp("act", pkv[:, 1, :], ps[:, 0:T], (ps_b,), (pkv_b[1],))
        psr, psr_b = proj(lambda kc: wkr_sb[:, kc, 0:64], wres_b, M=64)
        psrs, psrs_b = proj(lambda kc: wkr_sb[:, kc, 64:128], wres_b, M=64)
        rope_combine(psr, psr_b, psrs, psrs_b, Krst[0:64, :], (Krst_b,))
        dma("pool", Krc_d[:, t0:t0 + T], Krst[0:64, :], (Krst_b,), (Krc_buf[it],), "krst")

        norm_mod(lambda c: drv[:, DR_GQ + c:DR_GQ + c + 1], None, qn, qn_b, Dn=384, src=pq, src_b=pq_b, nchunk=3)
        norm_mod(lambda c: drv[:, DR_GKV + c:DR_GKV + c + 1], None, kvn, kvn_b, Dn=256, src=pkv, src_b=pkv_b, nchunk=2)

        for h in range(4):
            ps, ps_b = bankA()
            for kc in range(3):
                mm(ps[:, 0:T], wuq_sb[:, kc, 192 * h:192 * h + 128], qn[:, kc, :], kc == 0, kc == 2,
                   (wres_b, qn_b[kc]), (ps_b,))
            cp("act", QnT[:, h, :], ps[:, 0:T], (ps_b,), (QnT_b[h],))
            psr, psr_b = bankA()
            for kc in range(3):
                mm(psr[0:64, 0:T], wuq_sb[:, kc, 192 * h + 128:192 * h + 192], qn[:, kc, :], kc == 0, kc == 2,
                   (wres_b, qn_b[kc]), (psr_b,))
            psrs, psrs_b = bankA()
            for kc in range(3):
                mm(psrs[0:64, 0:T], wuq_rs[:, kc, h, :], qn[:, kc, :], kc == 0, kc == 2,
                   (wres_b, qn_b[kc]), (psrs_b,))
            rope_combine(psr, psr_b, psrs, psrs_b, QrT[0:64, h, :], (QrT_b[h],))
        for h in range(4):
            ps, ps_b = bankA()
            for kc in range(2):
                mm(ps[:, 0:T], wukv_sb[:, kc, 256 * h:256 * h + 128], kvn[:, kc, :], kc == 0, kc == 1,
                   (wres_b, kvn_b[kc]), (ps_b,))
            cp("act", Kst[:, h, :], ps[:, 0:T], (ps_b,), (Kst_b,))
        dma("pool", Kc_d[:, :, t0:t0 + T].rearrange("h p t -> p h t"), Kst[:], (Kst_b,),
            tuple(Kc_buf[h][it] for h in range(4)), "kst")
        for tb in range(NTB):
            ps, ps_b = bankA()
            for kc in range(2):
                mm(ps[:, 0:512], kvn[:, kc, tb * 128:(tb + 1) * 128], wuv_sb[:, kc, :, :].rearrange("p h d -> p (h d)"),
                   kc == 0, kc == 1, (wres_b, kvn_b[kc]), (ps_b,))
            cp("dve", Vst[:, :, tb, 0:128], ps[:, 0:512].rearrange("p (h d) -> p h d", h=4), (ps_b,), (Vst_b,))
        for h in range(4):
            dma("pool", Vc_d[h, it], Vst[:, h, :, :].rearrange("p t d -> p (t d)"), (Vst_b,), (Vc_buf[h][it],), f"vst{h}")

        for h in range(4):
            opsl = [bankB(qs) for qs in range(NTB)]
            for jg in range(0, it + 1, KVG):
                njt = min(KVG, it + 1 - jg)
                k = kvslot[0] % 2; kvslot[0] += 1
                dma("sp", Kt[k][:, 0:njt * T], Kc_d[h][:, jg * T:(jg + njt) * T],
                    tuple(Kc_buf[h][jg:jg + njt]), (Kt_b[k],), f"kt{k}")
                dma("sp", Krt[k][0:64, 0:njt * T], Krc_d[:, jg * T:(jg + njt) * T],
                    tuple(Krc_buf[jg:jg + njt]), (Krt_b[k],), f"krt{k}")
                dma("sp", Vt[k][:, 0:njt, :], Vc_d[h, jg:jg + njt].rearrange("t p d -> p t d"),
                    tuple(Vc_buf[h][jg:jg + njt]), (Vt_b[k],), f"vt{k}")
                for jj in range(njt):
                    jt = jg + jj
                    for kb in range(NTB):
                        diag = jt == it
                        q0 = kb * 128 if diag else 0
                        ko = jj * T + kb * 128
                        sps, sps_b = bankA()
                        mm(sps[:, q0:T], Kt[k][:, ko:ko + 128], QnT[:, h, q0:T], True, False,
                           (Kt_b[k], QnT_b[h]), (sps_b,))
                        mm(sps[:, q0:T], Krt[k][0:64, ko:ko + 128], QrT[0:64, h, q0:T], False, True,
                           (Krt_b[k], QrT_b[h]), (sps_b,))
                        p = ptslot[0] % 3; ptslot[0] += 1
                        act(PT[p][:, q0:T], sps[:, q0:T], AF.Exp, (sps_b,), (PT_b[p],), scale=SCALE)
                        if diag:
                            memset("pool", PT[p][64:128, q0:q0 + 64], 0.0, (PT_b[p],))
                        for qs in range(q0 // 128, NTB):
                            first = (jt == 0 and kb == 0)
                            last = (diag and kb == qs)
                            ops, ops_b = opsl[qs]
                            mm(ops[:, 0:129], PT[p][:, qs * 128:(qs + 1) * 128],
                               Vt[k][:, jj, kb * 129:(kb + 1) * 129], first, last, (PT_b[p], Vt_b[k]), (ops_b,))
            for qs in range(NTB):
                ops, ops_b = opsl[qs]
                P.add("dve", lambda e, qs=qs, h=h, ops=ops: e.reciprocal(out=rec[:, qs:qs + 1], in_=ops[:, 128:129]),
                      (ops_b,), (rec_b,))
                ts("dve", ya[:, qs, h * 128:(h + 1) * 128], ops[:, 0:128], rec[:, qs:qs + 1], None,
                   ALU.mult, None, (ops_b, rec_b), (ya_b[qs],))

        for qs in range(NTB):
            t1, t1_b = rt()
            t2, t2_b = rt()
            act(t1[:], ya[:, qs, 0:T], AF.Square, (ya_b[qs],), (t1_b,))
            act(t2[:], ya[:, qs, T:2 * T], AF.Square, (ya_b[qs],), (t2_b,))
            tt("dve", t1[:], t1[:], t2[:], ALU.add, (t1_b, t2_b), (t1_b,))
            P.add("dve", lambda e, qs=qs, t1=t1: e.reduce_sum(out=ssq[:, qs:qs + 1], in_=t1[:], axis=AX.X), (t1_b,), (ssq_b,))
        if it == NT - 1:
            dump("cos", cosT[:], (rope_b,), T)
            dump("sin", sinS[:], (rope_b,), T)
            dump("qn0", QnT[:, 0, :], (QnT_b[0],), T)
            dump("qr0", QrT[:, 0, :], (QrT_b[0],), T)
            dump("ya0", ya[:, 0, :], (ya_b[0],), 512)
            dump("kst0", Kst[:, 0, :], (Kst_b,), T)
            dump("krst", Krst[:], (Krst_b,), T)
        ts("dve", ssq[:, 0:NTB], ssq[:, 0:NTB], 1.0 / 512.0, 1e-6, ALU.mult, ALU.add, (ssq_b,), (ssq_b,))
        act(ssq[:, 0:NTB], ssq[:, 0:NTB], AF.Sqrt, (ssq_b,), (ssq_b,))
        P.add("dve", lambda e: e.reciprocal(out=ssq[:, 0:NTB], in_=ssq[:, 0:NTB]), (ssq_b,), (ssq_b,))
        for qs in range(NTB):
            stt("dve", yab[:, qs, :], ya[:, qs, :], ssq[:, qs:qs + 1], gat[:], ALU.mult, ALU.mult,
                (ya_b[qs], ssq_b, kconst_b), (yab_b[qs],))
        for cch in range(4):
            ps, ps_b = bankA()
            psbf = ps[:].bitcast(BF16)
            for qs in range(NTB):
                tr(psbf[:, qs * 128:(qs + 1) * 128], yab[:, qs, cch * 128:(cch + 1) * 128], ident_bf[:],
                   (yab_b[qs], kconst_b), (ps_b,))
            cp("act", yT[:, cch, :], psbf[:, 0:T], (ps_b,), (yT_b[cch],))


    btslot = [0]

    SSUB = int(os.environ.get("SSUB", "9"))
    RSUB = int(os.environ.get("RSUB", "9"))

    def shift_evac(ps, ps_b, ci, mixcol, omcol, dst, dst_b):
        k = btslot[0] % 2; btslot[0] += 1
        ts("dve", Bt[k][:, 8:T + 8], ps[:, 0:T], vt[:, mixcol:mixcol + 1], None, ALU.mult, None, (ps_b, vt_b), (Bt_b[k],))
        if SSUB <= 1:
            return
        cp("pool", Bt[k][:, 7:8], carry[:, ci:ci + 1], (carry_b[ci],), (Bt_b[k],))
        if SSUB <= 2:
            return
        ts("dve", dst, ps[:, 0:T], drv[:, omcol:omcol + 1], None, ALU.mult, None, (ps_b,) + CONST, (dst_b,))
        if SSUB <= 3:
            return
        tt("pool", dst, dst, Bt[k][:, 7:T + 7], ALU.add, (dst_b, Bt_b[k]), (dst_b,))
        if SSUB <= 4:
            return
        cp("pool", carry[:, ci:ci + 1], Bt[k][:, T + 7:T + 8], (Bt_b[k],), (carry_b[ci],))

    def R_(n):
        return RT[n][0][:], RT[n][1]


    RCUT = int(os.environ.get("RCUT", "9"))

    def rwkv_fill():
        for c in range(4, 8):
            memset("pool", yT[:, c, :], 0.0, (yT_b[c],))

    def rwkv(it, t0):
        w, w_b = load_win("win", 2240, 256)
        ps, ps_b = proj(lambda kc, w=w: w[:, kc, 0:128], w_b)
        if RSUB <= 1:
            cp("act", xwxa[:], ps[:, 0:T], (ps_b,), (xwxa_b,))
            return rwkv_fill()
        shift_evac(ps, ps_b, 12, V_MIXWA, DR_OMWA, xwxa[:], xwxa_b)
        if RSUB <= 2:
            return rwkv_fill()
        ps, ps_b = proj(lambda kc, w=w: w[:, kc, 128:256], w_b)
        shift_evac(ps, ps_b, 13, V_MIXG, DR_OMG, xg[:], xg_b)
        if RSUB <= 3:
            return rwkv_fill()
        act(thb[0:64, :], xwxa[0:64, :], AF.Tanh, (xwxa_b,), (lor_b,))
        if RSUB <= 4:
            return rwkv_fill()
        cp("dve", xab[64:128, :], xwxa[64:128, :], (xwxa_b,), (lor_b,))
        if RSUB <= 5:
            return rwkv_fill()
        act(sgx[:], xg[:], AF.Sigmoid, (xg_b,), (lor_b,))
        if RCUT <= 1:
            return rwkv_fill()
        rS, rS_b = R_("rS"); kS, kS_b = R_("kS"); vS, vS_b = R_("vS"); sg, sg_b = R_("sg"); Lc, Lc_b = R_("Lc")
        E1, E1_b = R_("E1"); E3, E3_b = R_("E3"); aG, aG_b = R_("aG"); kk, kk_b = R_("kk"); rn, rn_b = R_("rn")
        kkn, kkn_b = R_("kkn"); bb, bb_b = R_("bb"); E2, E2_b = R_("E2"); tk, tk_b = R_("tk"); kmod, kmod_b = R_("kmod")
        E4, E4_b = R_("E4"); rk, rk_b = R_("bb"); yc, yc_b = R_("kkn"); yn, yn_b = R_("kk")
        for cc in range(4):
            sl = slice(cc * 128, (cc + 1) * 128)
            for (dst, dst_b, c0, ci, mixc, omc) in ((rS, rS_b, 704, cc, V_MIXR + cc, DR_OMR + cc),
                                                   (kS, kS_b, 1216, 4 + cc, V_MIXK + cc, DR_OMK + cc),
                                                   (vS, vS_b, 1728, 8 + cc, V_MIXV + cc, DR_OMV + cc)):
                w, w_b = load_w128("win", c0 + 128 * cc)
                ps, ps_b = proj(lambda kc, w=w: w[:, kc, :], w_b)
                shift_evac(ps, ps_b, ci, mixc, omc, dst, dst_b)
            ps, ps_b = bankA()
            mm(ps[:, 0:T], wa_sb[0:64, sl], thb[0:64, :], True, True, (wres_b, lor_b), (ps_b,))
            act(sg, ps[:, 0:T], AF.Sigmoid, (ps_b,) + CONST, (sg_b,), bias=vt[:, V_W0 + cc:V_W0 + cc + 1])
            if os.environ.get("NOSCAN"):
                cp("dve", Lc, sg, (sg_b,), (Lc_b,))
            else:
                P.add("dve", lambda e: e.tensor_tensor_scan(out=Lc, data0=cst[:, C_RST:C_RST + T], data1=sg, initial=0.0,
                                                            op0=ALU.mult, op1=ALU.add), (sg_b, cst_b), (Lc_b,))
            act(E1, Lc, AF.Exp, (Lc_b,), (E1_b,), scale=-KAPPA)
            cp("pool", gL[:, cc, :], E1.rearrange("p (c t) -> p c t", t=64)[:, :, 63], (E1_b,), (gL_b,))
            tt("dve", Rb[0:64, 2 * cc, :], rS[0:64, :], E1[0:64, :], ALU.mult, (rS_b, E1_b), (Rb_b[cc],))
            tt("dve", Rb[64:128, 2 * cc + 1, :], rS[64:128, :], E1[64:128, :], ALU.mult, (rS_b, E1_b), (Rb_b[cc],))
            tt("pool", E3, Lc, sg, ALU.subtract, (Lc_b, sg_b), (E3_b,))
            act(E3, E3, AF.Exp, (E3_b,), (E3_b,), scale=-KAPPA)
            ps, ps_b = bankA()
            mm(ps[:, 0:T], wa_sb[64:128, sl], xab[64:128, :], True, True, (wres_b, lor_b), (ps_b,))
            act(aG, ps[:, 0:T], AF.Sigmoid, (ps_b,) + CONST, (aG_b,), bias=vt[:, V_A0 + cc:V_A0 + cc + 1])
            ts("dve", kk, kS, vt[:, V_KK + cc:V_KK + cc + 1], None, ALU.mult, None, (kS_b, vt_b), (kk_b,))
            act(sqk[:], kk, AF.Square, (kk_b,), (sqk_b,))
            ps, ps_b = bankA()
            mm(ps[:, 0:T], bones_bf[:], sqk[:], True, True, (sqk_b, kconst_b), (ps_b,))
            rsqrt_ps(ps, ps_b, 1e-30, dst=RT["rn"][0], dst_b=rn_b)
            tt("dve", kkn, kk, rn, ALU.mult, (kk_b, rn_b), (kkn_b,))
            stt("dve", Ab[0:64, 2 * cc, :], kkn[0:64, :], -1.0, E3[0:64, :], ALU.mult, ALU.mult, (kkn_b, E3_b), (Ab_b[cc],))
            stt("dve", Ab[64:128, 2 * cc + 1, :], kkn[64:128, :], -1.0, E3[64:128, :], ALU.mult, ALU.mult, (kkn_b, E3_b), (Ab_b[cc],))
            tt("pool", bb, kkn, aG, ALU.mult, (kkn_b, aG_b), (bb_b,))
            act(E2, Lc, AF.Exp, (Lc_b,), (E2_b,), scale=KAPPA)
            tt("dve", Btb[:, cc, :], bb, E2, ALU.mult, (bb_b, E2_b), (Btb_b[cc],))
            ts("dve", tk, aG, vt[:, V_KA + cc:V_KA + cc + 1], drv[:, DR_OMA + cc:DR_OMA + cc + 1], ALU.mult, ALU.add,
               (aG_b,) + CONST, (tk_b,))
            tt("pool", kmod, kS, tk, ALU.mult, (kS_b, tk_b), (kmod_b,))
            tt("dve", Ktb[:, cc, :], kmod, E2, ALU.mult, (kmod_b, E2_b), (Ktb_b[cc],))
            Lc3 = Lc.rearrange("p (c t) -> p c t", t=64)
            if os.environ.get("NOBC"):
                tt("pool", E4, Lc, Lc, ALU.subtract, (Lc_b,), (E4_b,))
            else:
                tt("pool", E4.rearrange("p (c t) -> p c t", t=64), Lc3[:, :, 63:64].broadcast_to([128, NCH, 64]), Lc3,
                   ALU.subtract, (Lc_b,), (E4_b,))
            act(E4, E4, AF.Exp, (E4_b,), (E4_b,), scale=-KAPPA)
            tt("dve", Bhb[:, cc, :], bb, E4, ALU.mult, (bb_b, E4_b), (Bhb_b[cc],))
            tt("pool", Khb[:, cc, :], kmod, E4, ALU.mult, (kmod_b, E4_b), (Khb_b[cc],))
            cp("act", vb[:, cc, :], vS, (vS_b,), (vb_b[cc],))
            tt("pool", rk, rS, kmod, ALU.mult, (rS_b, kmod_b), (rk_b,))
            ts("dve", sqk[:], rk, vt[:, V_RK + cc:V_RK + cc + 1], None, ALU.mult, None, (rk_b, vt_b), (sqk_b,))
            ps, ps_b = bankA()
            mm(ps[:, 0:T], bones_bf[:], sqk[:], True, True, (sqk_b, kconst_b), (ps_b,))
            tt("dve", bon[:, cc, :], ps[:, 0:T], vS, ALU.mult, (ps_b, vS_b), bon_b[cc])
            ps, ps_b = bankA()
            mm(ps[:, 0:T], g2_sb[:, sl], sgx[:], True, True, (wres_b, lor_b), (ps_b,))
            cp("act", gG[:, cc, :], ps[:, 0:T], (ps_b,), gG_b[cc])

        if RCUT <= 2:
            return rwkv_fill()
        for tb in range(NTB):
            blk = slice(tb * 128, (tb + 1) * 128)
            for (src, src_b, dst, dst_b) in ((vb, vb_b, Vtm, Vtm_b), (Bhb, Bhb_b, Bhtm, Bhtm_b), (Khb, Khb_b, None, Khtm_b)):
                ps, ps_b = bankA()
                psbf = ps[:].bitcast(BF16)
                for cc in range(4):
                    tr(psbf[:, cc * 128:(cc + 1) * 128], src[:, cc, blk], ident_bf[:], (src_b[cc], kconst_b), (ps_b,))
                if dst is None:
                    cp("act", Khtmc[0][0:64, tb, :], psbf[0:64, 0:512], (ps_b,), (dst_b[tb],))
                    cp("act", Khtmc[1][64:128, tb, :], psbf[64:128, 0:512], (ps_b,), (dst_b[tb],))
                else:
                    cp("act", dst[:, tb, :], psbf[:, 0:512], (ps_b,), (dst_b[tb],))

        if RCUT <= 3:
            return rwkv_fill()
        NSUB = int(os.environ.get("NSUB", "99"))
        for tb in range(NTB):
            blk = slice(tb * 128, (tb + 1) * 128)
            L = LL[0]
            TtA, TtA_b = L["TtA"]; AkT, AkT_b = L["AkT"]; ArbT, ArbT_b = L["ArbT"]; ArkT, ArkT_b = L["ArkT"]
            for bi in range(2):
                hs = [4 * bi + i for i in range(4)]

                def opnd(X, h):
                    if X is Ab or X is Rb:
                        return X[:, h, blk]
                    return X[:, h // 2, blk]

                def batch_mm(lfn, rfn, reads):
                    ps, ps_b = bankA()
                    for i, h in enumerate(hs):
                        mm(ps[:, i * 128:(i + 1) * 128], lfn(i, h), rfn(i, h), True, True, reads, (ps_b,))
                    return ps, ps_b

                ab_r = tuple(Ab_b) + tuple(Btb_b) + tuple(Ktb_b) + tuple(Rb_b)
                (M0, M0_b), (M0t, M0t_b), (M1, M1_b), (M1t, M1t_b) = MM
                ps, ps_b = batch_mm(lambda i, h: opnd(Ab, h), lambda i, h: opnd(Btb, h), ab_r)
                tt("dve", M0[:], ps[:].rearrange("p (i s) -> p i s", i=4), mL4[:], ALU.mult, (ps_b, kconst_b), (M0_b,))
                if NSUB <= 1:
                    continue
                ps, ps_b = batch_mm(lambda i, h: opnd(Btb, h), lambda i, h: opnd(Ab, h), ab_r)
                tt("dve", M0t[:], ps[:].rearrange("p (i s) -> p i s", i=4), mU4[:], ALU.mult, (ps_b, kconst_b), (M0t_b,))
                if NSUB <= 2:
                    continue
                tt("pool", TtA[:, 4 * bi:4 * bi + 4, :], M0t[:], id4[:], ALU.add, (M0t_b, kconst_b), (TtA_b[bi],))
                if NSUB <= 3:
                    continue
                cur, cur_b, curt, curt_b = M0, M0_b, M0t, M0t_b
                nxt, nxt_b, nxtt, nxtt_b = M1, M1_b, M1t, M1t_b
                for lvl in range(5):
                    ps, ps_b = batch_mm(lambda i, h: curt[:, i, :], lambda i, h: cur[:, i, :], (cur_b, curt_b))
                    cp("act", nxt[:], ps[:].rearrange("p (i s) -> p i s", i=4), (ps_b,), (nxt_b,))
                    if lvl < 4:
                        ps, ps_b = batch_mm(lambda i, h: cur[:, i, :], lambda i, h: curt[:, i, :], (cur_b, curt_b))
                        cp("act", nxtt[:], ps[:].rearrange("p (i s) -> p i s", i=4), (ps_b,), (nxtt_b,))
                    ps, ps_b = batch_mm(lambda i, h: nxt[:, i, :], lambda i, h: TtA[:, h, :], (nxt_b, TtA_b[bi]))
                    tt("dve", TtA[:, 4 * bi:4 * bi + 4, :], ps[:].rearrange("p (i s) -> p i s", i=4),
                       TtA[:, 4 * bi:4 * bi + 4, :], ALU.add, (ps_b, TtA_b[bi]), (TtA_b[bi],))
                    cur, cur_b, curt, curt_b, nxt, nxt_b, nxtt, nxtt_b = nxt, nxt_b, nxtt, nxtt_b, cur, cur_b, curt, curt_b
                if NSUB <= 4:
                    continue
                ps, ps_b = batch_mm(lambda i, h: opnd(Ktb, h), lambda i, h: opnd(Ab, h), ab_r)
                tt("dve", AkT[:, 4 * bi:4 * bi + 4, :], ps[:].rearrange("p (i s) -> p i s", i=4), mU4[:], ALU.mult,
                   (ps_b, kconst_b), (AkT_b[bi],))
                ps, ps_b = batch_mm(lambda i, h: opnd(Btb, h), lambda i, h: opnd(Rb, h), ab_r)
                tt("dve", ArbT[:, 4 * bi:4 * bi + 4, :], ps[:].rearrange("p (i s) -> p i s", i=4), mUI4[:], ALU.mult,
                   (ps_b, kconst_b), (ArbT_b[bi],))
                ps, ps_b = batch_mm(lambda i, h: opnd(Ktb, h), lambda i, h: opnd(Rb, h), ab_r)
                tt("dve", ArkT[:, 4 * bi:4 * bi + 4, :], ps[:].rearrange("p (i s) -> p i s", i=4), mUI4[:], ALU.mult,
                   (ps_b, kconst_b), (ArkT_b[bi],))

            if RCUT <= 4:
                continue
            LLr = tuple(TtA_b) + tuple(AkT_b) + tuple(ArbT_b) + tuple(ArkT_b)
            for half in range(2):
                ch = 2 * tb + half
                tp = 64 * half
                tsl = slice(tp, tp + 64)
                csl = slice(tb * 128 + tp, tb * 128 + tp + 64)
                Ub = Ubc[half]
                Khtm = Khtmc[half]
                ps, ps_b = bankA()
                for h in range(8):
                    pb, cc = 64 * (h % 2), h // 2
                    hs_ = slice(h * 64, (h + 1) * 64)
                    mm(ps[:, hs_], Ab[:, h, blk], Hb[:, cc, :], True, False, (Ab_b[cc], Hb_b), (ps_b,))
                    mm(ps[:, hs_], AkT[:, h, :], Vtm[:, tb, hs_], False, True, LLr + (Vtm_b[tb],), (ps_b,))
                cp("act", Xb[:], ps[:], (ps_b,), (Xb_b,))
                ps, ps_b = bankA()
                for h in range(8):
                    hs_ = slice(h * 64, (h + 1) * 64)
                    mm(ps[:, hs_], TtA[:, h, :], Xb[:, hs_], True, True, LLr + (Xb_b,), (ps_b,))
                cp("dve", Ub[tsl, :], ps[tsl, :], (ps_b,), (Ub_b,))
                ps, ps_b = bankA()
                for h in range(8):
                    pb, cc = 64 * (h % 2), h // 2
                    hs_ = slice(h * 64, (h + 1) * 64)
                    o = ps[pb:pb + 64, cc * 64:(cc + 1) * 64]
                    mm(o, Hb[:, cc, :], Rb[:, h, csl], True, False, (Hb_b, Rb_b[cc]), (ps_b,))
                    mm(o, Ub[:, hs_], ArbT[:, h, tsl], False, False, LLr + (Ub_b,), (ps_b,))
                    mm(o, Vtm[:, tb, hs_], ArkT[:, h, tsl], False, True, LLr + (Vtm_b[tb],), (ps_b,))
                cp("act", yR[:, :, csl], ps[:, 0:256].rearrange("p (c t) -> p c t", c=4), (ps_b,), (yR_b,))
                ps, ps_b = bankA()
                for h in range(8):
                    pb, cc = 64 * (h % 2), h // 2
                    hs_ = slice(h * 64, (h + 1) * 64)
                    o = ps[pb:pb + 64, cc * 64:(cc + 1) * 64]
                    mm(o, Bhtm[:, tb, hs_], Ub[:, hs_], True, False, (Bhtm_b[tb], Ub_b), (ps_b,))
                    mm(o, Khtm[:, tb, hs_], Vtm[:, tb, hs_], False, True, (Khtm_b[tb], Vtm_b[tb]), (ps_b,))
                tt("dve", Hst[:], Hst[:], gL[:, :, ch:ch + 1].broadcast_to([128, 4, 64]), ALU.mult, (Hst_b, gL_b), (Hst_b,))
                tt("dve", Hst[:], Hst[:], ps[:, 0:256].rearrange("p (c v) -> p c v", c=4), ALU.add, (Hst_b, ps_b), (Hst_b,))
                cp("act", Hb[:], Hst[:], (Hst_b,), (Hb_b,))

        if RCUT <= 5:
            return rwkv_fill()
        for cc in range(4):
            ps, ps_b = bankA()
            mm(ps[:, 0:T], bones_f[:], yR[:, cc, :], True, True, (yR_b, kconst_b), (ps_b,))
            tt("dve", yc, yR[:, cc, :], ps[:, 0:T], ALU.subtract, (yR_b, ps_b), (yc_b,))
            act(sqk[:], yc, AF.Square, (yc_b,), (sqk_b,))
            ps, ps_b = bankA()
            mm(ps[:, 0:T], bones_bf[:], sqk[:], True, True, (sqk_b, kconst_b), (ps_b,))
            rsqrt_ps(ps, ps_b, 64 * 64e-5, dst=RT["rn"][0], dst_b=rn_b)
            tt("dve", yc, yc, rn, ALU.mult, (yc_b, rn_b), (yc_b,))
            act(yn, yc, AF.Identity, (yc_b,) + CONST, (yn_b,), bias=vt[:, V_LNB + cc:V_LNB + cc + 1],
                scale=drv[:, DR_LNW8 + cc:DR_LNW8 + cc + 1])
            tt("pool", yn, yn, bon[:, cc, :], ALU.add, (yn_b,) + bon_b[cc], (yn_b,))
            tt("dve", yT[:, 4 + cc, :], yn, gG[:, cc, :], ALU.mult, (yn_b,) + gG_b[cc], (yT_b[4 + cc],))

    def outproj():
        for g in range(4):
            w, w_b = load_win("wout", g * 256, 256)
            for jj in range(2):
                c = 2 * g + jj
                ps, ps_b = bankA()
                for kc in range(8):
                    mm(ps[:, 0:T], w[:, kc, jj * 128:(jj + 1) * 128], yT[:, kc, :], kc == 0, kc == 7,
                       (w_b, yT_b[kc]), (ps_b,))
                stt("dve", hT[:, c, :], ps[:, 0:T], modT[:, GT2 + c:GT2 + c + 1], hT[:, c, :], ALU.mult, ALU.add,
                    (ps_b, hT_b[c]) + CONST, (hT_b[c],))

    xslot = [0]
    oslot = [0]
    for it in range(NT):
        t0 = it * T
        for tb in range(NTB):
            s = 0
            dma("sp", xin[s][:], x_d[t0 + tb * 128:t0 + (tb + 1) * 128, :], (), (xin_b[s],), "xin0")
            for half in range(2):
                ps, ps_b = bankA()
                for c4 in range(4):
                    c = half * 4 + c4
                    tr(ps[:, c4 * 128:(c4 + 1) * 128], xin[s][:, c * 128:(c + 1) * 128], ident_f,
                       (xin_b[s], cst_b), (ps_b,))
                cp("act", hT[:, half * 4:half * 4 + 4, tb * 128:(tb + 1) * 128],
                   ps[:].rearrange("p (c t) -> p c t", c=4), (ps_b,), tuple(hT_b[half * 4:half * 4 + 4]))

        if "ffn1" in stages:
            norm_mod(lambda c: drv[:, DR_GM1 + c:DR_GM1 + c + 1], lambda c: modT[:, SH1 + c:SH1 + c + 1], uT, uT_b)
            ffn("w1g", "w1u", "w1d", DR_CG1)

        if "mix" in stages:
            norm_mod(lambda c: drv[:, DR_GM2 + c:DR_GM2 + c + 1], lambda c: modT[:, SH2 + c:SH2 + c + 1], uT, uT_b)
            if "nomla" in stages:
                for c in range(4):
                    memset("pool", yT[:, c, :], 0.0, (yT_b[c],))
            else:
                mla(it, t0)
            if "norwkv" in stages:
                for c in range(4, 8):
                    memset("pool", yT[:, c, :], 0.0, (yT_b[c],))
            else:
                rwkv(it, t0)
            outproj()

        if "ffn2" in stages:
            norm_mod(lambda c: drv[:, DR_GM3 + c:DR_GM3 + c + 1], lambda c: modT[:, SH3 + c:SH3 + c + 1], uT, uT_b)
            ffn("w3g", "w3u", "w3d", DR_CG3)

        for c in range(8):
            act(sqb[:, c, :], hT[:, c, :], AF.Square, (hT_b[c],), (sqb_b[c],))
        ps, ps_b = bankA()
        for c in range(8):
            mm(ps[:, 0:T], ones_bf[:], sqb[:, c, :], c == 0, c == 7, (sqb_b[c], kconst_b), (ps_b,))
        rsqrt_ps(ps, ps_b, D * 1e-6)
        for c in range(8):
            stt("dve", hT[:, c, :], hT[:, c, :], drv[:, DR_GF + c:DR_GF + c + 1], tmpA[:], ALU.mult, ALU.mult,
                (hT_b[c], tmpA_b) + CONST, (hT_b[c],))
        for tb in range(NTB):
            s = 0
            for half in range(2):
                ps, ps_b = bankA()
                for c4 in range(4):
                    c = half * 4 + c4
                    tr(ps[:, c4 * 128:(c4 + 1) * 128], hT[:, c, tb * 128:(tb + 1) * 128], ident_f,
                       (hT_b[c], cst_b), (ps_b,))
                cp("act" if half == 0 else "dve", ost[s][:, half * 512:(half + 1) * 512], ps[:], (ps_b,), (ost_b[s],))
            dma("pool", out_d[t0 + tb * 128:t0 + (tb + 1) * 128, :], ost[s][:], (ost_b[s],), (), "xin0o")

    P.emit()

    if os.environ.get("KDBG"):
        print("STATS", P.stats)
    es.close()
    return nc


def _consts():
    c = np.zeros((128, NCST), np.float32)
    p = np.arange(128)
    c[:, C_ID:C_ID + 128] = np.eye(128, dtype=np.float32)
    same = (p[:, None] // 64) == (p[None, :] // 64)
    c[:, C_ML:C_ML + 128] = (same & (p[None, :] < p[:, None])).astype(np.float32)
    c[:, C_MU:C_MU + 128] = (same & (p[:, None] < p[None, :])).astype(np.float32)
    c[:, C_MUI:C_MUI + 128] = (same & (p[:, None] <= p[None, :])).astype(np.float32)
    c[:, C_BO:C_BO + 128] = same.astype(np.float32)
    c[:, C_INVF] = (10000.0 ** (-(np.arange(128) % 32).astype(np.float32) / 32.0)).astype(np.float32)
    c[:, C_SS] = np.where((p % 64) < 32, -1.0, 1.0)
    c[:, C_RST:C_RST + 512] = ((np.arange(512) % 64) != 0).astype(np.float32)[None, :]
    return c


def _vt(inp, b):
    def ch(v):
        v = np.asarray(v, np.float32).reshape(-1, 128)
        return v.T
    sm = inp["rwkv_shift_mix"][0]
    cols = [
        ch(inp["c"][b]), ch(inp["b_mod"][0]), ch(inp["ffn1_norm_g"][0]), ch(inp["mix_norm_g"][0]),
        ch(inp["ffn2_norm_g"][0]), ch(inp["final_norm_g"]), ch(inp["q_norm_g"][0]), ch(inp["kv_norm_g"][0]),
        ch(sm[0:512]), ch(sm[512:1024]), ch(sm[1024:1536]), ch(sm[1536:1664]), ch(sm[1664:1792]),
        ch(inp["rwkv_w0"][0]), ch(inp["rwkv_a0"][0]), ch(inp["rwkv_k_k"][0]), ch(inp["rwkv_k_a"][0]),
        ch(inp["rwkv_r_k"][0].reshape(-1)), ch(inp["rwkv_ln_w"][0]), ch(inp["rwkv_ln_b"][0]),
    ]
    v = np.concatenate(cols, axis=1)
    assert v.shape == (128, NV), v.shape
    return np.ascontiguousarray(v, np.float32)


def make_in_maps(inp, S, cores):
    f = lambda a: np.ascontiguousarray(np.asarray(a, np.float32))
    shared = {
        "cst": _consts(),
        "gat": np.ascontiguousarray(np.tile(f(inp["attn_out_norm_g"][0])[None, :], (128, 1))),
        "w_mod": f(inp["w_mod"][0]),
        "w1g": f(inp["ffn1_w_gate"][0]), "w1u": f(inp["ffn1_w_up"][0]), "w1d": f(inp["ffn1_w_down"][0]),
        "win": f(inp["w_in"][0]), "wuq": f(inp["w_uq"][0]), "wukv": f(inp["w_ukv"][0]),
        "w2": f(inp["rwkv_w2"][0]), "a2": f(inp["rwkv_a2"][0]), "g2": f(inp["rwkv_g2"][0]),
        "wout": f(inp["w_out"][0]),
        "w3g": f(inp["ffn2_w_gate"][0]), "w3u": f(inp["ffn2_w_up"][0]), "w3d": f(inp["ffn2_w_down"][0]),
    }
    maps = []
    for b in cores:
        m = dict(shared)
        m["x"] = f(inp["x"][b][:S])
        m["pos"] = np.ascontiguousarray(np.tile(np.asarray(inp["positions"][b][:S], np.int32)[None, :], (64, 1)))
        m["vt"] = _vt(inp, b)
        maps.append(m)
    return maps


def kernel(**inputs):
    S = inputs["x"].shape[1]
    B = inputs["x"].shape[0]
    nc = build_nc(S)
    in_maps = make_in_maps(inputs, S, list(range(B)))
    res = run_bass_kernel_spmd(nc, in_maps, core_ids=list(range(B)))
    return np.stack([np.asarray(r["out"], np.float32) for r in res.results], axis=0)
```

```python
import math
from contextlib import ExitStack

import numpy as np
import concourse.bass as bass
import concourse.mybir as mybir
from concourse.bass_utils import run_bass_kernel_spmd

F32 = mybir.dt.float32
BF16 = mybir.dt.bfloat16
I32 = mybir.dt.int32
AF = mybir.ActivationFunctionType
ALU = mybir.AluOpType
AX = mybir.AxisListType

D = 1024
DFF = 2816
T = 256
NTB = T // 128
NJ = DFF // 128
KAPPA = math.exp(-0.5)
SCALE = 192.0 ** -0.5
PI = math.pi

V_CT, V_BM, V_G1, V_G2, V_G3, V_GF, V_GQ, V_GKV = 0, 8, 80, 88, 96, 104, 112, 115
V_MIXR, V_MIXK, V_MIXV, V_MIXWA, V_MIXG = 117, 121, 125, 129, 130
V_W0, V_A0, V_KK, V_KA, V_RK, V_LNW, V_LNB = 131, 135, 139, 143, 147, 151, 155
NV = 159
C_ID, C_ML, C_MU, C_MUI, C_BO, C_INVF, C_SS, C_RST = 0, 128, 256, 384, 512, 640, 641, 642
NCST = 642 + 512


import os
POOL2DVE = bool(os.environ.get('POOL2DVE'))


class Buf:
    __slots__ = ("name", "w", "wd", "r", "rd")

    def __init__(self, name):
        self.name = name
        self.w = {}
        self.wd = []
        self.r = {}
        self.rd = []


class Op:
    __slots__ = ("idx", "eng", "fn", "deps", "is_dma", "dkey", "dval", "signal", "sigval", "raw")

    def __init__(self, idx, eng, fn, is_dma, dkey, dval):
        self.idx = idx
        self.eng = eng
        self.fn = fn
        self.deps = []
        self.is_dma = is_dma
        self.dkey = dkey
        self.dval = dval
        self.signal = False
        self.sigval = 0


class Prog:
    ENGS = ("pe", "act", "dve", "pool", "sp")

    def __init__(self, nc, es):
        self.nc = nc
        self.es = es
        self.ops = []
        self.eng_ops = {e: [] for e in self.ENGS}
        self.dcount = {}
        self.nbuf = 0

    def buf(self, name=None):
        self.nbuf += 1
        return Buf(name or f"b{self.nbuf}")

    def bufs(self, n, name=None):
        return [self.buf(f"{name}{i}") for i in range(n)]

    def add(self, eng, fn, reads=(), writes=(), dkey=None):
        is_dma = dkey is not None
        if eng == "pool" and not is_dma and POOL2DVE:
            eng = "dve"
        dval = 0
        if is_dma:
            self.dcount[dkey] = self.dcount.get(dkey, 0) + 1
            dval = 16 * self.dcount[dkey]
        op = Op(len(self.ops), eng, fn, is_dma, dkey, dval)
        deps = {}

        def dep(o, raw):
            if o is op:
                return
            if (not o.is_dma) and (not is_dma) and o.eng == eng:
                if eng == "pe":
                    return
            deps[o.idx] = o

        for b in reads:
            for o in b.w.values():
                dep(o, True)
            for o in b.wd:
                dep(o, True)
        for b in writes:
            for o in b.r.values():
                dep(o, False)
            for o in b.rd:
                dep(o, False)
            for o in b.w.values():
                dep(o, False)
            for o in b.wd:
                dep(o, False)
        for b in reads:
            if is_dma:
                b.rd.append(op)
            else:
                b.r[eng] = op
        for b in writes:
            if b.r or b.rd:
                keep_r = None
                b.w = {}
                b.wd = []
                b.r = {}
                b.rd = []
            if is_dma:
                b.wd.append(op)
            else:
                b.w[eng] = op
        op.deps = list(deps.values())
        self.ops.append(op)
        self.eng_ops[eng].append(op)
        return op

    def emit(self):
        nc = self.nc
        es = self.es
        for op in self.ops:
            for d in op.deps:
                if not d.is_dma:
                    d.signal = True
        cnt = {e: 0 for e in self.ENGS}
        for op in self.ops:
            if (not op.is_dma) and op.signal:
                cnt[op.eng] += 1
                op.sigval = cnt[op.eng]
        self.stats = dict(cnt=dict(cnt), nops={e: len(v) for e, v in self.eng_ops.items()}, dmax=max(self.dcount.values()) * 16)
        esem = {e: es.enter_context(nc.semaphore("s_" + e)) for e in self.ENGS}
        dsem = {k: es.enter_context(nc.semaphore("d_" + k)) for k in self.dcount}
        block = es.enter_context(nc.Block())

        def gen(engname):
            def body(e):
                known = {}
                for op in self.eng_ops[engname]:
                    for d in op.deps:
                        if d.is_dma:
                            key, sem, val = "d_" + d.dkey, dsem[d.dkey], d.dval
                        else:
                            key, sem, val = "e_" + d.eng, esem[d.eng], d.sigval
                        if known.get(key, 0) >= val:
                            continue
                        e.wait_ge(sem, val)
                        known[key] = val
                    ins = op.fn(e)
                    if op.is_dma:
                        ins.then_inc(dsem[op.dkey], 16)
                    elif op.signal:
                        ins.then_inc(esem[op.eng], 1)
                if engname == "sp":
                    for k, n in self.dcount.items():
                        e.wait_ge(dsem[k], 16 * n)

            return body

        block.tensor(gen("pe"))
        block.vector(gen("dve"))
        block.scalar(gen("act"))
        block.gpsimd(gen("pool"))
        block.sync(gen("sp"))


def build_nc(S, stages=("ffn1", "mix", "ffn2"), dbg=()):
    NT = S // T
    nc = bass.Bass("TRN2", target_bir_lowering=False)
    es = ExitStack()
    P = Prog(nc, es)

    def din(name, shape, dt=F32):
        return nc.dram_tensor(name, list(shape), dt, kind="ExternalInput").ap()

    x_d = din("x", [S, D])
    pos_d = din("pos", [64, S], I32)
    vt_d = din("vt", [128, NV])
    cst_d = din("cst", [128, NCST])
    gat_d = din("gat", [128, 512])
    wmod_d = din("w_mod", [D, 9 * D])
    wsrc = {
        "w1g": din("w1g", [D, DFF]), "w1u": din("w1u", [D, DFF]), "w1d": din("w1d", [DFF, D]),
        "win": din("win", [D, 2496]), "wuq": din("wuq", [384, 768]), "wukv": din("wukv", [256, 1024]),
        "w2": din("w2", [64, 512]), "a2": din("a2", [64, 512]), "g2": din("g2", [128, 512]),
        "wout": din("wout", [D, D]),
        "w3g": din("w3g", [D, DFF]), "w3u": din("w3u", [D, DFF]), "w3d": din("w3d", [DFF, D]),
    }
    out_d = nc.dram_tensor("out", [S, D], F32, kind="ExternalOutput").ap()
    dbg_d = {}
    for name, shape in dbg:
        dbg_d[name] = nc.dram_tensor("dbg_" + name, list(shape), F32, kind="ExternalOutput").ap()

    TILED = ("w1g", "w1u", "w3g", "w3u")
    wb = {k: nc.dram_tensor(k + "_bf", ([NJ, 128, 8 * 128] if k in TILED else list(v.shape)), BF16).ap()
          for k, v in wsrc.items()}
    wb_buf = {k: P.buf("wb_" + k) for k in wsrc}
    Kc_d = nc.dram_tensor("Kc", [4, 128, S], BF16).ap()
    Vc_d = nc.dram_tensor("Vc", [4, NT, 128, NTB * 129], BF16).ap()
    Kc_buf = [[P.buf(f"Kc{h}_{i}") for i in range(NT)] for h in range(4)]
    Vc_buf = [[P.buf(f"Vc{h}_{i}") for i in range(NT)] for h in range(4)]

    def sb(name, shape, dt=F32):
        return es.enter_context(nc.sbuf_tensor("sb_" + name, list(shape), dt))

    vt = sb("vt", [128, NV]); vt_b = P.buf("vt")
    cst = sb("cst", [128, NCST]); cst_b = P.buf("cst")
    drv = sb("drv", [128, 96]); drv_b = P.buf("drv")
    modT = sb("modT", [128, 72]); mod_b = P.buf("modT")
    cact = sb("cact", [128, 8]); cact_b = P.buf("cact")
    ident_bf = sb("ident_bf", [128, 128], BF16)
    ones_bf = sb("ones_bf", [128, 128], BF16)
    bones_bf = sb("bones_bf", [128, 128], BF16)
    bones_f = sb("bones_f", [128, 128])
    mL4 = sb("mL4", [128, 4, 128], BF16); mU4 = sb("mU4", [128, 4, 128], BF16); mUI4 = sb("mUI4", [128, 4, 128], BF16)
    id4 = sb("id4", [128, 4, 128], BF16)
    gat = sb("gat", [128, 512])
    kconst_b = P.buf("kconst")

    hT = sb("hT", [128, 8, T]); hT_b = P.bufs(8, "hT")
    uT = sb("uT", [128, 8, T], BF16); uT_b = P.bufs(8, "uT")
    sqb = sb("sqb", [128, 8, T], BF16); sqb_b = P.bufs(8, "sqb")
    tmpA = sb("tmpA", [128, T]); tmpA_b = P.buf("tmpA")
    tmpB = [sb(f"tmpB{i}", [128, T]) for i in range(2)]; tmpB_b = P.bufs(2, "tmpB")
    arena = sb("arena", [128, NJ * T], BF16)
    actT = arena[:].rearrange("p (j t) -> p j t", j=NJ); actT_b = P.bufs(NJ, "actT")
    sgt = [sb(f"sgt{i}", [128, T]) for i in range(2)]; sgt_b = P.bufs(2, "sgt")
    xin = [sb(f"xin{i}", [128, D]) for i in range(1)]; xin_b = P.bufs(1, "xin")
    ost = xin; ost_b = xin_b
    wgs = [sb(f"wgs{i}", [128, 8, 128], BF16) for i in range(2)]; wgs_b = P.bufs(2, "wgs")
    wus = [sb(f"wus{i}", [128, 8, 128], BF16) for i in range(2)]; wus_b = P.bufs(2, "wus")
    wds = [sb(f"wds{i}", [128, 4, 512], BF16) for i in range(2)]; wds_b = P.bufs(2, "wds")
    wmods = [xin[0][:].rearrange("p (kc n) -> p kc n", kc=8)]; wmods_b = [xin_b[0]]


    Krc_d = nc.dram_tensor("Krc", [64, S], BF16).ap()
    Krc_buf = [P.buf(f"Krc{i}") for i in range(NT)]
    wuq_sb = sb("wuq_sb", [128, 3, 768], BF16)
    wuq_rs = sb("wuq_rs", [128, 3, 4, 64], BF16)
    wukv_sb = sb("wukv_sb", [128, 2, 1024], BF16)
    wuv_sb = sb("wuv_sb", [128, 2, 4, 128], BF16)
    wkr_sb = sb("wkr_sb", [128, 8, 128], BF16)
    wa_sb = sb("wa_sb", [128, 512], BF16)
    g2_sb = sb("g2_sb", [128, 512], BF16)
    wres_b = P.buf("wres")
    wins = [sb(f"wins{i}", [128, 8, 256], BF16) for i in range(2)]; wins_b = P.bufs(2, "wins")
    w128 = [sb(f"w128_{i}", [128, 8, 128], BF16) for i in range(2)]; w128_b = P.bufs(2, "w128")
    pq = sb("pq", [128, 3, T]); pq_b = P.bufs(3, "pq")
    qn = sb("qn", [128, 3, T], BF16); qn_b = P.bufs(3, "qn")
    pkv = sb("pkv", [128, 2, T]); pkv_b = P.bufs(2, "pkv")
    kvn = sb("kvn", [128, 2, T], BF16); kvn_b = P.bufs(2, "kvn")
    QnT = sb("QnT", [128, 4, T], BF16); QnT_b = P.bufs(4, "QnT")
    QrT = sb("QrT", [128, 4, T], BF16); QrT_b = P.bufs(4, "QrT")
    Kst = sb("Kst", [128, 4, T], BF16); Kst_b = P.buf("Kst")
    Krst = sb("Krst", [128, T], BF16); Krst_b = P.buf("Krst")
    Vst = sb("Vst", [128, 4, NTB, 129], BF16); Vst_b = P.buf("Vst")
    KVG = 4
    Kt = [sb(f"Kt{i}", [128, KVG * T], BF16) for i in range(2)]; Kt_b = P.bufs(2, "Kt")
    Krt = [sb(f"Krt{i}", [128, KVG * T], BF16) for i in range(2)]; Krt_b = P.bufs(2, "Krt")
    Vt = [sb(f"Vt{i}", [128, KVG, NTB * 129], BF16) for i in range(2)]; Vt_b = P.bufs(2, "Vt")
    PT = [sb(f"PT{i}", [128, T], BF16) for i in range(3)]; PT_b = P.bufs(3, "PT")
    posi = sb("posi", [128, T], I32); posi_b = P.buf("posi")
    ang = sb("ang", [128, T]); ang_b = P.buf("ang")
    ang2 = sb("ang2", [128, T]); ang2_b = P.buf("ang2")
    cosT = sb("cosT", [128, T]); sinS = sb("sinS", [128, T]); rope_b = P.buf("rope")
    rtmp = [sb(f"rtmp{i}", [128, T]) for i in range(4)]; rtmp_b = P.bufs(4, "rtmp")
    ya = sb("ya", [128, NTB, 512]); ya_b = P.bufs(NTB, "ya")
    yab = sb("yab", [128, NTB, 512], BF16); yab_b = P.bufs(NTB, "yab")
    yT = sb("yT", [128, 8, T], BF16); yT_b = P.bufs(8, "yT")
    ssq = sb("ssq", [128, 8]); ssq_b = P.buf("ssq")
    rec = sb("rec", [128, 8]); rec_b = P.buf("rec")


    NCH = T // 64
    Hst = sb("Hst", [128, 4, 64]); Hst_b = P.buf("Hst")
    Hb = sb("Hb", [128, 4, 64], BF16); Hb_b = P.buf("Hb")
    carry = sb("carry", [128, 14]); carry_b = P.bufs(14, "carry")
    Bt = [sb(f"Bt{i}", [128, T + 8]) for i in range(2)]; Bt_b = P.bufs(2, "Bt")
    xwxa = sb("xwxa", [128, T]); xwxa_b = P.buf("xwxa")
    xg = sb("xg", [128, T]); xg_b = P.buf("xg")
    thb = sb("thb", [128, T], BF16); xab = sb("xab", [128, T], BF16); sgx = sb("sgx", [128, T], BF16)
    lor_b = P.buf("lor")
    RT = {}
    for _n in ("rS", "kS", "vS", "sg", "Lc", "E1", "E3", "aG", "kk", "rn", "kkn", "bb", "E2", "tk", "kmod", "E4"):
        RT[_n] = (sb("r_" + _n, [128, T]), P.buf("r_" + _n))
    sqk = sb("sqk", [128, T], BF16); sqk_b = P.buf("sqk")
    Ab = sb("Ab", [128, 8, T], BF16); Ab_b = P.bufs(4, "Ab")
    Btb = sb("Btb", [128, 4, T], BF16); Btb_b = P.bufs(4, "Btb")
    Ktb = sb("Ktb", [128, 4, T], BF16); Ktb_b = P.bufs(4, "Ktb")
    Rb = sb("Rb", [128, 8, T], BF16); Rb_b = P.bufs(4, "Rb")
    vb = sqb[:, 0:4, :]; vb_b = sqb_b[0:4]
    Bhb = sqb[:, 4:8, :]; Bhb_b = sqb_b[4:8]
    Khb = sb("Khb", [128, 4, T], BF16); Khb_b = P.bufs(4, "Khb")
    Vtm = sb("Vtm", [128, NTB, 512], BF16); Vtm_b = P.bufs(NTB, "Vtm")
    Bhtm = sb("Bhtm", [128, NTB, 512], BF16); Bhtm_b = P.bufs(NTB, "Bhtm")
    Khtmc = [sb(f"Khtm{i}", [128, NTB, 512], BF16) for i in range(2)]; Khtm_b = P.bufs(NTB, "Khtm")
    bon = arena[:, 0:8 * T].bitcast(F32).rearrange("p (c t) -> p c t", c=4)
    gG = arena[:, 8 * T:16 * T].bitcast(F32).rearrange("p (c t) -> p c t", c=4)
    bon_b = [(actT_b[2 * c], actT_b[2 * c + 1]) for c in range(4)]
    gG_b = [(actT_b[8 + 2 * c], actT_b[8 + 2 * c + 1]) for c in range(4)]
    yR = sb("yR", [128, 4, T]); yR_b = P.buf("yR")
    gL = sb("gL", [128, 4, NCH]); gL_b = P.buf("gL")
    LL = []
    for _i in range(1):
        LL.append({n: (sb(f"{n}{_i}", [128, 8, 128], BF16), P.bufs(2, f"{n}{_i}")) for n in ("TtA", "AkT", "ArbT", "ArkT")})
    MM = [(sb(f"Mm{i}", [128, 4, 128], BF16), P.buf(f"Mm{i}")) for i in range(4)]
    Xb = sb("Xb", [128, 512], BF16); Xb_b = P.buf("Xb")
    Ubc = [sb(f"Ub{i}", [128, 512], BF16) for i in range(2)]; Ub_b = P.buf("Ub")

    pbank = [es.enter_context(nc.psum_tensor(f"pb{i}", [128, 512], F32)) for i in range(8)]
    pbank_b = P.bufs(8, "pb")
    rrA = [0]

    def bankA():
        k = rrA[0] % 4
        rrA[0] += 1
        return pbank[k], pbank_b[k]

    def bankB(i):
        return pbank[4 + i], pbank_b[4 + i]

    def dma(q, out, in_, reads, writes, key):
        return P.add(q, lambda e: e.dma_start(out=out, in_=in_), reads, writes, dkey=key)

    def mm(out, lhsT, rhs, start, stop, reads, writes):
        return P.add("pe", lambda e: e.matmul(out, lhsT=lhsT, rhs=rhs, start=start, stop=stop), reads, writes)

    def tr(out, in_, ident, reads, writes):
        return P.add("pe", lambda e: e.transpose(out, in_, ident), reads, writes)

    def act(out, in_, func, reads, writes, bias=0.0, scale=1.0, accum_out=None):
        if accum_out is None:
            return P.add("act", lambda e: e.activation(out=out, in_=in_, func=func, bias=bias, scale=scale), reads, writes)
        return P.add("act", lambda e: e.activation(out=out, in_=in_, func=func, bias=bias, scale=scale, accum_out=accum_out), reads, writes)

    def tt(eng, out, in0, in1, op, reads, writes):
        return P.add(eng, lambda e: e.tensor_tensor(out=out, in0=in0, in1=in1, op=op), reads, writes)

    def ts(eng, out, in0, s1, s2, op0, op1, reads, writes):
        if op1 is None:
            return P.add(eng, lambda e: e.tensor_scalar(out=out, in0=in0, scalar1=s1, scalar2=None, op0=op0), reads, writes)
        return P.add(eng, lambda e: e.tensor_scalar(out=out, in0=in0, scalar1=s1, scalar2=s2, op0=op0, op1=op1), reads, writes)

    def stt(eng, out, in0, scalar, in1, op0, op1, reads, writes):
        return P.add(eng, lambda e: e.scalar_tensor_tensor(out=out, in0=in0, scalar=scalar, in1=in1, op0=op0, op1=op1), reads, writes)

    def cp(eng, out, in_, reads, writes):
        if eng == "act":
            return P.add(eng, lambda e: e.activation(out=out, in_=in_, func=AF.Identity), reads, writes)
        return P.add(eng, lambda e: e.tensor_copy(out=out, in_=in_), reads, writes)

    def memset(eng, ap, val, writes):
        return P.add(eng, lambda e: e.memset(ap, val), (), writes)

    for k, src in wsrc.items():
        if k in TILED:
            for j in range(NJ):
                dma("pool", wb[k][j].rearrange("p (kc n) -> p kc n", kc=8),
                    src[:, j * 128:(j + 1) * 128].rearrange("(kc p) n -> p kc n", p=128), (), (wb_buf[k],), "wc_" + k)
            continue
        rows = src.shape[0]
        step = 256 if rows >= 256 else rows
        for r0 in range(0, rows, step):
            r1 = min(rows, r0 + step)
            dma("pool", wb[k][r0:r1, :], src[r0:r1, :], (), (wb_buf[k],), "wc_" + k)

    dma("sp", vt[:], vt_d, (), (vt_b,), "c_vt")
    dma("sp", cst[:], cst_d, (), (cst_b,), "c_cst")
    dma("sp", gat[:], gat_d, (), (kconst_b,), "c_gat")

    cp("dve", ident_bf[:], cst[:, C_ID:C_ID + 128], (cst_b,), (kconst_b,))
    memset("dve", ones_bf[:], 1.0, (kconst_b,))
    cp("dve", bones_bf[:], cst[:, C_BO:C_BO + 128], (cst_b,), (kconst_b,))
    ts("dve", bones_f[:], cst[:, C_BO:C_BO + 128], 1.0 / 64.0, None, ALU.mult, None, (cst_b,), (kconst_b,))
    for i in range(4):
        cp("dve", mL4[:, i, :], cst[:, C_ML:C_ML + 128], (cst_b,), (kconst_b,))
        cp("dve", mU4[:, i, :], cst[:, C_MU:C_MU + 128], (cst_b,), (kconst_b,))
        cp("dve", mUI4[:, i, :], cst[:, C_MUI:C_MUI + 128], (cst_b,), (kconst_b,))
        cp("dve", id4[:, i, :], cst[:, C_ID:C_ID + 128], (cst_b,), (kconst_b,))
    ident_f = cst[:, C_ID:C_ID + 128]

    act(cact[:], vt[:, V_CT:V_CT + 8], AF.Silu, (vt_b,), (cact_b,))
    mps, mps_b = bankA()
    for j in range(72):
        s = 0
        dma("sp", wmods[s], wmod_d[:, j * 128:(j + 1) * 128].rearrange("(kc p) n -> p kc n", p=128),
            (), (wmods_b[s],), "xin0")
        for kc in range(8):
            mm(mps[:, j:j + 1], wmods[s][:, kc, :], cact[:, kc:kc + 1],
               kc == 0, kc == 7, (wmods_b[s], cact_b), (mps_b,))
    tt("dve", modT[:], mps[:, 0:72], vt[:, V_BM:V_BM + 72], ALU.add, (mps_b, vt_b), (mod_b,))

    DR_GM1, DR_GM2, DR_GM3, DR_CG1, DR_CG3, DR_GF, DR_GQ, DR_GKV = 0, 8, 16, 24, 32, 40, 48, 51
    DR_OMR, DR_OMK, DR_OMV, DR_OMWA, DR_OMG, DR_OMA, DR_LNW8 = 53, 57, 61, 65, 66, 67, 71
    SH1, SC1, GT1, SH2, SC2, GT2, SH3, SC3, GT3 = [8 * i for i in range(9)]

    def dts(out_c, n, in_ap, s1, s2, op0, op1):
        ts("dve", drv[:, out_c:out_c + n], in_ap, s1, s2, op0, op1, (mod_b, vt_b, drv_b), (drv_b,))

    for (dst, sc, g) in ((DR_GM1, SC1, V_G1), (DR_GM2, SC2, V_G2), (DR_GM3, SC3, V_G3)):
        dts(dst, 8, modT[:, sc:sc + 8], 1.0, 32.0, ALU.add, ALU.mult)
        tt("dve", drv[:, dst:dst + 8], drv[:, dst:dst + 8], vt[:, g:g + 8], ALU.mult, (drv_b, vt_b), (drv_b,))
    dts(DR_CG1, 8, modT[:, GT1:GT1 + 8], 0.5, None, ALU.mult, None)
    dts(DR_CG3, 8, modT[:, GT3:GT3 + 8], 0.5, None, ALU.mult, None)
    dts(DR_GF, 8, vt[:, V_GF:V_GF + 8], 32.0, None, ALU.mult, None)
    dts(DR_GQ, 3, vt[:, V_GQ:V_GQ + 3], math.sqrt(384.0), None, ALU.mult, None)
    dts(DR_GKV, 2, vt[:, V_GKV:V_GKV + 2], 16.0, None, ALU.mult, None)
    dts(DR_OMR, 14, vt[:, V_MIXR:V_MIXR + 14], -1.0, 1.0, ALU.mult, ALU.add)
    dts(DR_OMA, 4, vt[:, V_KA:V_KA + 4], -1.0, 1.0, ALU.mult, ALU.add)
    dts(DR_LNW8, 4, vt[:, V_LNW:V_LNW + 4], 8.0, None, ALU.mult, None)
    CONST = (drv_b, mod_b, vt_b, kconst_b, cst_b)
    epst = sb("epst", [128, 16])
    eps_cols = {}

    def eps_ap(val, np_=128):
        if val not in eps_cols:
            k = len(eps_cols)
            eps_cols[val] = k
            memset("dve", epst[:, k:k + 1], float(val), (kconst_b,))
        k = eps_cols[val]
        return epst[0:np_, k:k + 1]

    for _v in (D * 1e-6, 384 * 1e-6, 256 * 1e-6, 1e-30, 64 * 64e-5, -PI, 0.0):
        eps_ap(_v)


    def ldw(out, in_, k):
        dma("sp", out, in_, (wb_buf[k],), (wres_b,), "wres")
    ldw(wuq_sb[:], wb["wuq"].rearrange("(kc p) n -> p kc n", p=128), "wuq")
    wuq4 = wb["wuq"].rearrange("(kc p) (h d) -> p kc h d", p=128, d=192)
    for kc in range(3):
        ldw(wuq_rs[:, kc, :, 0:32], wuq4[:, kc, :, 160:192], "wuq")
        ldw(wuq_rs[:, kc, :, 32:64], wuq4[:, kc, :, 128:160], "wuq")
    ldw(wukv_sb[:], wb["wukv"].rearrange("(kc p) n -> p kc n", p=128), "wukv")
    wukv5 = wb["wukv"].rearrange("(kc p) (h two d) -> p kc h two d", p=128, two=2, d=128)
    for kc in range(2):
        ldw(wuv_sb[:, kc, :, :], wukv5[:, kc, :, 1, :], "wukv")
    win3 = wb["win"].rearrange("(kc p) n -> p kc n", p=128)
    ldw(wkr_sb[:, :, 0:64], win3[:, :, 640:704], "win")
    ldw(wkr_sb[:, :, 64:96], win3[:, :, 672:704], "win")
    ldw(wkr_sb[:, :, 96:128], win3[:, :, 640:672], "win")
    ldw(wa_sb[0:64, :], wb["w2"], "w2")
    ldw(wa_sb[64:128, :], wb["a2"], "a2")
    ldw(g2_sb[:], wb["g2"], "g2")
    memset("pool", Vst[:, :, :, 128:129], 1.0, (Vst_b,))

    dbg_n = [0]

    def dump(name, ap, bufs, n):
        if name not in dbg_d:
            return
        k = dbg_n[0]; dbg_n[0] += 1
        tmp = sb(f"dbgtmp{k}", [128, n])
        tb_ = P.buf()
        cp("dve", tmp[:], ap, tuple(bufs), (tb_,))
        dma("pool", dbg_d[name], tmp[:], (tb_,), (), f"dbg{k}")

    memset("pool", Ab[:], 0.0, tuple(Ab_b))
    memset("pool", Rb[:], 0.0, tuple(Rb_b))
    for _i in range(2):
        memset("pool", Khtmc[_i][:], 0.0, tuple(Khtm_b))
        memset("pool", Ubc[_i][:], 0.0, (Ub_b,))
    memset("pool", Hst[:], 0.0, (Hst_b,))
    memset("pool", Hb[:], 0.0, (Hb_b,))
    memset("pool", carry[:], 0.0, tuple(carry_b))

    tmpS = sb("tmpS", [128, T]); tmpS_b = P.buf("tmpS")

    def rsqrt_ps(ps, ps_b, addc, np_=128, n=T, dst=None, dst_b=None):
        dst = tmpA if dst is None else dst
        dst_b = tmpA_b if dst_b is None else dst_b
        act(tmpS[0:np_, 0:n], ps[0:np_, 0:n], AF.Sqrt, (ps_b,), (tmpS_b,), bias=eps_ap(addc, np_))
        P.add("dve", lambda e: e.reciprocal(out=dst[0:np_, 0:n], in_=tmpS[0:np_, 0:n]), (tmpS_b,), (dst_b,))

    def norm_mod(gm_ap_fn, sh_ap_fn, dst, dst_b, Dn=D, src=None, src_b=None, nchunk=8, eps=1e-6):
        src = hT if src is None else src
        src_b = hT_b if src_b is None else src_b
        for c in range(nchunk):
            act(sqb[:, c, :], src[:, c, :], AF.Square, (src_b[c],), (sqb_b[c],))
        ps, ps_b = bankA()
        for c in range(nchunk):
            mm(ps[:, 0:T], ones_bf[:], sqb[:, c, :], c == 0, c == nchunk - 1, (sqb_b[c], kconst_b), (ps_b,))
        rsqrt_ps(ps, ps_b, Dn * eps)
        for c in range(nchunk):
            if sh_ap_fn is None:
                stt("dve", dst[:, c, :], src[:, c, :], gm_ap_fn(c), tmpA[:], ALU.mult, ALU.mult,
                    (src_b[c], tmpA_b) + CONST, (dst_b[c],))
            else:
                k = c % 2
                tt("dve", tmpB[k][:], src[:, c, :], tmpA[:], ALU.mult, (src_b[c], tmpA_b), (tmpB_b[k],))
                act(dst[:, c, :], tmpB[k][:], AF.Identity, (tmpB_b[k],) + CONST, (dst_b[c],),
                    bias=sh_ap_fn(c), scale=gm_ap_fn(c))

    wslot = {"g": 0, "u": 0, "d": 0}

    def ffn(kg, ku, kd, cg_col):
        for j in range(NJ):
            sg_ = wslot["g"] % 2; wslot["g"] += 1
            su_ = wslot["u"] % 2; wslot["u"] += 1
            dma("sp", wgs[sg_][:], wb[kg][j].rearrange("p (kc n) -> p kc n", kc=8),
                (wb_buf[kg],), (wgs_b[sg_],), f"wgs{sg_}")
            dma("sp", wus[su_][:], wb[ku][j].rearrange("p (kc n) -> p kc n", kc=8),
                (wb_buf[ku],), (wus_b[su_],), f"wus{su_}")
            gps, gps_b = bankA()
            ups, ups_b = bankA()
            for kc in range(8):
                mm(gps[:, 0:T], wgs[sg_][:, kc, :], uT[:, kc, :], kc == 0, kc == 7,
                   (wgs_b[sg_], uT_b[kc]), (gps_b,))
            for kc in range(8):
                mm(ups[:, 0:T], wus[su_][:, kc, :], uT[:, kc, :], kc == 0, kc == 7,
                   (wus_b[su_], uT_b[kc]), (ups_b,))
            k = j % 2
            act(sgt[k][:], gps[:, 0:T], AF.Silu, (gps_b,), (sgt_b[k],))
            tt("dve", actT[:, j, :], sgt[k][:], ups[:, 0:T], ALU.mult, (sgt_b[k], ups_b), (actT_b[j],))
        for half in range(2):
            for jg in range(6):
                nj = 4 if jg < 5 else 2
                sd_ = wslot["d"] % 2; wslot["d"] += 1
                dma("sp", wds[sd_][:, 0:nj, :],
                    wb[kd][jg * 512:jg * 512 + nj * 128, half * 512:(half + 1) * 512].rearrange("(j p) n -> p j n", p=128),
                    (wb_buf[kd],), (wds_b[sd_],), f"wds{sd_}")
                for jj in range(nj):
                    j = 4 * jg + jj
                    for c4 in range(4):
                        bk, bk_b = bankB(c4)
                        mm(bk[:, 0:T], wds[sd_][:, jj, c4 * 128:(c4 + 1) * 128], actT[:, j, :], j == 0, j == NJ - 1,
                           (wds_b[sd_], actT_b[j]), (bk_b,))
            for c4 in range(4):
                c = half * 4 + c4
                bk, bk_b = bankB(c4)
                stt("dve", hT[:, c, :], bk[:, 0:T], drv[:, cg_col + c:cg_col + c + 1], hT[:, c, :], ALU.mult, ALU.add,
                    (bk_b, hT_b[c]) + CONST, (hT_b[c],))


    RWKV_STEPS = 190
    winslot = [0]
    w128slot = [0]
    kvslot = [0]
    ptslot = [0]
    rtslot = [0]

    def load_win(key, c0, ncols):
        k = winslot[0] % 2; winslot[0] += 1
        dma("sp", wins[k][:, :, 0:ncols], wb[key][:, c0:c0 + ncols].rearrange("(kc p) n -> p kc n", p=128),
            (wb_buf[key],), (wins_b[k],), f"wins{k}")
        return wins[k], wins_b[k]

    def load_w128(key, c0):
        k = w128slot[0] % 2; w128slot[0] += 1
        dma("sp", w128[k][:], wb[key][:, c0:c0 + 128].rearrange("(kc p) n -> p kc n", p=128),
            (wb_buf[key],), (w128_b[k],), f"w128_{k}")
        return w128[k], w128_b[k]

    def proj(lhs_fn, w_b, M=128):
        ps, ps_b = bankA()
        for kc in range(8):
            mm(ps[0:M, 0:T], lhs_fn(kc), uT[:, kc, :], kc == 0, kc == 7, (w_b, uT_b[kc]), (ps_b,))
        return ps, ps_b

    def rt():
        k = rtslot[0] % 4; rtslot[0] += 1
        return rtmp[k], rtmp_b[k]

    def rope_combine(psr, psr_b, psrs, psrs_b, dst_ap, dst_bufs):
        t1, t1_b = rt()
        t2, t2_b = rt()
        tt("dve", t1[0:64, :], psr[0:64, 0:T], cosT[0:64, :], ALU.mult, (psr_b, rope_b), (t1_b,))
        tt("dve", t2[0:64, :], psrs[0:64, 0:T], sinS[0:64, :], ALU.mult, (psrs_b, rope_b), (t2_b,))
        tt("pool", dst_ap, t1[0:64, :], t2[0:64, :], ALU.add, (t1_b, t2_b), dst_bufs)

    def mla(it, t0, step=lambda: None):
        dma("sp", posi[0:64, :], pos_d[:, t0:t0 + T], (), (posi_b,), "posi")
        cp("dve", ang[0:64, :], posi[0:64, :], (posi_b,), (ang_b,))
        ts("dve", ang[0:64, :], ang[0:64, :], cst[0:64, C_INVF:C_INVF + 1], None, ALU.mult, None, (ang_b, cst_b), (ang_b,))
        C1 = 6.28125
        C2 = 2 * PI - C1

        def sin_of(dst, shift):
            ts("dve", ang2[0:64, :], ang[0:64, :], shift, 1.0 / (2 * PI), ALU.add, ALU.mult, (ang_b, rope_b), (ang2_b,))
            cp("dve", posi[0:64, :], ang2[0:64, :], (ang2_b,), (posi_b,))
            cp("dve", ang2[0:64, :], posi[0:64, :], (posi_b,), (ang2_b,))
            t1, t1_b = rt()
            ts("dve", t1[0:64, :], ang[0:64, :], shift, None, ALU.add, None, (ang_b,), (t1_b,))
            stt("dve", t1[0:64, :], ang2[0:64, :], -C1, t1[0:64, :], ALU.mult, ALU.add, (ang2_b, t1_b), (t1_b,))
            stt("dve", t1[0:64, :], ang2[0:64, :], -C2, t1[0:64, :], ALU.mult, ALU.add, (ang2_b, t1_b), (t1_b,))
            ts("dve", ang2[0:64, :], t1[0:64, :], PI, None, ALU.is_gt, None, (t1_b,), (ang2_b,))
            stt("dve", t1[0:64, :], ang2[0:64, :], -2 * PI, t1[0:64, :], ALU.mult, ALU.add, (ang2_b, t1_b), (t1_b,))
            ts("dve", ang2[0:64, :], t1[0:64, :], -PI, None, ALU.is_lt, None, (t1_b,), (ang2_b,))
            stt("dve", t1[0:64, :], ang2[0:64, :], 2 * PI, t1[0:64, :], ALU.mult, ALU.add, (ang2_b, t1_b), (t1_b,))
            ts("dve", t1[0:64, :], t1[0:64, :], PI, -PI, ALU.min, ALU.max, (t1_b,), (t1_b,))
            act(dst, t1[0:64, :], AF.Sin, (t1_b,), (rope_b,))

        sin_of(sinS[0:64, :], 0.0)
        ts("dve", sinS[0:64, :], sinS[0:64, :], cst[0:64, C_SS:C_SS + 1], None, ALU.mult, None, (rope_b, cst_b), (rope_b,))
        sin_of(cosT[0:64, :], 0.5 * PI)

        w, w_b = load_win("win", 0, 256)
        for jj in range(2):
            ps, ps_b = proj(lambda kc, jj=jj, w=w: w[:, kc, jj * 128:(jj + 1) * 128], w_b)
            cp("act", pq[:, jj, :], ps[:, 0:T], (ps_b,), (pq_b[jj],))
        w, w_b = load_win("win", 256, 256)
        ps, ps_b = proj(lambda kc, w=w: w[:, kc, 0:128], w_b)
        cp("act", pq[:, 2, :], ps[:, 0:T], (ps_b,), (pq_b[2],))
        ps, ps_b = proj(lambda kc, w=w: w[:, kc, 128:256], w_b)
        cp("act", pkv[:, 0, :], ps[:, 0:T], (ps_b,), (pkv_b[0],))
        w, w_b = load_win("win", 512, 128)
        ps, ps_b = proj(lambda kc, w=w: w[:, kc, 0:128], w_b)
        cp("act", pkv[:, 1, :], ps[:, 0:T], (ps_b,), (pkv_b[1],))
        psr, psr_b = proj(lambda kc: wkr_sb[:, kc, 0:64], wres_b, M=64)
        psrs, psrs_b = proj(lambda kc: wkr_sb[:, kc, 64:128], wres_b, M=64)
        rope_combine(psr, psr_b, psrs, psrs_b, Krst[0:64, :], (Krst_b,))
        dma("pool", Krc_d[:, t0:t0 + T], Krst[0:64, :], (Krst_b,), (Krc_buf[it],), "krst")

        norm_mod(lambda c: drv[:, DR_GQ + c:DR_GQ + c + 1], None, qn, qn_b, Dn=384, src=pq, src_b=pq_b, nchunk=3)
        norm_mod(lambda c: drv[:, DR_GKV + c:DR_GKV + c + 1], None, kvn, kvn_b, Dn=256, src=pkv, src_b=pkv_b, nchunk=2)

        for h in range(4):
            ps, ps_b = bankA()
            for kc in range(3):
                mm(ps[:, 0:T], wuq_sb[:, kc, 192 * h:192 * h + 128], qn[:, kc, :], kc == 0, kc == 2,
                   (wres_b, qn_b[kc]), (ps_b,))
            cp("act", QnT[:, h, :], ps[:, 0:T], (ps_b,), (QnT_b[h],))
            psr, psr_b = bankA()
            for kc in range(3):
                mm(psr[0:64, 0:T], wuq_sb[:, kc, 192 * h + 128:192 * h + 192], qn[:, kc, :], kc == 0, kc == 2,
                   (wres_b, qn_b[kc]), (psr_b,))
            psrs, psrs_b = bankA()
            for kc in range(3):
                mm(psrs[0:64, 0:T], wuq_rs[:, kc, h, :], qn[:, kc, :], kc == 0, kc == 2,
                   (wres_b, qn_b[kc]), (psrs_b,))
            rope_combine(psr, psr_b, psrs, psrs_b, QrT[0:64, h, :], (QrT_b[h],))
        for h in range(4):
            ps, ps_b = bankA()
            for kc in range(2):
                mm(ps[:, 0:T], wukv_sb[:, kc, 256 * h:256 * h + 128], kvn[:, kc, :], kc == 0, kc == 1,
                   (wres_b, kvn_b[kc]), (ps_b,))
            cp("act", Kst[:, h, :], ps[:, 0:T], (ps_b,), (Kst_b,))
        dma("pool", Kc_d[:, :, t0:t0 + T].rearrange("h p t -> p h t"), Kst[:], (Kst_b,),
            tuple(Kc_buf[h][it] for h in range(4)), "kst")
        for tb in range(NTB):
            ps, ps_b = bankA()
            for kc in range(2):
                mm(ps[:, 0:512], kvn[:, kc, tb * 128:(tb + 1) * 128], wuv_sb[:, kc, :, :].rearrange("p h d -> p (h d)"),
                   kc == 0, kc == 1, (wres_b, kvn_b[kc]), (ps_b,))
            cp("dve", Vst[:, :, tb, 0:128], ps[:, 0:512].rearrange("p (h d) -> p h d", h=4), (ps_b,), (Vst_b,))
        for h in range(4):
            dma("pool", Vc_d[h, it], Vst[:, h, :, :].rearrange("p t d -> p (t d)"), (Vst_b,), (Vc_buf[h][it],), f"vst{h}")

        for h in range(4):
            opsl = [bankB(qs) for qs in range(NTB)]
            for jg in range(0, it + 1, KVG):
                njt = min(KVG, it + 1 - jg)
                k = kvslot[0] % 2; kvslot[0] += 1
                dma("sp", Kt[k][:, 0:njt * T], Kc_d[h][:, jg * T:(jg + njt) * T],
                    tuple(Kc_buf[h][jg:jg + njt]), (Kt_b[k],), f"kt{k}")
                dma("sp", Krt[k][0:64, 0:njt * T], Krc_d[:, jg * T:(jg + njt) * T],
                    tuple(Krc_buf[jg:jg + njt]), (Krt_b[k],), f"krt{k}")
                dma("sp", Vt[k][:, 0:njt, :], Vc_d[h, jg:jg + njt].rearrange("t p d -> p t d"),
                    tuple(Vc_buf[h][jg:jg + njt]), (Vt_b[k],), f"vt{k}")
                for jj in range(njt):
                    jt = jg + jj
                    for kb in range(NTB):
                        diag = jt == it
                        q0 = kb * 128 if diag else 0
                        ko = jj * T + kb * 128
                        sps, sps_b = bankA()
                        mm(sps[:, q0:T], Kt[k][:, ko:ko + 128], QnT[:, h, q0:T], True, False,
                           (Kt_b[k], QnT_b[h]), (sps_b,))
                        mm(sps[:, q0:T], Krt[k][0:64, ko:ko + 128], QrT[0:64, h, q0:T], False, True,
                           (Krt_b[k], QrT_b[h]), (sps_b,))
                        p = ptslot[0] % 3; ptslot[0] += 1
                        act(PT[p][:, q0:T], sps[:, q0:T], AF.Exp, (sps_b,), (PT_b[p],), scale=SCALE)
                        if diag:
                            memset("pool", PT[p][64:128, q0:q0 + 64], 0.0, (PT_b[p],))
                        for qs in range(q0 // 128, NTB):
                            first = (jt == 0 and kb == 0)
                            last = (diag and kb == qs)
                            ops, ops_b = opsl[qs]
                            mm(ops[:, 0:129], PT[p][:, qs * 128:(qs + 1) * 128],
                               Vt[k][:, jj, kb * 129:(kb + 1) * 129], first, last, (PT_b[p], Vt_b[k]), (ops_b,))
                        step()
            for qs in range(NTB):
                ops, ops_b = opsl[qs]
                P.add("dve", lambda e, qs=qs, h=h, ops=ops: e.reciprocal(out=rec[:, qs:qs + 1], in_=ops[:, 128:129]),
                      (ops_b,), (rec_b,))
                ts("dve", ya[:, qs, h * 128:(h + 1) * 128], ops[:, 0:128], rec[:, qs:qs + 1], None,
                   ALU.mult, None, (ops_b, rec_b), (ya_b[qs],))

        for qs in range(NTB):
            t1, t1_b = rt()
            t2, t2_b = rt()
            act(t1[:], ya[:, qs, 0:T], AF.Square, (ya_b[qs],), (t1_b,))
            act(t2[:], ya[:, qs, T:2 * T], AF.Square, (ya_b[qs],), (t2_b,))
            tt("dve", t1[:], t1[:], t2[:], ALU.add, (t1_b, t2_b), (t1_b,))
            P.add("dve", lambda e, qs=qs, t1=t1: e.reduce_sum(out=ssq[:, qs:qs + 1], in_=t1[:], axis=AX.X), (t1_b,), (ssq_b,))
        if it == NT - 1:
            dump("cos", cosT[:], (rope_b,), T)
            dump("sin", sinS[:], (rope_b,), T)
            dump("qn0", QnT[:, 0, :], (QnT_b[0],), T)
            dump("qr0", QrT[:, 0, :], (QrT_b[0],), T)
            dump("ya0", ya[:, 0, :], (ya_b[0],), 512)
            dump("kst0", Kst[:, 0, :], (Kst_b,), T)
            dump("krst", Krst[:], (Krst_b,), T)
        ts("dve", ssq[:, 0:NTB], ssq[:, 0:NTB], 1.0 / 512.0, 1e-6, ALU.mult, ALU.add, (ssq_b,), (ssq_b,))
        act(ssq[:, 0:NTB], ssq[:, 0:NTB], AF.Sqrt, (ssq_b,), (ssq_b,))
        P.add("dve", lambda e: e.reciprocal(out=ssq[:, 0:NTB], in_=ssq[:, 0:NTB]), (ssq_b,), (ssq_b,))
        for qs in range(NTB):
            stt("dve", yab[:, qs, :], ya[:, qs, :], ssq[:, qs:qs + 1], gat[:], ALU.mult, ALU.mult,
                (ya_b[qs], ssq_b, kconst_b), (yab_b[qs],))
        for cch in range(4):
            ps, ps_b = bankA()
            psbf = ps[:].bitcast(BF16)
            for qs in range(NTB):
                tr(psbf[:, qs * 128:(qs + 1) * 128], yab[:, qs, cch * 128:(cch + 1) * 128], ident_bf[:],
                   (yab_b[qs], kconst_b), (ps_b,))
            cp("act", yT[:, cch, :], psbf[:, 0:T], (ps_b,), (yT_b[cch],))


    btslot = [0]

    SSUB = int(os.environ.get("SSUB", "9"))
    RSUB = int(os.environ.get("RSUB", "9"))

    def shift_evac(ps, ps_b, ci, mixcol, omcol, dst, dst_b):
        k = btslot[0] % 2; btslot[0] += 1
        ts("dve", Bt[k][:, 8:T + 8], ps[:, 0:T], vt[:, mixcol:mixcol + 1], None, ALU.mult, None, (ps_b, vt_b), (Bt_b[k],))
        if SSUB <= 1:
            return
        cp("pool", Bt[k][:, 7:8], carry[:, ci:ci + 1], (carry_b[ci],), (Bt_b[k],))
        if SSUB <= 2:
            return
        ts("dve", dst, ps[:, 0:T], drv[:, omcol:omcol + 1], None, ALU.mult, None, (ps_b,) + CONST, (dst_b,))
        if SSUB <= 3:
            return
        tt("pool", dst, dst, Bt[k][:, 7:T + 7], ALU.add, (dst_b, Bt_b[k]), (dst_b,))
        if SSUB <= 4:
            return
        cp("pool", carry[:, ci:ci + 1], Bt[k][:, T + 7:T + 8], (Bt_b[k],), (carry_b[ci],))

    def R_(n):
        return RT[n][0][:], RT[n][1]


    RCUT = int(os.environ.get("RCUT", "9"))

    def rwkv_fill():
        for c in range(4, 8):
            memset("pool", yT[:, c, :], 0.0, (yT_b[c],))

    def rwkv(it, t0):
        w, w_b = load_win("win", 2240, 256)
        ps, ps_b = proj(lambda kc, w=w: w[:, kc, 0:128], w_b)
        if RSUB <= 1:
            cp("act", xwxa[:], ps[:, 0:T], (ps_b,), (xwxa_b,))
            rwkv_fill(); return
        shift_evac(ps, ps_b, 12, V_MIXWA, DR_OMWA, xwxa[:], xwxa_b)
        yield
        if RSUB <= 2:
            rwkv_fill(); return
        ps, ps_b = proj(lambda kc, w=w: w[:, kc, 128:256], w_b)
        shift_evac(ps, ps_b, 13, V_MIXG, DR_OMG, xg[:], xg_b)
        yield
        if RSUB <= 3:
            rwkv_fill(); return
        act(thb[0:64, :], xwxa[0:64, :], AF.Tanh, (xwxa_b,), (lor_b,))
        if RSUB <= 4:
            rwkv_fill(); return
        cp("dve", xab[64:128, :], xwxa[64:128, :], (xwxa_b,), (lor_b,))
        if RSUB <= 5:
            rwkv_fill(); return
        act(sgx[:], xg[:], AF.Sigmoid, (xg_b,), (lor_b,))
        yield
        if RCUT <= 1:
            rwkv_fill(); return
        rS, rS_b = R_("rS"); kS, kS_b = R_("kS"); vS, vS_b = R_("vS"); sg, sg_b = R_("sg"); Lc, Lc_b = R_("Lc")
        E1, E1_b = R_("E1"); E3, E3_b = R_("E3"); aG, aG_b = R_("aG"); kk, kk_b = R_("kk"); rn, rn_b = R_("rn")
        kkn, kkn_b = R_("kkn"); bb, bb_b = R_("bb"); E2, E2_b = R_("E2"); tk, tk_b = R_("tk"); kmod, kmod_b = R_("kmod")
        E4, E4_b = R_("E4"); rk, rk_b = R_("bb"); yc, yc_b = R_("kkn"); yn, yn_b = R_("kk")
        for cc in range(4):
            sl = slice(cc * 128, (cc + 1) * 128)
            for (dst, dst_b, c0, ci, mixc, omc) in ((rS, rS_b, 704, cc, V_MIXR + cc, DR_OMR + cc),
                                                   (kS, kS_b, 1216, 4 + cc, V_MIXK + cc, DR_OMK + cc),
                                                   (vS, vS_b, 1728, 8 + cc, V_MIXV + cc, DR_OMV + cc)):
                w, w_b = load_w128("win", c0 + 128 * cc)
                ps, ps_b = proj(lambda kc, w=w: w[:, kc, :], w_b)
                shift_evac(ps, ps_b, ci, mixc, omc, dst, dst_b)
                yield
            ps, ps_b = bankA()
            mm(ps[:, 0:T], wa_sb[0:64, sl], thb[0:64, :], True, True, (wres_b, lor_b), (ps_b,))
            act(sg, ps[:, 0:T], AF.Sigmoid, (ps_b,) + CONST, (sg_b,), bias=vt[:, V_W0 + cc:V_W0 + cc + 1])
            if os.environ.get("NOSCAN"):
                cp("dve", Lc, sg, (sg_b,), (Lc_b,))
            else:
                P.add("dve", lambda e: e.tensor_tensor_scan(out=Lc, data0=cst[:, C_RST:C_RST + T], data1=sg, initial=0.0,
                                                            op0=ALU.mult, op1=ALU.add), (sg_b, cst_b), (Lc_b,))
            act(E1, Lc, AF.Exp, (Lc_b,), (E1_b,), scale=-KAPPA)
            yield
            cp("pool", gL[:, cc, :], E1.rearrange("p (c t) -> p c t", t=64)[:, :, 63], (E1_b,), (gL_b,))
            tt("dve", Rb[0:64, 2 * cc, :], rS[0:64, :], E1[0:64, :], ALU.mult, (rS_b, E1_b), (Rb_b[cc],))
            tt("dve", Rb[64:128, 2 * cc + 1, :], rS[64:128, :], E1[64:128, :], ALU.mult, (rS_b, E1_b), (Rb_b[cc],))
            tt("pool", E3, Lc, sg, ALU.subtract, (Lc_b, sg_b), (E3_b,))
            act(E3, E3, AF.Exp, (E3_b,), (E3_b,), scale=-KAPPA)
            yield
            ps, ps_b = bankA()
            mm(ps[:, 0:T], wa_sb[64:128, sl], xab[64:128, :], True, True, (wres_b, lor_b), (ps_b,))
            act(aG, ps[:, 0:T], AF.Sigmoid, (ps_b,) + CONST, (aG_b,), bias=vt[:, V_A0 + cc:V_A0 + cc + 1])
            yield
            ts("dve", kk, kS, vt[:, V_KK + cc:V_KK + cc + 1], None, ALU.mult, None, (kS_b, vt_b), (kk_b,))
            act(sqk[:], kk, AF.Square, (kk_b,), (sqk_b,))
            ps, ps_b = bankA()
            mm(ps[:, 0:T], bones_bf[:], sqk[:], True, True, (sqk_b, kconst_b), (ps_b,))
            rsqrt_ps(ps, ps_b, 1e-30, dst=RT["rn"][0], dst_b=rn_b)
            tt("dve", kkn, kk, rn, ALU.mult, (kk_b, rn_b), (kkn_b,))
            yield
            stt("dve", Ab[0:64, 2 * cc, :], kkn[0:64, :], -1.0, E3[0:64, :], ALU.mult, ALU.mult, (kkn_b, E3_b), (Ab_b[cc],))
            stt("dve", Ab[64:128, 2 * cc + 1, :], kkn[64:128, :], -1.0, E3[64:128, :], ALU.mult, ALU.mult, (kkn_b, E3_b), (Ab_b[cc],))
            tt("pool", bb, kkn, aG, ALU.mult, (kkn_b, aG_b), (bb_b,))
            act(E2, Lc, AF.Exp, (Lc_b,), (E2_b,), scale=KAPPA)
            tt("dve", Btb[:, cc, :], bb, E2, ALU.mult, (bb_b, E2_b), (Btb_b[cc],))
            yield
            ts("dve", tk, aG, vt[:, V_KA + cc:V_KA + cc + 1], drv[:, DR_OMA + cc:DR_OMA + cc + 1], ALU.mult, ALU.add,
               (aG_b,) + CONST, (tk_b,))
            tt("pool", kmod, kS, tk, ALU.mult, (kS_b, tk_b), (kmod_b,))
            tt("dve", Ktb[:, cc, :], kmod, E2, ALU.mult, (kmod_b, E2_b), (Ktb_b[cc],))
            yield
            Lc3 = Lc.rearrange("p (c t) -> p c t", t=64)
            if os.environ.get("NOBC"):
                tt("pool", E4, Lc, Lc, ALU.subtract, (Lc_b,), (E4_b,))
            else:
                tt("pool", E4.rearrange("p (c t) -> p c t", t=64), Lc3[:, :, 63:64].broadcast_to([128, NCH, 64]), Lc3,
                   ALU.subtract, (Lc_b,), (E4_b,))
            act(E4, E4, AF.Exp, (E4_b,), (E4_b,), scale=-KAPPA)
            tt("dve", Bhb[:, cc, :], bb, E4, ALU.mult, (bb_b, E4_b), (Bhb_b[cc],))
            tt("pool", Khb[:, cc, :], kmod, E4, ALU.mult, (kmod_b, E4_b), (Khb_b[cc],))
            cp("act", vb[:, cc, :], vS, (vS_b,), (vb_b[cc],))
            yield
            tt("pool", rk, rS, kmod, ALU.mult, (rS_b, kmod_b), (rk_b,))
            ts("dve", sqk[:], rk, vt[:, V_RK + cc:V_RK + cc + 1], None, ALU.mult, None, (rk_b, vt_b), (sqk_b,))
            ps, ps_b = bankA()
            mm(ps[:, 0:T], bones_bf[:], sqk[:], True, True, (sqk_b, kconst_b), (ps_b,))
            tt("dve", bon[:, cc, :], ps[:, 0:T], vS, ALU.mult, (ps_b, vS_b), bon_b[cc])
            yield
            ps, ps_b = bankA()
            mm(ps[:, 0:T], g2_sb[:, sl], sgx[:], True, True, (wres_b, lor_b), (ps_b,))
            cp("act", gG[:, cc, :], ps[:, 0:T], (ps_b,), gG_b[cc])
            yield

        if RCUT <= 2:
            rwkv_fill(); return
        for tb in range(NTB):
            blk = slice(tb * 128, (tb + 1) * 128)
            for (src, src_b, dst, dst_b) in ((vb, vb_b, Vtm, Vtm_b), (Bhb, Bhb_b, Bhtm, Bhtm_b), (Khb, Khb_b, None, Khtm_b)):
                ps, ps_b = bankA()
                psbf = ps[:].bitcast(BF16)
                for cc in range(4):
                    tr(psbf[:, cc * 128:(cc + 1) * 128], src[:, cc, blk], ident_bf[:], (src_b[cc], kconst_b), (ps_b,))
                if dst is None:
                    cp("act", Khtmc[0][0:64, tb, :], psbf[0:64, 0:512], (ps_b,), (dst_b[tb],))
                    cp("act", Khtmc[1][64:128, tb, :], psbf[64:128, 0:512], (ps_b,), (dst_b[tb],))
                    yield
                else:
                    cp("act", dst[:, tb, :], psbf[:, 0:512], (ps_b,), (dst_b[tb],))
                    yield

        if RCUT <= 3:
            rwkv_fill(); return
        NSUB = int(os.environ.get("NSUB", "99"))
        for tb in range(NTB):
            blk = slice(tb * 128, (tb + 1) * 128)
            L = LL[0]
            TtA, TtA_b = L["TtA"]; AkT, AkT_b = L["AkT"]; ArbT, ArbT_b = L["ArbT"]; ArkT, ArkT_b = L["ArkT"]
            for bi in range(2):
                hs = [4 * bi + i for i in range(4)]

                def opnd(X, h):
                    if X is Ab or X is Rb:
                        return X[:, h, blk]
                    return X[:, h // 2, blk]

                def batch_mm(lfn, rfn, reads):
                    ps, ps_b = bankA()
                    for i, h in enumerate(hs):
                        mm(ps[:, i * 128:(i + 1) * 128], lfn(i, h), rfn(i, h), True, True, reads, (ps_b,))
                    return ps, ps_b

                ab_r = tuple(Ab_b) + tuple(Btb_b) + tuple(Ktb_b) + tuple(Rb_b)
                (M0, M0_b), (M0t, M0t_b), (M1, M1_b), (M1t, M1t_b) = MM
                ps, ps_b = batch_mm(lambda i, h: opnd(Ab, h), lambda i, h: opnd(Btb, h), ab_r)
                tt("dve", M0[:], ps[:].rearrange("p (i s) -> p i s", i=4), mL4[:], ALU.mult, (ps_b, kconst_b), (M0_b,))
                yield
                if NSUB <= 1:
                    continue
                ps, ps_b = batch_mm(lambda i, h: opnd(Btb, h), lambda i, h: opnd(Ab, h), ab_r)
                tt("dve", M0t[:], ps[:].rearrange("p (i s) -> p i s", i=4), mU4[:], ALU.mult, (ps_b, kconst_b), (M0t_b,))
                if NSUB <= 2:
                    continue
                tt("pool", TtA[:, 4 * bi:4 * bi + 4, :], M0t[:], id4[:], ALU.add, (M0t_b, kconst_b), (TtA_b[bi],))
                yield
                if NSUB <= 3:
                    continue
                cur, cur_b, curt, curt_b = M0, M0_b, M0t, M0t_b
                nxt, nxt_b, nxtt, nxtt_b = M1, M1_b, M1t, M1t_b
                for lvl in range(5):
                    ps, ps_b = batch_mm(lambda i, h: curt[:, i, :], lambda i, h: cur[:, i, :], (cur_b, curt_b))
                    cp("act", nxt[:], ps[:].rearrange("p (i s) -> p i s", i=4), (ps_b,), (nxt_b,))
                    yield
                    if lvl < 4:
                        ps, ps_b = batch_mm(lambda i, h: cur[:, i, :], lambda i, h: curt[:, i, :], (cur_b, curt_b))
                        cp("act", nxtt[:], ps[:].rearrange("p (i s) -> p i s", i=4), (ps_b,), (nxtt_b,))
                        yield
                    ps, ps_b = batch_mm(lambda i, h: nxt[:, i, :], lambda i, h: TtA[:, h, :], (nxt_b, TtA_b[bi]))
                    tt("dve", TtA[:, 4 * bi:4 * bi + 4, :], ps[:].rearrange("p (i s) -> p i s", i=4),
                       TtA[:, 4 * bi:4 * bi + 4, :], ALU.add, (ps_b, TtA_b[bi]), (TtA_b[bi],))
                    yield
                    cur, cur_b, curt, curt_b, nxt, nxt_b, nxtt, nxtt_b = nxt, nxt_b, nxtt, nxtt_b, cur, cur_b, curt, curt_b
                if NSUB <= 4:
                    continue
                ps, ps_b = batch_mm(lambda i, h: opnd(Ktb, h), lambda i, h: opnd(Ab, h), ab_r)
                tt("dve", AkT[:, 4 * bi:4 * bi + 4, :], ps[:].rearrange("p (i s) -> p i s", i=4), mU4[:], ALU.mult,
                   (ps_b, kconst_b), (AkT_b[bi],))
                yield
                ps, ps_b = batch_mm(lambda i, h: opnd(Btb, h), lambda i, h: opnd(Rb, h), ab_r)
                tt("dve", ArbT[:, 4 * bi:4 * bi + 4, :], ps[:].rearrange("p (i s) -> p i s", i=4), mUI4[:], ALU.mult,
                   (ps_b, kconst_b), (ArbT_b[bi],))
                yield
                ps, ps_b = batch_mm(lambda i, h: opnd(Ktb, h), lambda i, h: opnd(Rb, h), ab_r)
                tt("dve", ArkT[:, 4 * bi:4 * bi + 4, :], ps[:].rearrange("p (i s) -> p i s", i=4), mUI4[:], ALU.mult,
                   (ps_b, kconst_b), (ArkT_b[bi],))
                yield

            if RCUT <= 4:
                continue
            LLr = tuple(TtA_b) + tuple(AkT_b) + tuple(ArbT_b) + tuple(ArkT_b)
            for half in range(2):
                ch = 2 * tb + half
                tp = 64 * half
                tsl = slice(tp, tp + 64)
                csl = slice(tb * 128 + tp, tb * 128 + tp + 64)
                Ub = Ubc[half]
                Khtm = Khtmc[half]
                ps, ps_b = bankA()
                for h in range(8):
                    pb, cc = 64 * (h % 2), h // 2
                    hs_ = slice(h * 64, (h + 1) * 64)
                    mm(ps[:, hs_], Ab[:, h, blk], Hb[:, cc, :], True, False, (Ab_b[cc], Hb_b), (ps_b,))
                    mm(ps[:, hs_], AkT[:, h, :], Vtm[:, tb, hs_], False, True, LLr + (Vtm_b[tb],), (ps_b,))
                cp("act", Xb[:], ps[:], (ps_b,), (Xb_b,))
                yield
                ps, ps_b = bankA()
                for h in range(8):
                    hs_ = slice(h * 64, (h + 1) * 64)
                    mm(ps[:, hs_], TtA[:, h, :], Xb[:, hs_], True, True, LLr + (Xb_b,), (ps_b,))
                cp("dve", Ub[tsl, :], ps[tsl, :], (ps_b,), (Ub_b,))
                yield
                ps, ps_b = bankA()
                for h in range(8):
                    pb, cc = 64 * (h % 2), h // 2
                    hs_ = slice(h * 64, (h + 1) * 64)
                    o = ps[pb:pb + 64, cc * 64:(cc + 1) * 64]
                    mm(o, Hb[:, cc, :], Rb[:, h, csl], True, False, (Hb_b, Rb_b[cc]), (ps_b,))
                    mm(o, Ub[:, hs_], ArbT[:, h, tsl], False, False, LLr + (Ub_b,), (ps_b,))
                    mm(o, Vtm[:, tb, hs_], ArkT[:, h, tsl], False, True, LLr + (Vtm_b[tb],), (ps_b,))
                cp("act", yR[:, :, csl], ps[:, 0:256].rearrange("p (c t) -> p c t", c=4), (ps_b,), (yR_b,))
                yield
                ps, ps_b = bankA()
                for h in range(8):
                    pb, cc = 64 * (h % 2), h // 2
                    hs_ = slice(h * 64, (h + 1) * 64)
                    o = ps[pb:pb + 64, cc * 64:(cc + 1) * 64]
                    mm(o, Bhtm[:, tb, hs_], Ub[:, hs_], True, False, (Bhtm_b[tb], Ub_b), (ps_b,))
                    mm(o, Khtm[:, tb, hs_], Vtm[:, tb, hs_], False, True, (Khtm_b[tb], Vtm_b[tb]), (ps_b,))
                tt("dve", Hst[:], Hst[:], gL[:, :, ch:ch + 1].broadcast_to([128, 4, 64]), ALU.mult, (Hst_b, gL_b), (Hst_b,))
                tt("dve", Hst[:], Hst[:], ps[:, 0:256].rearrange("p (c v) -> p c v", c=4), ALU.add, (Hst_b, ps_b), (Hst_b,))
                cp("act", Hb[:], Hst[:], (Hst_b,), (Hb_b,))
                yield

        if RCUT <= 5:
            rwkv_fill(); return
        for cc in range(4):
            ps, ps_b = bankA()
            mm(ps[:, 0:T], bones_f[:], yR[:, cc, :], True, True, (yR_b, kconst_b), (ps_b,))
            tt("dve", yc, yR[:, cc, :], ps[:, 0:T], ALU.subtract, (yR_b, ps_b), (yc_b,))
            act(sqk[:], yc, AF.Square, (yc_b,), (sqk_b,))
            ps, ps_b = bankA()
            mm(ps[:, 0:T], bones_bf[:], sqk[:], True, True, (sqk_b, kconst_b), (ps_b,))
            rsqrt_ps(ps, ps_b, 64 * 64e-5, dst=RT["rn"][0], dst_b=rn_b)
            tt("dve", yc, yc, rn, ALU.mult, (yc_b, rn_b), (yc_b,))
            act(yn, yc, AF.Identity, (yc_b,) + CONST, (yn_b,), bias=vt[:, V_LNB + cc:V_LNB + cc + 1],
                scale=drv[:, DR_LNW8 + cc:DR_LNW8 + cc + 1])
            tt("pool", yn, yn, bon[:, cc, :], ALU.add, (yn_b,) + bon_b[cc], (yn_b,))
            tt("dve", yT[:, 4 + cc, :], yn, gG[:, cc, :], ALU.mult, (yn_b,) + gG_b[cc], (yT_b[4 + cc],))
            yield

    def outproj():
        for g in range(4):
            w, w_b = load_win("wout", g * 256, 256)
            for jj in range(2):
                c = 2 * g + jj
                ps, ps_b = bankA()
                for kc in range(8):
                    mm(ps[:, 0:T], w[:, kc, jj * 128:(jj + 1) * 128], yT[:, kc, :], kc == 0, kc == 7,
                       (w_b, yT_b[kc]), (ps_b,))
                stt("dve", hT[:, c, :], ps[:, 0:T], modT[:, GT2 + c:GT2 + c + 1], hT[:, c, :], ALU.mult, ALU.add,
                    (ps_b, hT_b[c]) + CONST, (hT_b[c],))

    xslot = [0]
    oslot = [0]
    for it in range(NT):
        t0 = it * T
        for tb in range(NTB):
            s = 0
            dma("sp", xin[s][:], x_d[t0 + tb * 128:t0 + (tb + 1) * 128, :], (), (xin_b[s],), "xin0")
            for half in range(2):
                ps, ps_b = bankA()
                for c4 in range(4):
                    c = half * 4 + c4
                    tr(ps[:, c4 * 128:(c4 + 1) * 128], xin[s][:, c * 128:(c + 1) * 128], ident_f,
                       (xin_b[s], cst_b), (ps_b,))
                cp("act", hT[:, half * 4:half * 4 + 4, tb * 128:(tb + 1) * 128],
                   ps[:].rearrange("p (c t) -> p c t", c=4), (ps_b,), tuple(hT_b[half * 4:half * 4 + 4]))

        if "ffn1" in stages:
            norm_mod(lambda c: drv[:, DR_GM1 + c:DR_GM1 + c + 1], lambda c: modT[:, SH1 + c:SH1 + c + 1], uT, uT_b)
            ffn("w1g", "w1u", "w1d", DR_CG1)

        if "mix" in stages:
            norm_mod(lambda c: drv[:, DR_GM2 + c:DR_GM2 + c + 1], lambda c: modT[:, SH2 + c:SH2 + c + 1], uT, uT_b)
            if "nomla" in stages:
                for c in range(4):
                    memset("pool", yT[:, c, :], 0.0, (yT_b[c],))
            else:
                gen = None if "norwkv" in stages else rwkv(it, t0)
                nper = max(1, -(-RWKV_STEPS // (8 * (it + 1))))

                def step(gen=gen, nper=nper):
                    if gen is None:
                        return
                    for _ in range(nper):
                        next(gen, None)

                mla(it, t0, step)
                if gen is not None:
                    for _ in gen:
                        pass
            if "norwkv" in stages:
                for c in range(4, 8):
                    memset("pool", yT[:, c, :], 0.0, (yT_b[c],))
            elif "nomla" in stages:
                for _ in rwkv(it, t0):
                    pass
            outproj()

        if "ffn2" in stages:
            norm_mod(lambda c: drv[:, DR_GM3 + c:DR_GM3 + c + 1], lambda c: modT[:, SH3 + c:SH3 + c + 1], uT, uT_b)
            ffn("w3g", "w3u", "w3d", DR_CG3)

        for c in range(8):
            act(sqb[:, c, :], hT[:, c, :], AF.Square, (hT_b[c],), (sqb_b[c],))
        ps, ps_b = bankA()
        for c in range(8):
            mm(ps[:, 0:T], ones_bf[:], sqb[:, c, :], c == 0, c == 7, (sqb_b[c], kconst_b), (ps_b,))
        rsqrt_ps(ps, ps_b, D * 1e-6)
        for c in range(8):
            stt("dve", hT[:, c, :], hT[:, c, :], drv[:, DR_GF + c:DR_GF + c + 1], tmpA[:], ALU.mult, ALU.mult,
                (hT_b[c], tmpA_b) + CONST, (hT_b[c],))
        for tb in range(NTB):
            s = 0
            for half in range(2):
                ps, ps_b = bankA()
                for c4 in range(4):
                    c = half * 4 + c4
                    tr(ps[:, c4 * 128:(c4 + 1) * 128], hT[:, c, tb * 128:(tb + 1) * 128], ident_f,
                       (hT_b[c], cst_b), (ps_b,))
                cp("act" if half == 0 else "dve", ost[s][:, half * 512:(half + 1) * 512], ps[:], (ps_b,), (ost_b[s],))
            dma("pool", out_d[t0 + tb * 128:t0 + (tb + 1) * 128, :], ost[s][:], (ost_b[s],), (), "xin0o")

    P.emit()

    if os.environ.get("KDBG"):
        print("STATS", P.stats)
    es.close()
    return nc


def _consts():
    c = np.zeros((128, NCST), np.float32)
    p = np.arange(128)
    c[:, C_ID:C_ID + 128] = np.eye(128, dtype=np.float32)
    same = (p[:, None] // 64) == (p[None, :] // 64)
    c[:, C_ML:C_ML + 128] = (same & (p[None, :] < p[:, None])).astype(np.float32)
    c[:, C_MU:C_MU + 128] = (same & (p[:, None] < p[None, :])).astype(np.float32)
    c[:, C_MUI:C_MUI + 128] = (same & (p[:, None] <= p[None, :])).astype(np.float32)
    c[:, C_BO:C_BO + 128] = same.astype(np.float32)
    c[:, C_INVF] = (10000.0 ** (-(np.arange(128) % 32).astype(np.float32) / 32.0)).astype(np.float32)
    c[:, C_SS] = np.where((p % 64) < 32, -1.0, 1.0)
    c[:, C_RST:C_RST + 512] = ((np.arange(512) % 64) != 0).astype(np.float32)[None, :]
    return c


def _vt(inp, b):
    def ch(v):
        v = np.asarray(v, np.float32).reshape(-1, 128)
        return v.T
    sm = inp["rwkv_shift_mix"][0]
    cols = [
        ch(inp["c"][b]), ch(inp["b_mod"][0]), ch(inp["ffn1_norm_g"][0]), ch(inp["mix_norm_g"][0]),
        ch(inp["ffn2_norm_g"][0]), ch(inp["final_norm_g"]), ch(inp["q_norm_g"][0]), ch(inp["kv_norm_g"][0]),
        ch(sm[0:512]), ch(sm[512:1024]), ch(sm[1024:1536]), ch(sm[1536:1664]), ch(sm[1664:1792]),
        ch(inp["rwkv_w0"][0]), ch(inp["rwkv_a0"][0]), ch(inp["rwkv_k_k"][0]), ch(inp["rwkv_k_a"][0]),
        ch(inp["rwkv_r_k"][0].reshape(-1)), ch(inp["rwkv_ln_w"][0]), ch(inp["rwkv_ln_b"][0]),
    ]
    v = np.concatenate(cols, axis=1)
    assert v.shape == (128, NV), v.shape
    return np.ascontiguousarray(v, np.float32)


def make_in_maps(inp, S, cores):
    f = lambda a: np.ascontiguousarray(np.asarray(a, np.float32))
    shared = {
        "cst": _consts(),
        "gat": np.ascontiguousarray(np.tile(f(inp["attn_out_norm_g"][0])[None, :], (128, 1))),
        "w_mod": f(inp["w_mod"][0]),
        "w1g": f(inp["ffn1_w_gate"][0]), "w1u": f(inp["ffn1_w_up"][0]), "w1d": f(inp["ffn1_w_down"][0]),
        "win": f(inp["w_in"][0]), "wuq": f(inp["w_uq"][0]), "wukv": f(inp["w_ukv"][0]),
        "w2": f(inp["rwkv_w2"][0]), "a2": f(inp["rwkv_a2"][0]), "g2": f(inp["rwkv_g2"][0]),
        "wout": f(inp["w_out"][0]),
        "w3g": f(inp["ffn2_w_gate"][0]), "w3u": f(inp["ffn2_w_up"][0]), "w3d": f(inp["ffn2_w_down"][0]),
    }
    maps = []
    for b in cores:
        m = dict(shared)
        m["x"] = f(inp["x"][b][:S])
        m["pos"] = np.ascontiguousarray(np.tile(np.asarray(inp["positions"][b][:S], np.int32)[None, :], (64, 1)))
        m["vt"] = _vt(inp, b)
        maps.append(m)
    return maps


def kernel(**inputs):
    S = inputs["x"].shape[1]
    B = inputs["x"].shape[0]
    nc = build_nc(S)
    in_maps = make_in_maps(inputs, S, list(range(B)))
    res = run_bass_kernel_spmd(nc, in_maps, core_ids=list(range(B)))
    return np.stack([np.asarray(r["out"], np.float32) for r in res.results], axis=0)
```

```python
import math
from contextlib import ExitStack

import numpy as np
import concourse.bass as bass
import concourse.mybir as mybir
from concourse.bass_utils import run_bass_kernel_spmd

F32 = mybir.dt.float32
BF16 = mybir.dt.bfloat16
I32 = mybir.dt.int32
AF = mybir.ActivationFunctionType
ALU = mybir.AluOpType
AX = mybir.AxisListType

D = 1024
DFF = 2816
T = 256
NTB = T // 128
NJ = DFF // 128
KAPPA = math.exp(-0.5)
SCALE = 192.0 ** -0.5
PI = math.pi

V_CT, V_BM, V_G1, V_G2, V_G3, V_GF, V_GQ, V_GKV = 0, 8, 80, 88, 96, 104, 112, 115
V_MIXR, V_MIXK, V_MIXV, V_MIXWA, V_MIXG = 117, 121, 125, 129, 130
V_W0, V_A0, V_KK, V_KA, V_RK, V_LNW, V_LNB = 131, 135, 139, 143, 147, 151, 155
NV = 159
C_ID, C_ML, C_MU, C_MUI, C_BO, C_INVF, C_SS, C_RST = 0, 128, 256, 384, 512, 640, 641, 642
NCST = 642 + 512


import os
POOL2DVE = bool(os.environ.get('POOL2DVE'))


class Buf:
    __slots__ = ("name", "w", "wd", "r", "rd")

    def __init__(self, name):
        self.name = name
        self.w = {}
        self.wd = []
        self.r = {}
        self.rd = []


class Op:
    __slots__ = ("idx", "eng", "fn", "deps", "is_dma", "dkey", "dval", "signal", "sigval", "raw")

    def __init__(self, idx, eng, fn, is_dma, dkey, dval):
        self.idx = idx
        self.eng = eng
        self.fn = fn
        self.deps = []
        self.is_dma = is_dma
        self.dkey = dkey
        self.dval = dval
        self.signal = False
        self.sigval = 0


class Prog:
    ENGS = ("pe", "act", "dve", "pool", "sp")

    def __init__(self, nc, es):
        self.nc = nc
        self.es = es
        self.ops = []
        self.eng_ops = {e: [] for e in self.ENGS}
        self.dcount = {}
        self.nbuf = 0

    def buf(self, name=None):
        self.nbuf += 1
        return Buf(name or f"b{self.nbuf}")

    def bufs(self, n, name=None):
        return [self.buf(f"{name}{i}") for i in range(n)]

    def add(self, eng, fn, reads=(), writes=(), dkey=None):
        is_dma = dkey is not None
        if eng == "pool" and not is_dma and POOL2DVE:
            eng = "dve"
        dval = 0
        if is_dma:
            self.dcount[dkey] = self.dcount.get(dkey, 0) + 1
            dval = 16 * self.dcount[dkey]
        op = Op(len(self.ops), eng, fn, is_dma, dkey, dval)
        deps = {}

        def dep(o, raw):
            if o is op:
                return
            if (not o.is_dma) and (not is_dma) and o.eng == eng:
                if eng == "pe":
                    return
            deps[o.idx] = o

        for b in reads:
            for o in b.w.values():
                dep(o, True)
            for o in b.wd:
                dep(o, True)
        for b in writes:
            for o in b.r.values():
                dep(o, False)
            for o in b.rd:
                dep(o, False)
            for o in b.w.values():
                dep(o, False)
            for o in b.wd:
                dep(o, False)
        for b in reads:
            if is_dma:
                b.rd.append(op)
            else:
                b.r[eng] = op
        for b in writes:
            if b.r or b.rd:
                keep_r = None
                b.w = {}
                b.wd = []
                b.r = {}
                b.rd = []
            if is_dma:
                b.wd.append(op)
            else:
                b.w[eng] = op
        op.deps = list(deps.values())
        self.ops.append(op)
        self.eng_ops[eng].append(op)
        return op

    def emit(self):
        nc = self.nc
        es = self.es
        for op in self.ops:
            for d in op.deps:
                if not d.is_dma:
                    d.signal = True
        cnt = {e: 0 for e in self.ENGS}
        for op in self.ops:
            if (not op.is_dma) and op.signal:
                cnt[op.eng] += 1
                op.sigval = cnt[op.eng]
        self.stats = dict(cnt=dict(cnt), nops={e: len(v) for e, v in self.eng_ops.items()}, dmax=max(self.dcount.values()) * 16)
        esem = {e: es.enter_context(nc.semaphore("s_" + e)) for e in self.ENGS}
        dsem = {k: es.enter_context(nc.semaphore("d_" + k)) for k in self.dcount}
        block = es.enter_context(nc.Block())

        def gen(engname):
            def body(e):
                known = {}
                for op in self.eng_ops[engname]:
                    for d in op.deps:
                        if d.is_dma:
                            key, sem, val = "d_" + d.dkey, dsem[d.dkey], d.dval
                        else:
                            key, sem, val = "e_" + d.eng, esem[d.eng], d.sigval
                        if known.get(key, 0) >= val:
                            continue
                        e.wait_ge(sem, val)
                        known[key] = val
                    ins = op.fn(e)
                    if op.is_dma:
                        ins.then_inc(dsem[op.dkey], 16)
                    elif op.signal:
                        ins.then_inc(esem[op.eng], 1)
                if engname == "sp":
                    for k, n in self.dcount.items():
                        e.wait_ge(dsem[k], 16 * n)

            return body

        block.tensor(gen("pe"))
        block.vector(gen("dve"))
        block.scalar(gen("act"))
        block.gpsimd(gen("pool"))
        block.sync(gen("sp"))


def build_nc(S, stages=("ffn1", "mix", "ffn2"), dbg=()):
    NT = S // T
    nc = bass.Bass("TRN2", target_bir_lowering=False)
    es = ExitStack()
    P = Prog(nc, es)

    def din(name, shape, dt=F32):
        return nc.dram_tensor(name, list(shape), dt, kind="ExternalInput").ap()

    x_d = din("x", [S, D])
    pos_d = din("pos", [64, S], I32)
    vt_d = din("vt", [128, NV])
    cst_d = din("cst", [128, NCST])
    gat_d = din("gat", [128, 512])
    wmod_d = din("w_mod", [D, 9 * D])
    wsrc = {
        "w1g": din("w1g", [D, DFF]), "w1u": din("w1u", [D, DFF]), "w1d": din("w1d", [DFF, D]),
        "win": din("win", [D, 2496]), "wuq": din("wuq", [384, 768]), "wukv": din("wukv", [256, 1024]),
        "w2": din("w2", [64, 512]), "a2": din("a2", [64, 512]), "g2": din("g2", [128, 512]),
        "wout": din("wout", [D, D]),
        "w3g": din("w3g", [D, DFF]), "w3u": din("w3u", [D, DFF]), "w3d": din("w3d", [DFF, D]),
    }
    out_d = nc.dram_tensor("out", [S, D], F32, kind="ExternalOutput").ap()
    dbg_d = {}
    for name, shape in dbg:
        dbg_d[name] = nc.dram_tensor("dbg_" + name, list(shape), F32, kind="ExternalOutput").ap()

    TILED = ("w1g", "w1u", "w3g", "w3u")
    wb = {k: nc.dram_tensor(k + "_bf", ([NJ, 128, 8 * 128] if k in TILED else list(v.shape)), BF16).ap()
          for k, v in wsrc.items()}
    wb_buf = {k: P.buf("wb_" + k) for k in wsrc}
    Kc_d = nc.dram_tensor("Kc", [4, 128, S], BF16).ap()
    Vc_d = nc.dram_tensor("Vc", [4, NT, 128, NTB * 129], BF16).ap()
    Kc_buf = [[P.buf(f"Kc{h}_{i}") for i in range(NT)] for h in range(4)]
    Vc_buf = [[P.buf(f"Vc{h}_{i}") for i in range(NT)] for h in range(4)]

    def sb(name, shape, dt=F32):
        return es.enter_context(nc.sbuf_tensor("sb_" + name, list(shape), dt))

    vt = sb("vt", [128, NV]); vt_b = P.buf("vt")
    cst = sb("cst", [128, NCST]); cst_b = P.buf("cst")
    drv = sb("drv", [128, 96]); drv_b = P.buf("drv")
    modT = sb("modT", [128, 72]); mod_b = P.buf("modT")
    cact = sb("cact", [128, 8]); cact_b = P.buf("cact")
    ident_bf = sb("ident_bf", [128, 128], BF16)
    ones_bf = sb("ones_bf", [128, 128], BF16)
    bones_bf = sb("bones_bf", [128, 128], BF16)
    bones_f = sb("bones_f", [128, 128])
    mL4 = sb("mL4", [128, 4, 128], BF16); mU4 = sb("mU4", [128, 4, 128], BF16); mUI4 = sb("mUI4", [128, 4, 128], BF16)
    id4 = sb("id4", [128, 4, 128], BF16)
    gat = sb("gat", [128, 512])
    kconst_b = P.buf("kconst")

    hT = sb("hT", [128, 8, T]); hT_b = P.bufs(8, "hT")
    uT = sb("uT", [128, 8, T], BF16); uT_b = P.bufs(8, "uT")
    sqb = sb("sqb", [128, 8, T], BF16); sqb_b = P.bufs(8, "sqb")
    tmpA = sb("tmpA", [128, T]); tmpA_b = P.buf("tmpA")
    tmpB = [sb(f"tmpB{i}", [128, T]) for i in range(2)]; tmpB_b = P.bufs(2, "tmpB")
    arena = sb("arena", [128, NJ * T], BF16)
    actT = arena[:].rearrange("p (j t) -> p j t", j=NJ); actT_b = P.bufs(NJ, "actT")
    sgt = [sb(f"sgt{i}", [128, T]) for i in range(2)]; sgt_b = P.bufs(2, "sgt")
    xin = [sb(f"xin{i}", [128, D]) for i in range(1)]; xin_b = P.bufs(1, "xin")
    ost = xin; ost_b = xin_b
    wgs = [sb(f"wgs{i}", [128, 8, 128], BF16) for i in range(2)]; wgs_b = P.bufs(2, "wgs")
    wus = [sb(f"wus{i}", [128, 8, 128], BF16) for i in range(2)]; wus_b = P.bufs(2, "wus")
    wds = [sb(f"wds{i}", [128, 4, 512], BF16) for i in range(2)]; wds_b = P.bufs(2, "wds")
    wmods = [xin[0][:].rearrange("p (kc n) -> p kc n", kc=8)]; wmods_b = [xin_b[0]]


    Krc_d = nc.dram_tensor("Krc", [64, S], BF16).ap()
    Krc_buf = [P.buf(f"Krc{i}") for i in range(NT)]
    wuq_sb = sb("wuq_sb", [128, 3, 768], BF16)
    wuq_rs = sb("wuq_rs", [128, 3, 4, 64], BF16)
    wukv_sb = sb("wukv_sb", [128, 2, 1024], BF16)
    wuv_sb = sb("wuv_sb", [128, 2, 4, 128], BF16)
    wkr_sb = sb("wkr_sb", [128, 8, 128], BF16)
    wa_sb = sb("wa_sb", [128, 512], BF16)
    g2_sb = sb("g2_sb", [128, 512], BF16)
    wres_b = P.buf("wres")
    wins = [sb(f"wins{i}", [128, 8, 256], BF16) for i in range(2)]; wins_b = P.bufs(2, "wins")
    w128 = [sb(f"w128_{i}", [128, 8, 128], BF16) for i in range(2)]; w128_b = P.bufs(2, "w128")
    pq = sb("pq", [128, 3, T]); pq_b = P.bufs(3, "pq")
    qn = sb("qn", [128, 3, T], BF16); qn_b = P.bufs(3, "qn")
    pkv = sb("pkv", [128, 2, T]); pkv_b = P.bufs(2, "pkv")
    kvn = sb("kvn", [128, 2, T], BF16); kvn_b = P.bufs(2, "kvn")
    QnT = sb("QnT", [128, 4, T], BF16); QnT_b = P.bufs(4, "QnT")
    QrT = sb("QrT", [128, 4, T], BF16); QrT_b = P.bufs(4, "QrT")
    Kst = sb("Kst", [128, 4, T], BF16); Kst_b = P.buf("Kst")
    Krst = sb("Krst", [128, T], BF16); Krst_b = P.buf("Krst")
    Vst = sb("Vst", [128, 4, NTB, 129], BF16); Vst_b = P.buf("Vst")
    KVG = 4
    Kt = [sb(f"Kt{i}", [128, KVG * T], BF16) for i in range(2)]; Kt_b = P.bufs(2, "Kt")
    Krt = [sb(f"Krt{i}", [128, KVG * T], BF16) for i in range(2)]; Krt_b = P.bufs(2, "Krt")
    Vt = [sb(f"Vt{i}", [128, KVG, NTB * 129], BF16) for i in range(2)]; Vt_b = P.bufs(2, "Vt")
    PT = [sb(f"PT{i}", [128, T], BF16) for i in range(3)]; PT_b = P.bufs(3, "PT")
    posi = sb("posi", [128, T], I32); posi_b = P.buf("posi")
    ang = sb("ang", [128, T]); ang_b = P.buf("ang")
    ang2 = sb("ang2", [128, T]); ang2_b = P.buf("ang2")
    cosT = sb("cosT", [128, T]); sinS = sb("sinS", [128, T]); rope_b = P.buf("rope")
    rtmp = [sb(f"rtmp{i}", [128, T]) for i in range(4)]; rtmp_b = P.bufs(4, "rtmp")
    ya = sb("ya", [128, NTB, 512]); ya_b = P.bufs(NTB, "ya")
    yab = sb("yab", [128, NTB, 512], BF16); yab_b = P.bufs(NTB, "yab")
    yT = sb("yT", [128, 8, T], BF16); yT_b = P.bufs(8, "yT")
    ssq = sb("ssq", [128, 8]); ssq_b = P.buf("ssq")
    rec = sb("rec", [128, 8]); rec_b = P.buf("rec")


    NCH = T // 64
    Hst = sb("Hst", [128, 4, 64]); Hst_b = P.buf("Hst")
    Hb = sb("Hb", [128, 4, 64], BF16); Hb_b = P.buf("Hb")
    carry = sb("carry", [128, 14]); carry_b = P.bufs(14, "carry")
    Bt = [sb(f"Bt{i}", [128, T + 8]) for i in range(2)]; Bt_b = P.bufs(2, "Bt")
    xwxa = sb("xwxa", [128, T]); xwxa_b = P.buf("xwxa")
    xg = sb("xg", [128, T]); xg_b = P.buf("xg")
    thb = sb("thb", [128, T], BF16); xab = sb("xab", [128, T], BF16); sgx = sb("sgx", [128, T], BF16)
    lor_b = P.buf("lor")
    RT = {}
    for _n in ("rS", "kS", "vS", "sg", "Lc", "E1", "E3", "aG", "kk", "rn", "kkn", "bb", "E2", "tk", "kmod", "E4"):
        RT[_n] = (sb("r_" + _n, [128, T]), P.buf("r_" + _n))
    sqk = sb("sqk", [128, T], BF16); sqk_b = P.buf("sqk")
    Ab = sb("Ab", [128, 8, T], BF16); Ab_b = P.bufs(4, "Ab")
    Btb = sb("Btb", [128, 4, T], BF16); Btb_b = P.bufs(4, "Btb")
    Ktb = sb("Ktb", [128, 4, T], BF16); Ktb_b = P.bufs(4, "Ktb")
    Rb = sb("Rb", [128, 8, T], BF16); Rb_b = P.bufs(4, "Rb")
    vb = sqb[:, 0:4, :]; vb_b = sqb_b[0:4]
    Bhb = sqb[:, 4:8, :]; Bhb_b = sqb_b[4:8]
    Khb = sb("Khb", [128, 4, T], BF16); Khb_b = P.bufs(4, "Khb")
    Vtm = sb("Vtm", [128, NTB, 512], BF16); Vtm_b = P.bufs(NTB, "Vtm")
    Bhtm = sb("Bhtm", [128, NTB, 512], BF16); Bhtm_b = P.bufs(NTB, "Bhtm")
    Khtmc = [sb(f"Khtm{i}", [128, NTB, 512], BF16) for i in range(2)]; Khtm_b = P.bufs(NTB, "Khtm")
    bon = arena[:, 0:8 * T].bitcast(F32).rearrange("p (c t) -> p c t", c=4)
    gG = arena[:, 8 * T:16 * T].bitcast(F32).rearrange("p (c t) -> p c t", c=4)
    bon_b = [(actT_b[2 * c], actT_b[2 * c + 1]) for c in range(4)]
    gG_b = [(actT_b[8 + 2 * c], actT_b[8 + 2 * c + 1]) for c in range(4)]
    yR = sb("yR", [128, 4, T]); yR_b = P.buf("yR")
    gL = sb("gL", [128, 4, NCH]); gL_b = P.buf("gL")
    LL = []
    for _i in range(1):
        LL.append({n: (sb(f"{n}{_i}", [128, 8, 128], BF16), P.bufs(2, f"{n}{_i}")) for n in ("TtA", "AkT", "ArbT", "ArkT")})
    MM = [(sb(f"Mm{i}", [128, 4, 128], BF16), P.buf(f"Mm{i}")) for i in range(4)]
    Xb = sb("Xb", [128, 512], BF16); Xb_b = P.buf("Xb")
    Ubc = [sb(f"Ub{i}", [128, 512], BF16) for i in range(2)]; Ub_b = P.buf("Ub")

    pbank = [es.enter_context(nc.psum_tensor(f"pb{i}", [128, 512], F32)) for i in range(8)]
    pbank_b = P.bufs(8, "pb")
    rrA = [0]

    def bankA():
        k = rrA[0] % 4
        rrA[0] += 1
        return pbank[k], pbank_b[k]

    def bankB(i):
        return pbank[4 + i], pbank_b[4 + i]

    def dma(q, out, in_, reads, writes, key):
        return P.add(q, lambda e: e.dma_start(out=out, in_=in_), reads, writes, dkey=key)

    def mm(out, lhsT, rhs, start, stop, reads, writes):
        return P.add("pe", lambda e: e.matmul(out, lhsT=lhsT, rhs=rhs, start=start, stop=stop), reads, writes)

    def tr(out, in_, ident, reads, writes):
        return P.add("pe", lambda e: e.transpose(out, in_, ident), reads, writes)

    def act(out, in_, func, reads, writes, bias=0.0, scale=1.0, accum_out=None):
        if accum_out is None:
            return P.add("act", lambda e: e.activation(out=out, in_=in_, func=func, bias=bias, scale=scale), reads, writes)
        return P.add("act", lambda e: e.activation(out=out, in_=in_, func=func, bias=bias, scale=scale, accum_out=accum_out), reads, writes)

    def tt(eng, out, in0, in1, op, reads, writes):
        return P.add(eng, lambda e: e.tensor_tensor(out=out, in0=in0, in1=in1, op=op), reads, writes)

    def ts(eng, out, in0, s1, s2, op0, op1, reads, writes):
        if op1 is None:
            return P.add(eng, lambda e: e.tensor_scalar(out=out, in0=in0, scalar1=s1, scalar2=None, op0=op0), reads, writes)
        return P.add(eng, lambda e: e.tensor_scalar(out=out, in0=in0, scalar1=s1, scalar2=s2, op0=op0, op1=op1), reads, writes)

    def stt(eng, out, in0, scalar, in1, op0, op1, reads, writes):
        return P.add(eng, lambda e: e.scalar_tensor_tensor(out=out, in0=in0, scalar=scalar, in1=in1, op0=op0, op1=op1), reads, writes)

    def cp(eng, out, in_, reads, writes):
        if eng == "act":
            return P.add(eng, lambda e: e.activation(out=out, in_=in_, func=AF.Identity), reads, writes)
        return P.add(eng, lambda e: e.tensor_copy(out=out, in_=in_), reads, writes)

    def memset(eng, ap, val, writes):
        return P.add(eng, lambda e: e.memset(ap, val), (), writes)

    for k, src in wsrc.items():
        if k in TILED:
            for j in range(NJ):
                dma("pool", wb[k][j].rearrange("p (kc n) -> p kc n", kc=8),
                    src[:, j * 128:(j + 1) * 128].rearrange("(kc p) n -> p kc n", p=128), (), (wb_buf[k],), "wc_" + k)
            continue
        rows = src.shape[0]
        step = 256 if rows >= 256 else rows
        for r0 in range(0, rows, step):
            r1 = min(rows, r0 + step)
            dma("pool", wb[k][r0:r1, :], src[r0:r1, :], (), (wb_buf[k],), "wc_" + k)

    dma("sp", vt[:], vt_d, (), (vt_b,), "c_vt")
    dma("sp", cst[:], cst_d, (), (cst_b,), "c_cst")
    dma("sp", gat[:], gat_d, (), (kconst_b,), "c_gat")

    cp("dve", ident_bf[:], cst[:, C_ID:C_ID + 128], (cst_b,), (kconst_b,))
    memset("dve", ones_bf[:], 1.0, (kconst_b,))
    cp("dve", bones_bf[:], cst[:, C_BO:C_BO + 128], (cst_b,), (kconst_b,))
    ts("dve", bones_f[:], cst[:, C_BO:C_BO + 128], 1.0 / 64.0, None, ALU.mult, None, (cst_b,), (kconst_b,))
    for i in range(4):
        cp("dve", mL4[:, i, :], cst[:, C_ML:C_ML + 128], (cst_b,), (kconst_b,))
        cp("dve", mU4[:, i, :], cst[:, C_MU:C_MU + 128], (cst_b,), (kconst_b,))
        cp("dve", mUI4[:, i, :], cst[:, C_MUI:C_MUI + 128], (cst_b,), (kconst_b,))
        cp("dve", id4[:, i, :], cst[:, C_ID:C_ID + 128], (cst_b,), (kconst_b,))
    ident_f = cst[:, C_ID:C_ID + 128]

    act(cact[:], vt[:, V_CT:V_CT + 8], AF.Silu, (vt_b,), (cact_b,))
    mps, mps_b = bankA()
    for j in range(72):
        s = 0
        dma("sp", wmods[s], wmod_d[:, j * 128:(j + 1) * 128].rearrange("(kc p) n -> p kc n", p=128),
            (), (wmods_b[s],), "xin0")
        for kc in range(8):
            mm(mps[:, j:j + 1], wmods[s][:, kc, :], cact[:, kc:kc + 1],
               kc == 0, kc == 7, (wmods_b[s], cact_b), (mps_b,))
    tt("dve", modT[:], mps[:, 0:72], vt[:, V_BM:V_BM + 72], ALU.add, (mps_b, vt_b), (mod_b,))

    DR_GM1, DR_GM2, DR_GM3, DR_CG1, DR_CG3, DR_GF, DR_GQ, DR_GKV = 0, 8, 16, 24, 32, 40, 48, 51
    DR_OMR, DR_OMK, DR_OMV, DR_OMWA, DR_OMG, DR_OMA, DR_LNW8 = 53, 57, 61, 65, 66, 67, 71
    SH1, SC1, GT1, SH2, SC2, GT2, SH3, SC3, GT3 = [8 * i for i in range(9)]

    def dts(out_c, n, in_ap, s1, s2, op0, op1):
        ts("dve", drv[:, out_c:out_c + n], in_ap, s1, s2, op0, op1, (mod_b, vt_b, drv_b), (drv_b,))

    for (dst, sc, g) in ((DR_GM1, SC1, V_G1), (DR_GM2, SC2, V_G2), (DR_GM3, SC3, V_G3)):
        dts(dst, 8, modT[:, sc:sc + 8], 1.0, 32.0, ALU.add, ALU.mult)
        tt("dve", drv[:, dst:dst + 8], drv[:, dst:dst + 8], vt[:, g:g + 8], ALU.mult, (drv_b, vt_b), (drv_b,))
    dts(DR_CG1, 8, modT[:, GT1:GT1 + 8], 0.5, None, ALU.mult, None)
    dts(DR_CG3, 8, modT[:, GT3:GT3 + 8], 0.5, None, ALU.mult, None)
    dts(DR_GF, 8, vt[:, V_GF:V_GF + 8], 32.0, None, ALU.mult, None)
    dts(DR_GQ, 3, vt[:, V_GQ:V_GQ + 3], math.sqrt(384.0), None, ALU.mult, None)
    dts(DR_GKV, 2, vt[:, V_GKV:V_GKV + 2], 16.0, None, ALU.mult, None)
    dts(DR_OMR, 14, vt[:, V_MIXR:V_MIXR + 14], -1.0, 1.0, ALU.mult, ALU.add)
    dts(DR_OMA, 4, vt[:, V_KA:V_KA + 4], -1.0, 1.0, ALU.mult, ALU.add)
    dts(DR_LNW8, 4, vt[:, V_LNW:V_LNW + 4], 8.0, None, ALU.mult, None)
    CONST = (drv_b, mod_b, vt_b, kconst_b, cst_b)
    epst = sb("epst", [128, 16])
    eps_cols = {}

    def eps_ap(val, np_=128):
        if val not in eps_cols:
            k = len(eps_cols)
            eps_cols[val] = k
            memset("dve", epst[:, k:k + 1], float(val), (kconst_b,))
        k = eps_cols[val]
        return epst[0:np_, k:k + 1]

    for _v in (D * 1e-6, 384 * 1e-6, 256 * 1e-6, 1e-30, 64 * 64e-5, -PI, 0.0):
        eps_ap(_v)


    def ldw(out, in_, k):
        dma("sp", out, in_, (wb_buf[k],), (wres_b,), "wres")
    ldw(wuq_sb[:], wb["wuq"].rearrange("(kc p) n -> p kc n", p=128), "wuq")
    wuq4 = wb["wuq"].rearrange("(kc p) (h d) -> p kc h d", p=128, d=192)
    for kc in range(3):
        ldw(wuq_rs[:, kc, :, 0:32], wuq4[:, kc, :, 160:192], "wuq")
        ldw(wuq_rs[:, kc, :, 32:64], wuq4[:, kc, :, 128:160], "wuq")
    ldw(wukv_sb[:], wb["wukv"].rearrange("(kc p) n -> p kc n", p=128), "wukv")
    wukv5 = wb["wukv"].rearrange("(kc p) (h two d) -> p kc h two d", p=128, two=2, d=128)
    for kc in range(2):
        ldw(wuv_sb[:, kc, :, :], wukv5[:, kc, :, 1, :], "wukv")
    win3 = wb["win"].rearrange("(kc p) n -> p kc n", p=128)
    ldw(wkr_sb[:, :, 0:64], win3[:, :, 640:704], "win")
    ldw(wkr_sb[:, :, 64:96], win3[:, :, 672:704], "win")
    ldw(wkr_sb[:, :, 96:128], win3[:, :, 640:672], "win")
    ldw(wa_sb[0:64, :], wb["w2"], "w2")
    ldw(wa_sb[64:128, :], wb["a2"], "a2")
    ldw(g2_sb[:], wb["g2"], "g2")
    memset("pool", Vst[:, :, :, 128:129], 1.0, (Vst_b,))

    dbg_n = [0]

    def dump(name, ap, bufs, n):
        if name not in dbg_d:
            return
        k = dbg_n[0]; dbg_n[0] += 1
        tmp = sb(f"dbgtmp{k}", [128, n])
        tb_ = P.buf()
        cp("dve", tmp[:], ap, tuple(bufs), (tb_,))
        dma("pool", dbg_d[name], tmp[:], (tb_,), (), f"dbg{k}")

    memset("pool", Ab[:], 0.0, tuple(Ab_b))
    memset("pool", Rb[:], 0.0, tuple(Rb_b))
    for _i in range(2):
        memset("pool", Khtmc[_i][:], 0.0, tuple(Khtm_b))
        memset("pool", Ubc[_i][:], 0.0, (Ub_b,))
    memset("pool", Hst[:], 0.0, (Hst_b,))
    memset("pool", Hb[:], 0.0, (Hb_b,))
    memset("pool", carry[:], 0.0, tuple(carry_b))

    tmpS = sb("tmpS", [128, T]); tmpS_b = P.buf("tmpS")

    def rsqrt_ps(ps, ps_b, addc, np_=128, n=T, dst=None, dst_b=None):
        dst = tmpA if dst is None else dst
        dst_b = tmpA_b if dst_b is None else dst_b
        act(tmpS[0:np_, 0:n], ps[0:np_, 0:n], AF.Sqrt, (ps_b,), (tmpS_b,), bias=eps_ap(addc, np_))
        P.add("dve", lambda e: e.reciprocal(out=dst[0:np_, 0:n], in_=tmpS[0:np_, 0:n]), (tmpS_b,), (dst_b,))

    def norm_mod(gm_ap_fn, sh_ap_fn, dst, dst_b, Dn=D, src=None, src_b=None, nchunk=8, eps=1e-6):
        src = hT if src is None else src
        src_b = hT_b if src_b is None else src_b
        for c in range(nchunk):
            act(sqb[:, c, :], src[:, c, :], AF.Square, (src_b[c],), (sqb_b[c],))
        ps, ps_b = bankA()
        for c in range(nchunk):
            mm(ps[:, 0:T], ones_bf[:], sqb[:, c, :], c == 0, c == nchunk - 1, (sqb_b[c], kconst_b), (ps_b,))
        rsqrt_ps(ps, ps_b, Dn * eps)
        for c in range(nchunk):
            if sh_ap_fn is None:
                stt("dve", dst[:, c, :], src[:, c, :], gm_ap_fn(c), tmpA[:], ALU.mult, ALU.mult,
                    (src_b[c], tmpA_b) + CONST, (dst_b[c],))
            else:
                k = c % 2
                tt("dve", tmpB[k][:], src[:, c, :], tmpA[:], ALU.mult, (src_b[c], tmpA_b), (tmpB_b[k],))
                act(dst[:, c, :], tmpB[k][:], AF.Identity, (tmpB_b[k],) + CONST, (dst_b[c],),
                    bias=sh_ap_fn(c), scale=gm_ap_fn(c))

    wslot = {"g": 0, "u": 0, "d": 0}

    def ffn(kg, ku, kd, cg_col):
        for j in range(NJ):
            sg_ = wslot["g"] % 2; wslot["g"] += 1
            su_ = wslot["u"] % 2; wslot["u"] += 1
            dma("sp", wgs[sg_][:], wb[kg][j].rearrange("p (kc n) -> p kc n", kc=8),
                (wb_buf[kg],), (wgs_b[sg_],), f"wgs{sg_}")
            dma("sp", wus[su_][:], wb[ku][j].rearrange("p (kc n) -> p kc n", kc=8),
                (wb_buf[ku],), (wus_b[su_],), f"wus{su_}")
            gps, gps_b = bankA()
            ups, ups_b = bankA()
            for kc in range(8):
                mm(gps[:, 0:T], wgs[sg_][:, kc, :], uT[:, kc, :], kc == 0, kc == 7,
                   (wgs_b[sg_], uT_b[kc]), (gps_b,))
            for kc in range(8):
                mm(ups[:, 0:T], wus[su_][:, kc, :], uT[:, kc, :], kc == 0, kc == 7,
                   (wus_b[su_], uT_b[kc]), (ups_b,))
            k = j % 2
            act(sgt[k][:], gps[:, 0:T], AF.Silu, (gps_b,), (sgt_b[k],))
            tt("dve", actT[:, j, :], sgt[k][:], ups[:, 0:T], ALU.mult, (sgt_b[k], ups_b), (actT_b[j],))
        for half in range(2):
            for jg in range(6):
                nj = 4 if jg < 5 else 2
                sd_ = wslot["d"] % 2; wslot["d"] += 1
                dma("sp", wds[sd_][:, 0:nj, :],
                    wb[kd][jg * 512:jg * 512 + nj * 128, half * 512:(half + 1) * 512].rearrange("(j p) n -> p j n", p=128),
                    (wb_buf[kd],), (wds_b[sd_],), f"wds{sd_}")
                for jj in range(nj):
                    j = 4 * jg + jj
                    for c4 in range(4):
                        bk, bk_b = bankB(c4)
                        mm(bk[:, 0:T], wds[sd_][:, jj, c4 * 128:(c4 + 1) * 128], actT[:, j, :], j == 0, j == NJ - 1,
                           (wds_b[sd_], actT_b[j]), (bk_b,))
            for c4 in range(4):
                c = half * 4 + c4
                bk, bk_b = bankB(c4)
                stt("dve", hT[:, c, :], bk[:, 0:T], drv[:, cg_col + c:cg_col + c + 1], hT[:, c, :], ALU.mult, ALU.add,
                    (bk_b, hT_b[c]) + CONST, (hT_b[c],))


    RWKV_STEPS = 190
    spslot = [0]
    winslot = [0]
    w128slot = [0]
    kvslot = [0]
    ptslot = [0]
    rtslot = [0]

    def load_win(key, c0, ncols):
        k = winslot[0] % 2; winslot[0] += 1
        dma("sp", wins[k][:, :, 0:ncols], wb[key][:, c0:c0 + ncols].rearrange("(kc p) n -> p kc n", p=128),
            (wb_buf[key],), (wins_b[k],), f"wins{k}")
        return wins[k], wins_b[k]

    def load_w128(key, c0):
        k = w128slot[0] % 2; w128slot[0] += 1
        dma("sp", w128[k][:], wb[key][:, c0:c0 + 128].rearrange("(kc p) n -> p kc n", p=128),
            (wb_buf[key],), (w128_b[k],), f"w128_{k}")
        return w128[k], w128_b[k]

    def proj(lhs_fn, w_b, M=128):
        ps, ps_b = bankA()
        for kc in range(8):
            mm(ps[0:M, 0:T], lhs_fn(kc), uT[:, kc, :], kc == 0, kc == 7, (w_b, uT_b[kc]), (ps_b,))
        return ps, ps_b

    def rt():
        k = rtslot[0] % 4; rtslot[0] += 1
        return rtmp[k], rtmp_b[k]

    def rope_combine(psr, psr_b, psrs, psrs_b, dst_ap, dst_bufs):
        t1, t1_b = rt()
        t2, t2_b = rt()
        tt("dve", t1[0:64, :], psr[0:64, 0:T], cosT[0:64, :], ALU.mult, (psr_b, rope_b), (t1_b,))
        tt("dve", t2[0:64, :], psrs[0:64, 0:T], sinS[0:64, :], ALU.mult, (psrs_b, rope_b), (t2_b,))
        tt("pool", dst_ap, t1[0:64, :], t2[0:64, :], ALU.add, (t1_b, t2_b), dst_bufs)

    def mla(it, t0, step=lambda: None):
        dma("sp", posi[0:64, :], pos_d[:, t0:t0 + T], (), (posi_b,), "posi")
        cp("dve", ang[0:64, :], posi[0:64, :], (posi_b,), (ang_b,))
        ts("dve", ang[0:64, :], ang[0:64, :], cst[0:64, C_INVF:C_INVF + 1], None, ALU.mult, None, (ang_b, cst_b), (ang_b,))
        C1 = 6.28125
        C2 = 2 * PI - C1

        def sin_of(dst, shift):
            ts("dve", ang2[0:64, :], ang[0:64, :], shift, 1.0 / (2 * PI), ALU.add, ALU.mult, (ang_b, rope_b), (ang2_b,))
            cp("dve", posi[0:64, :], ang2[0:64, :], (ang2_b,), (posi_b,))
            cp("dve", ang2[0:64, :], posi[0:64, :], (posi_b,), (ang2_b,))
            t1, t1_b = rt()
            ts("dve", t1[0:64, :], ang[0:64, :], shift, None, ALU.add, None, (ang_b,), (t1_b,))
            stt("dve", t1[0:64, :], ang2[0:64, :], -C1, t1[0:64, :], ALU.mult, ALU.add, (ang2_b, t1_b), (t1_b,))
            stt("dve", t1[0:64, :], ang2[0:64, :], -C2, t1[0:64, :], ALU.mult, ALU.add, (ang2_b, t1_b), (t1_b,))
            ts("dve", ang2[0:64, :], t1[0:64, :], PI, None, ALU.is_gt, None, (t1_b,), (ang2_b,))
            stt("dve", t1[0:64, :], ang2[0:64, :], -2 * PI, t1[0:64, :], ALU.mult, ALU.add, (ang2_b, t1_b), (t1_b,))
            ts("dve", ang2[0:64, :], t1[0:64, :], -PI, None, ALU.is_lt, None, (t1_b,), (ang2_b,))
            stt("dve", t1[0:64, :], ang2[0:64, :], 2 * PI, t1[0:64, :], ALU.mult, ALU.add, (ang2_b, t1_b), (t1_b,))
            ts("dve", t1[0:64, :], t1[0:64, :], PI, -PI, ALU.min, ALU.max, (t1_b,), (t1_b,))
            act(dst, t1[0:64, :], AF.Sin, (t1_b,), (rope_b,))

        sin_of(sinS[0:64, :], 0.0)
        ts("dve", sinS[0:64, :], sinS[0:64, :], cst[0:64, C_SS:C_SS + 1], None, ALU.mult, None, (rope_b, cst_b), (rope_b,))
        sin_of(cosT[0:64, :], 0.5 * PI)

        w, w_b = load_win("win", 0, 256)
        for jj in range(2):
            ps, ps_b = proj(lambda kc, jj=jj, w=w: w[:, kc, jj * 128:(jj + 1) * 128], w_b)
            cp("act", pq[:, jj, :], ps[:, 0:T], (ps_b,), (pq_b[jj],))
        w, w_b = load_win("win", 256, 256)
        ps, ps_b = proj(lambda kc, w=w: w[:, kc, 0:128], w_b)
        cp("act", pq[:, 2, :], ps[:, 0:T], (ps_b,), (pq_b[2],))
        ps, ps_b = proj(lambda kc, w=w: w[:, kc, 128:256], w_b)
        cp("act", pkv[:, 0, :], ps[:, 0:T], (ps_b,), (pkv_b[0],))
        w, w_b = load_win("win", 512, 128)
        ps, ps_b = proj(lambda kc, w=w: w[:, kc, 0:128], w_b)
        cp("act", pkv[:, 1, :], ps[:, 0:T], (ps_b,), (pkv_b[1],))
        psr, psr_b = proj(lambda kc: wkr_sb[:, kc, 0:64], wres_b, M=64)
        psrs, psrs_b = proj(lambda kc: wkr_sb[:, kc, 64:128], wres_b, M=64)
        rope_combine(psr, psr_b, psrs, psrs_b, Krst[0:64, :], (Krst_b,))
        dma("pool", Krc_d[:, t0:t0 + T], Krst[0:64, :], (Krst_b,), (Krc_buf[it],), "krst")

        norm_mod(lambda c: drv[:, DR_GQ + c:DR_GQ + c + 1], None, qn, qn_b, Dn=384, src=pq, src_b=pq_b, nchunk=3)
        norm_mod(lambda c: drv[:, DR_GKV + c:DR_GKV + c + 1], None, kvn, kvn_b, Dn=256, src=pkv, src_b=pkv_b, nchunk=2)

        for h in range(4):
            ps, ps_b = bankA()
            for kc in range(3):
                mm(ps[:, 0:T], wuq_sb[:, kc, 192 * h:192 * h + 128], qn[:, kc, :], kc == 0, kc == 2,
                   (wres_b, qn_b[kc]), (ps_b,))
            cp("act", QnT[:, h, :], ps[:, 0:T], (ps_b,), (QnT_b[h],))
            psr, psr_b = bankA()
            for kc in range(3):
                mm(psr[0:64, 0:T], wuq_sb[:, kc, 192 * h + 128:192 * h + 192], qn[:, kc, :], kc == 0, kc == 2,
                   (wres_b, qn_b[kc]), (psr_b,))
            psrs, psrs_b = bankA()
            for kc in range(3):
                mm(psrs[0:64, 0:T], wuq_rs[:, kc, h, :], qn[:, kc, :], kc == 0, kc == 2,
                   (wres_b, qn_b[kc]), (psrs_b,))
            rope_combine(psr, psr_b, psrs, psrs_b, QrT[0:64, h, :], (QrT_b[h],))
        for h in range(4):
            ps, ps_b = bankA()
            for kc in range(2):
                mm(ps[:, 0:T], wukv_sb[:, kc, 256 * h:256 * h + 128], kvn[:, kc, :], kc == 0, kc == 1,
                   (wres_b, kvn_b[kc]), (ps_b,))
            cp("act", Kst[:, h, :], ps[:, 0:T], (ps_b,), (Kst_b,))
        dma("pool", Kc_d[:, :, t0:t0 + T].rearrange("h p t -> p h t"), Kst[:], (Kst_b,),
            tuple(Kc_buf[h][it] for h in range(4)), "kst")
        for tb in range(NTB):
            ps, ps_b = bankA()
            for kc in range(2):
                mm(ps[:, 0:512], kvn[:, kc, tb * 128:(tb + 1) * 128], wuv_sb[:, kc, :, :].rearrange("p h d -> p (h d)"),
                   kc == 0, kc == 1, (wres_b, kvn_b[kc]), (ps_b,))
            cp("dve", Vst[:, :, tb, 0:128], ps[:, 0:512].rearrange("p (h d) -> p h d", h=4), (ps_b,), (Vst_b,))
        for h in range(4):
            dma("pool", Vc_d[h, it], Vst[:, h, :, :].rearrange("p t d -> p (t d)"), (Vst_b,), (Vc_buf[h][it],), f"vst{h}")

        iters = []
        for h in range(4):
            for jg in range(0, it + 1, KVG):
                njt = min(KVG, it + 1 - jg)
                for jj in range(njt):
                    for kb in range(NTB):
                        iters.append(dict(h=h, jg=jg, njt=njt, jj=jj, kb=kb, jt=jg + jj,
                                          newgrp=(jj == 0 and kb == 0)))
        for I in iters:
            I["lasth"] = False
        for n, I in enumerate(iters):
            if n + 1 == len(iters) or iters[n + 1]["h"] != I["h"]:
                I["lasth"] = True
        cur = {}

        def emit_qk(I):
            h = I["h"]
            if I["newgrp"]:
                k = kvslot[0] % 2; kvslot[0] += 1
                jg, njt = I["jg"], I["njt"]
                dma("sp", Kt[k][:, 0:njt * T], Kc_d[h][:, jg * T:(jg + njt) * T],
                    tuple(Kc_buf[h][jg:jg + njt]), (Kt_b[k],), f"kt{k}")
                dma("sp", Krt[k][0:64, 0:njt * T], Krc_d[:, jg * T:(jg + njt) * T],
                    tuple(Krc_buf[jg:jg + njt]), (Krt_b[k],), f"krt{k}")
                dma("sp", Vt[k][:, 0:njt, :], Vc_d[h, jg:jg + njt].rearrange("t p d -> p t d"),
                    tuple(Vc_buf[h][jg:jg + njt]), (Vt_b[k],), f"vt{k}")
                cur["k"] = k
            k = cur["k"]
            I["k"] = k
            diag = I["jt"] == it
            q0 = I["kb"] * 128 if diag else 0
            ko = I["jj"] * T + I["kb"] * 128
            sps, sps_b = bankB(2 + spslot[0] % 2); spslot[0] += 1
            mm(sps[:, q0:T], Kt[k][:, ko:ko + 128], QnT[:, h, q0:T], True, False, (Kt_b[k], QnT_b[h]), (sps_b,))
            mm(sps[:, q0:T], Krt[k][0:64, ko:ko + 128], QrT[0:64, h, q0:T], False, True, (Krt_b[k], QrT_b[h]), (sps_b,))
            I["sps"] = (sps, sps_b); I["diag"] = diag; I["q0"] = q0

        def emit_rest(I):
            h, k, kb, jj, jt, diag, q0 = I["h"], I["k"], I["kb"], I["jj"], I["jt"], I["diag"], I["q0"]
            sps, sps_b = I["sps"]
            p = ptslot[0] % 3; ptslot[0] += 1
            act(PT[p][:, q0:T], sps[:, q0:T], AF.Exp, (sps_b,), (PT_b[p],), scale=SCALE)
            if diag:
                memset("pool", PT[p][64:128, q0:q0 + 64], 0.0, (PT_b[p],))
            for qs in range(q0 // 128, NTB):
                first = (jt == 0 and kb == 0)
                last = (diag and kb == qs)
                ops, ops_b = bankB(qs)
                mm(ops[:, 0:129], PT[p][:, qs * 128:(qs + 1) * 128],
                   Vt[k][:, jj, kb * 129:(kb + 1) * 129], first, last, (PT_b[p], Vt_b[k]), (ops_b,))
            step()
            if I["lasth"]:
                for qs in range(NTB):
                    ops, ops_b = bankB(qs)
                    P.add("dve", lambda e, qs=qs, ops=ops: e.reciprocal(out=rec[:, qs:qs + 1], in_=ops[:, 128:129]),
                          (ops_b,), (rec_b,))
                    ts("dve", ya[:, qs, h * 128:(h + 1) * 128], ops[:, 0:128], rec[:, qs:qs + 1], None,
                       ALU.mult, None, (ops_b, rec_b), (ya_b[qs],))

        emit_qk(iters[0])
        for n, I in enumerate(iters):
            if n + 1 < len(iters):
                emit_qk(iters[n + 1])
            emit_rest(I)

        for qs in range(NTB):
            t1, t1_b = rt()
            t2, t2_b = rt()
            act(t1[:], ya[:, qs, 0:T], AF.Square, (ya_b[qs],), (t1_b,))
            act(t2[:], ya[:, qs, T:2 * T], AF.Square, (ya_b[qs],), (t2_b,))
            tt("dve", t1[:], t1[:], t2[:], ALU.add, (t1_b, t2_b), (t1_b,))
            P.add("dve", lambda e, qs=qs, t1=t1: e.reduce_sum(out=ssq[:, qs:qs + 1], in_=t1[:], axis=AX.X), (t1_b,), (ssq_b,))
        if it == NT - 1:
            dump("cos", cosT[:], (rope_b,), T)
            dump("sin", sinS[:], (rope_b,), T)
            dump("qn0", QnT[:, 0, :], (QnT_b[0],), T)
            dump("qr0", QrT[:, 0, :], (QrT_b[0],), T)
            dump("ya0", ya[:, 0, :], (ya_b[0],), 512)
            dump("kst0", Kst[:, 0, :], (Kst_b,), T)
            dump("krst", Krst[:], (Krst_b,), T)
        ts("dve", ssq[:, 0:NTB], ssq[:, 0:NTB], 1.0 / 512.0, 1e-6, ALU.mult, ALU.add, (ssq_b,), (ssq_b,))
        act(ssq[:, 0:NTB], ssq[:, 0:NTB], AF.Sqrt, (ssq_b,), (ssq_b,))
        P.add("dve", lambda e: e.reciprocal(out=ssq[:, 0:NTB], in_=ssq[:, 0:NTB]), (ssq_b,), (ssq_b,))
        for qs in range(NTB):
            stt("dve", yab[:, qs, :], ya[:, qs, :], ssq[:, qs:qs + 1], gat[:], ALU.mult, ALU.mult,
                (ya_b[qs], ssq_b, kconst_b), (yab_b[qs],))
        for cch in range(4):
            ps, ps_b = bankA()
            psbf = ps[:].bitcast(BF16)
            for qs in range(NTB):
                tr(psbf[:, qs * 128:(qs + 1) * 128], yab[:, qs, cch * 128:(cch + 1) * 128], ident_bf[:],
                   (yab_b[qs], kconst_b), (ps_b,))
            cp("act", yT[:, cch, :], psbf[:, 0:T], (ps_b,), (yT_b[cch],))


    btslot = [0]

    SSUB = int(os.environ.get("SSUB", "9"))
    RSUB = int(os.environ.get("RSUB", "9"))

    def shift_evac(ps, ps_b, ci, mixcol, omcol, dst, dst_b):
        k = btslot[0] % 2; btslot[0] += 1
        ts("dve", Bt[k][:, 8:T + 8], ps[:, 0:T], vt[:, mixcol:mixcol + 1], None, ALU.mult, None, (ps_b, vt_b), (Bt_b[k],))
        if SSUB <= 1:
            return
        cp("pool", Bt[k][:, 7:8], carry[:, ci:ci + 1], (carry_b[ci],), (Bt_b[k],))
        if SSUB <= 2:
            return
        ts("dve", dst, ps[:, 0:T], drv[:, omcol:omcol + 1], None, ALU.mult, None, (ps_b,) + CONST, (dst_b,))
        if SSUB <= 3:
            return
        tt("pool", dst, dst, Bt[k][:, 7:T + 7], ALU.add, (dst_b, Bt_b[k]), (dst_b,))
        if SSUB <= 4:
            return
        cp("pool", carry[:, ci:ci + 1], Bt[k][:, T + 7:T + 8], (Bt_b[k],), (carry_b[ci],))

    def R_(n):
        return RT[n][0][:], RT[n][1]


    RCUT = int(os.environ.get("RCUT", "9"))

    def rwkv_fill():
        for c in range(4, 8):
            memset("pool", yT[:, c, :], 0.0, (yT_b[c],))

    def rwkv(it, t0):
        w, w_b = load_win("win", 2240, 256)
        ps, ps_b = proj(lambda kc, w=w: w[:, kc, 0:128], w_b)
        if RSUB <= 1:
            cp("act", xwxa[:], ps[:, 0:T], (ps_b,), (xwxa_b,))
            rwkv_fill(); return
        shift_evac(ps, ps_b, 12, V_MIXWA, DR_OMWA, xwxa[:], xwxa_b)
        yield
        if RSUB <= 2:
            rwkv_fill(); return
        ps, ps_b = proj(lambda kc, w=w: w[:, kc, 128:256], w_b)
        shift_evac(ps, ps_b, 13, V_MIXG, DR_OMG, xg[:], xg_b)
        yield
        if RSUB <= 3:
            rwkv_fill(); return
        act(thb[0:64, :], xwxa[0:64, :], AF.Tanh, (xwxa_b,), (lor_b,))
        if RSUB <= 4:
            rwkv_fill(); return
        cp("dve", xab[64:128, :], xwxa[64:128, :], (xwxa_b,), (lor_b,))
        if RSUB <= 5:
            rwkv_fill(); return
        act(sgx[:], xg[:], AF.Sigmoid, (xg_b,), (lor_b,))
        yield
        if RCUT <= 1:
            rwkv_fill(); return
        rS, rS_b = R_("rS"); kS, kS_b = R_("kS"); vS, vS_b = R_("vS"); sg, sg_b = R_("sg"); Lc, Lc_b = R_("Lc")
        E1, E1_b = R_("E1"); E3, E3_b = R_("E3"); aG, aG_b = R_("aG"); kk, kk_b = R_("kk"); rn, rn_b = R_("rn")
        kkn, kkn_b = R_("kkn"); bb, bb_b = R_("bb"); E2, E2_b = R_("E2"); tk, tk_b = R_("tk"); kmod, kmod_b = R_("kmod")
        E4, E4_b = R_("E4"); rk, rk_b = R_("bb"); yc, yc_b = R_("kkn"); yn, yn_b = R_("kk")
        for cc in range(4):
            sl = slice(cc * 128, (cc + 1) * 128)
            for (dst, dst_b, c0, ci, mixc, omc) in ((rS, rS_b, 704, cc, V_MIXR + cc, DR_OMR + cc),
                                                   (kS, kS_b, 1216, 4 + cc, V_MIXK + cc, DR_OMK + cc),
                                                   (vS, vS_b, 1728, 8 + cc, V_MIXV + cc, DR_OMV + cc)):
                w, w_b = load_w128("win", c0 + 128 * cc)
                ps, ps_b = proj(lambda kc, w=w: w[:, kc, :], w_b)
                shift_evac(ps, ps_b, ci, mixc, omc, dst, dst_b)
                yield
            ps, ps_b = bankA()
            mm(ps[:, 0:T], wa_sb[0:64, sl], thb[0:64, :], True, True, (wres_b, lor_b), (ps_b,))
            act(sg, ps[:, 0:T], AF.Sigmoid, (ps_b,) + CONST, (sg_b,), bias=vt[:, V_W0 + cc:V_W0 + cc + 1])
            if os.environ.get("NOSCAN"):
                cp("dve", Lc, sg, (sg_b,), (Lc_b,))
            else:
                P.add("dve", lambda e: e.tensor_tensor_scan(out=Lc, data0=cst[:, C_RST:C_RST + T], data1=sg, initial=0.0,
                                                            op0=ALU.mult, op1=ALU.add), (sg_b, cst_b), (Lc_b,))
            act(E1, Lc, AF.Exp, (Lc_b,), (E1_b,), scale=-KAPPA)
            yield
            cp("pool", gL[:, cc, :], E1.rearrange("p (c t) -> p c t", t=64)[:, :, 63], (E1_b,), (gL_b,))
            tt("dve", Rb[0:64, 2 * cc, :], rS[0:64, :], E1[0:64, :], ALU.mult, (rS_b, E1_b), (Rb_b[cc],))
            tt("dve", Rb[64:128, 2 * cc + 1, :], rS[64:128, :], E1[64:128, :], ALU.mult, (rS_b, E1_b), (Rb_b[cc],))
            tt("pool", E3, Lc, sg, ALU.subtract, (Lc_b, sg_b), (E3_b,))
            act(E3, E3, AF.Exp, (E3_b,), (E3_b,), scale=-KAPPA)
            yield
            ps, ps_b = bankA()
            mm(ps[:, 0:T], wa_sb[64:128, sl], xab[64:128, :], True, True, (wres_b, lor_b), (ps_b,))
            act(aG, ps[:, 0:T], AF.Sigmoid, (ps_b,) + CONST, (aG_b,), bias=vt[:, V_A0 + cc:V_A0 + cc + 1])
            yield
            ts("dve", kk, kS, vt[:, V_KK + cc:V_KK + cc + 1], None, ALU.mult, None, (kS_b, vt_b), (kk_b,))
            act(sqk[:], kk, AF.Square, (kk_b,), (sqk_b,))
            ps, ps_b = bankA()
            mm(ps[:, 0:T], bones_bf[:], sqk[:], True, True, (sqk_b, kconst_b), (ps_b,))
            rsqrt_ps(ps, ps_b, 1e-30, dst=RT["rn"][0], dst_b=rn_b)
            tt("dve", kkn, kk, rn, ALU.mult, (kk_b, rn_b), (kkn_b,))
            yield
            stt("dve", Ab[0:64, 2 * cc, :], kkn[0:64, :], -1.0, E3[0:64, :], ALU.mult, ALU.mult, (kkn_b, E3_b), (Ab_b[cc],))
            stt("dve", Ab[64:128, 2 * cc + 1, :], kkn[64:128, :], -1.0, E3[64:128, :], ALU.mult, ALU.mult, (kkn_b, E3_b), (Ab_b[cc],))
            tt("pool", bb, kkn, aG, ALU.mult, (kkn_b, aG_b), (bb_b,))
            act(E2, Lc, AF.Exp, (Lc_b,), (E2_b,), scale=KAPPA)
            tt("dve", Btb[:, cc, :], bb, E2, ALU.mult, (bb_b, E2_b), (Btb_b[cc],))
            yield
            ts("dve", tk, aG, vt[:, V_KA + cc:V_KA + cc + 1], drv[:, DR_OMA + cc:DR_OMA + cc + 1], ALU.mult, ALU.add,
               (aG_b,) + CONST, (tk_b,))
            tt("pool", kmod, kS, tk, ALU.mult, (kS_b, tk_b), (kmod_b,))
            tt("dve", Ktb[:, cc, :], kmod, E2, ALU.mult, (kmod_b, E2_b), (Ktb_b[cc],))
            yield
            Lc3 = Lc.rearrange("p (c t) -> p c t", t=64)
            if os.environ.get("NOBC"):
                tt("pool", E4, Lc, Lc, ALU.subtract, (Lc_b,), (E4_b,))
            else:
                tt("pool", E4.rearrange("p (c t) -> p c t", t=64), Lc3[:, :, 63:64].broadcast_to([128, NCH, 64]), Lc3,
                   ALU.subtract, (Lc_b,), (E4_b,))
            act(E4, E4, AF.Exp, (E4_b,), (E4_b,), scale=-KAPPA)
            tt("dve", Bhb[:, cc, :], bb, E4, ALU.mult, (bb_b, E4_b), (Bhb_b[cc],))
            tt("pool", Khb[:, cc, :], kmod, E4, ALU.mult, (kmod_b, E4_b), (Khb_b[cc],))
            cp("act", vb[:, cc, :], vS, (vS_b,), (vb_b[cc],))
            yield
            tt("pool", rk, rS, kmod, ALU.mult, (rS_b, kmod_b), (rk_b,))
            ts("dve", sqk[:], rk, vt[:, V_RK + cc:V_RK + cc + 1], None, ALU.mult, None, (rk_b, vt_b), (sqk_b,))
            ps, ps_b = bankA()
            mm(ps[:, 0:T], bones_bf[:], sqk[:], True, True, (sqk_b, kconst_b), (ps_b,))
            tt("dve", bon[:, cc, :], ps[:, 0:T], vS, ALU.mult, (ps_b, vS_b), bon_b[cc])
            yield
            ps, ps_b = bankA()
            mm(ps[:, 0:T], g2_sb[:, sl], sgx[:], True, True, (wres_b, lor_b), (ps_b,))
            cp("act", gG[:, cc, :], ps[:, 0:T], (ps_b,), gG_b[cc])
            yield

        if RCUT <= 2:
            rwkv_fill(); return
        for tb in range(NTB):
            blk = slice(tb * 128, (tb + 1) * 128)
            for (src, src_b, dst, dst_b) in ((vb, vb_b, Vtm, Vtm_b), (Bhb, Bhb_b, Bhtm, Bhtm_b), (Khb, Khb_b, None, Khtm_b)):
                ps, ps_b = bankA()
                psbf = ps[:].bitcast(BF16)
                for cc in range(4):
                    tr(psbf[:, cc * 128:(cc + 1) * 128], src[:, cc, blk], ident_bf[:], (src_b[cc], kconst_b), (ps_b,))
                if dst is None:
                    cp("act", Khtmc[0][0:64, tb, :], psbf[0:64, 0:512], (ps_b,), (dst_b[tb],))
                    cp("act", Khtmc[1][64:128, tb, :], psbf[64:128, 0:512], (ps_b,), (dst_b[tb],))
                    yield
                else:
                    cp("act", dst[:, tb, :], psbf[:, 0:512], (ps_b,), (dst_b[tb],))
                    yield

        if RCUT <= 3:
            rwkv_fill(); return
        NSUB = int(os.environ.get("NSUB", "99"))
        for tb in range(NTB):
            blk = slice(tb * 128, (tb + 1) * 128)
            L = LL[0]
            TtA, TtA_b = L["TtA"]; AkT, AkT_b = L["AkT"]; ArbT, ArbT_b = L["ArbT"]; ArkT, ArkT_b = L["ArkT"]
            for bi in range(2):
                hs = [4 * bi + i for i in range(4)]

                def opnd(X, h):
                    if X is Ab or X is Rb:
                        return X[:, h, blk]
                    return X[:, h // 2, blk]

                def batch_mm(lfn, rfn, reads):
                    ps, ps_b = bankA()
                    for i, h in enumerate(hs):
                        mm(ps[:, i * 128:(i + 1) * 128], lfn(i, h), rfn(i, h), True, True, reads, (ps_b,))
                    return ps, ps_b

                ab_r = tuple(Ab_b) + tuple(Btb_b) + tuple(Ktb_b) + tuple(Rb_b)
                (M0, M0_b), (M0t, M0t_b), (M1, M1_b), (M1t, M1t_b) = MM
                ps, ps_b = batch_mm(lambda i, h: opnd(Ab, h), lambda i, h: opnd(Btb, h), ab_r)
                tt("dve", M0[:], ps[:].rearrange("p (i s) -> p i s", i=4), mL4[:], ALU.mult, (ps_b, kconst_b), (M0_b,))
                yield
                if NSUB <= 1:
                    continue
                ps, ps_b = batch_mm(lambda i, h: opnd(Btb, h), lambda i, h: opnd(Ab, h), ab_r)
                tt("dve", M0t[:], ps[:].rearrange("p (i s) -> p i s", i=4), mU4[:], ALU.mult, (ps_b, kconst_b), (M0t_b,))
                if NSUB <= 2:
                    continue
                tt("pool", TtA[:, 4 * bi:4 * bi + 4, :], M0t[:], id4[:], ALU.add, (M0t_b, kconst_b), (TtA_b[bi],))
                yield
                if NSUB <= 3:
                    continue
                cur, cur_b, curt, curt_b = M0, M0_b, M0t, M0t_b
                nxt, nxt_b, nxtt, nxtt_b = M1, M1_b, M1t, M1t_b
                for lvl in range(5):
                    ps, ps_b = batch_mm(lambda i, h: curt[:, i, :], lambda i, h: cur[:, i, :], (cur_b, curt_b))
                    cp("act", nxt[:], ps[:].rearrange("p (i s) -> p i s", i=4), (ps_b,), (nxt_b,))
                    yield
                    if lvl < 4:
                        ps, ps_b = batch_mm(lambda i, h: cur[:, i, :], lambda i, h: curt[:, i, :], (cur_b, curt_b))
                        cp("act", nxtt[:], ps[:].rearrange("p (i s) -> p i s", i=4), (ps_b,), (nxtt_b,))
                        yield
                    ps, ps_b = batch_mm(lambda i, h: nxt[:, i, :], lambda i, h: TtA[:, h, :], (nxt_b, TtA_b[bi]))
                    tt("dve", TtA[:, 4 * bi:4 * bi + 4, :], ps[:].rearrange("p (i s) -> p i s", i=4),
                       TtA[:, 4 * bi:4 * bi + 4, :], ALU.add, (ps_b, TtA_b[bi]), (TtA_b[bi],))
                    yield
                    cur, cur_b, curt, curt_b, nxt, nxt_b, nxtt, nxtt_b = nxt, nxt_b, nxtt, nxtt_b, cur, cur_b, curt, curt_b
                if NSUB <= 4:
                    continue
                ps, ps_b = batch_mm(lambda i, h: opnd(Ktb, h), lambda i, h: opnd(Ab, h), ab_r)
                tt("dve", AkT[:, 4 * bi:4 * bi + 4, :], ps[:].rearrange("p (i s) -> p i s", i=4), mU4[:], ALU.mult,
                   (ps_b, kconst_b), (AkT_b[bi],))
                yield
                ps, ps_b = batch_mm(lambda i, h: opnd(Btb, h), lambda i, h: opnd(Rb, h), ab_r)
                tt("dve", ArbT[:, 4 * bi:4 * bi + 4, :], ps[:].rearrange("p (i s) -> p i s", i=4), mUI4[:], ALU.mult,
                   (ps_b, kconst_b), (ArbT_b[bi],))
                yield
                ps, ps_b = batch_mm(lambda i, h: opnd(Ktb, h), lambda i, h: opnd(Rb, h), ab_r)
                tt("dve", ArkT[:, 4 * bi:4 * bi + 4, :], ps[:].rearrange("p (i s) -> p i s", i=4), mUI4[:], ALU.mult,
                   (ps_b, kconst_b), (ArkT_b[bi],))
                yield

            if RCUT <= 4:
                continue
            LLr = tuple(TtA_b) + tuple(AkT_b) + tuple(ArbT_b) + tuple(ArkT_b)
            for half in range(2):
                ch = 2 * tb + half
                tp = 64 * half
                tsl = slice(tp, tp + 64)
                csl = slice(tb * 128 + tp, tb * 128 + tp + 64)
                Ub = Ubc[half]
                Khtm = Khtmc[half]
                ps, ps_b = bankA()
                for h in range(8):
                    pb, cc = 64 * (h % 2), h // 2
                    hs_ = slice(h * 64, (h + 1) * 64)
                    mm(ps[:, hs_], Ab[:, h, blk], Hb[:, cc, :], True, False, (Ab_b[cc], Hb_b), (ps_b,))
                    mm(ps[:, hs_], AkT[:, h, :], Vtm[:, tb, hs_], False, True, LLr + (Vtm_b[tb],), (ps_b,))
                cp("act", Xb[:], ps[:], (ps_b,), (Xb_b,))
                yield
                ps, ps_b = bankA()
                for h in range(8):
                    hs_ = slice(h * 64, (h + 1) * 64)
                    mm(ps[:, hs_], TtA[:, h, :], Xb[:, hs_], True, True, LLr + (Xb_b,), (ps_b,))
                cp("dve", Ub[tsl, :], ps[tsl, :], (ps_b,), (Ub_b,))
                yield
                ps, ps_b = bankA()
                for h in range(8):
                    pb, cc = 64 * (h % 2), h // 2
                    hs_ = slice(h * 64, (h + 1) * 64)
                    o = ps[pb:pb + 64, cc * 64:(cc + 1) * 64]
                    mm(o, Hb[:, cc, :], Rb[:, h, csl], True, False, (Hb_b, Rb_b[cc]), (ps_b,))
                    mm(o, Ub[:, hs_], ArbT[:, h, tsl], False, False, LLr + (Ub_b,), (ps_b,))
                    mm(o, Vtm[:, tb, hs_], ArkT[:, h, tsl], False, True, LLr + (Vtm_b[tb],), (ps_b,))
                cp("act", yR[:, :, csl], ps[:, 0:256].rearrange("p (c t) -> p c t", c=4), (ps_b,), (yR_b,))
                yield
                ps, ps_b = bankA()
                for h in range(8):
                    pb, cc = 64 * (h % 2), h // 2
                    hs_ = slice(h * 64, (h + 1) * 64)
                    o = ps[pb:pb + 64, cc * 64:(cc + 1) * 64]
                    mm(o, Bhtm[:, tb, hs_], Ub[:, hs_], True, False, (Bhtm_b[tb], Ub_b), (ps_b,))
                    mm(o, Khtm[:, tb, hs_], Vtm[:, tb, hs_], False, True, (Khtm_b[tb], Vtm_b[tb]), (ps_b,))
                tt("dve", Hst[:], Hst[:], gL[:, :, ch:ch + 1].broadcast_to([128, 4, 64]), ALU.mult, (Hst_b, gL_b), (Hst_b,))
                tt("dve", Hst[:], Hst[:], ps[:, 0:256].rearrange("p (c v) -> p c v", c=4), ALU.add, (Hst_b, ps_b), (Hst_b,))
                cp("act", Hb[:], Hst[:], (Hst_b,), (Hb_b,))
                yield

        if RCUT <= 5:
            rwkv_fill(); return
        for cc in range(4):
            ps, ps_b = bankA()
            mm(ps[:, 0:T], bones_f[:], yR[:, cc, :], True, True, (yR_b, kconst_b), (ps_b,))
            tt("dve", yc, yR[:, cc, :], ps[:, 0:T], ALU.subtract, (yR_b, ps_b), (yc_b,))
            act(sqk[:], yc, AF.Square, (yc_b,), (sqk_b,))
            ps, ps_b = bankA()
            mm(ps[:, 0:T], bones_bf[:], sqk[:], True, True, (sqk_b, kconst_b), (ps_b,))
            rsqrt_ps(ps, ps_b, 64 * 64e-5, dst=RT["rn"][0], dst_b=rn_b)
            tt("dve", yc, yc, rn, ALU.mult, (yc_b, rn_b), (yc_b,))
            act(yn, yc, AF.Identity, (yc_b,) + CONST, (yn_b,), bias=vt[:, V_LNB + cc:V_LNB + cc + 1],
                scale=drv[:, DR_LNW8 + cc:DR_LNW8 + cc + 1])
            tt("pool", yn, yn, bon[:, cc, :], ALU.add, (yn_b,) + bon_b[cc], (yn_b,))
            tt("dve", yT[:, 4 + cc, :], yn, gG[:, cc, :], ALU.mult, (yn_b,) + gG_b[cc], (yT_b[4 + cc],))
            yield

    def outproj():
        for g in range(4):
            w, w_b = load_win("wout", g * 256, 256)
            for jj in range(2):
                c = 2 * g + jj
                ps, ps_b = bankA()
                for kc in range(8):
                    mm(ps[:, 0:T], w[:, kc, jj * 128:(jj + 1) * 128], yT[:, kc, :], kc == 0, kc == 7,
                       (w_b, yT_b[kc]), (ps_b,))
                stt("dve", hT[:, c, :], ps[:, 0:T], modT[:, GT2 + c:GT2 + c + 1], hT[:, c, :], ALU.mult, ALU.add,
                    (ps_b, hT_b[c]) + CONST, (hT_b[c],))

    xslot = [0]
    oslot = [0]
    for it in range(NT):
        t0 = it * T
        for tb in range(NTB):
            s = 0
            dma("sp", xin[s][:], x_d[t0 + tb * 128:t0 + (tb + 1) * 128, :], (), (xin_b[s],), "xin0")
            for half in range(2):
                ps, ps_b = bankA()
                for c4 in range(4):
                    c = half * 4 + c4
                    tr(ps[:, c4 * 128:(c4 + 1) * 128], xin[s][:, c * 128:(c + 1) * 128], ident_f,
                       (xin_b[s], cst_b), (ps_b,))
                cp("act", hT[:, half * 4:half * 4 + 4, tb * 128:(tb + 1) * 128],
                   ps[:].rearrange("p (c t) -> p c t", c=4), (ps_b,), tuple(hT_b[half * 4:half * 4 + 4]))

        if "ffn1" in stages:
            norm_mod(lambda c: drv[:, DR_GM1 + c:DR_GM1 + c + 1], lambda c: modT[:, SH1 + c:SH1 + c + 1], uT, uT_b)
            ffn("w1g", "w1u", "w1d", DR_CG1)

        if "mix" in stages:
            norm_mod(lambda c: drv[:, DR_GM2 + c:DR_GM2 + c + 1], lambda c: modT[:, SH2 + c:SH2 + c + 1], uT, uT_b)
            if "nomla" in stages:
                for c in range(4):
                    memset("pool", yT[:, c, :], 0.0, (yT_b[c],))
            else:
                gen = None if "norwkv" in stages else rwkv(it, t0)
                nper = max(1, -(-RWKV_STEPS // (8 * (it + 1))))

                def step(gen=gen, nper=nper):
                    if gen is None:
                        return
                    for _ in range(nper):
                        next(gen, None)

                mla(it, t0, step)
                if gen is not None:
                    for _ in gen:
                        pass
            if "norwkv" in stages:
                for c in range(4, 8):
                    memset("pool", yT[:, c, :], 0.0, (yT_b[c],))
            elif "nomla" in stages:
                for _ in rwkv(it, t0):
                    pass
            outproj()

        if "ffn2" in stages:
            norm_mod(lambda c: drv[:, DR_GM3 + c:DR_GM3 + c + 1], lambda c: modT[:, SH3 + c:SH3 + c + 1], uT, uT_b)
            ffn("w3g", "w3u", "w3d", DR_CG3)

        for c in range(8):
            act(sqb[:, c, :], hT[:, c, :], AF.Square, (hT_b[c],), (sqb_b[c],))
        ps, ps_b = bankA()
        for c in range(8):
            mm(ps[:, 0:T], ones_bf[:], sqb[:, c, :], c == 0, c == 7, (sqb_b[c], kconst_b), (ps_b,))
        rsqrt_ps(ps, ps_b, D * 1e-6)
        for c in range(8):
            stt("dve", hT[:, c, :], hT[:, c, :], drv[:, DR_GF + c:DR_GF + c + 1], tmpA[:], ALU.mult, ALU.mult,
                (hT_b[c], tmpA_b) + CONST, (hT_b[c],))
        for tb in range(NTB):
            s = 0
            for half in range(2):
                ps, ps_b = bankA()
                for c4 in range(4):
                    c = half * 4 + c4
                    tr(ps[:, c4 * 128:(c4 + 1) * 128], hT[:, c, tb * 128:(tb + 1) * 128], ident_f,
                       (hT_b[c], cst_b), (ps_b,))
                cp("act" if half == 0 else "dve", ost[s][:, half * 512:(half + 1) * 512], ps[:], (ps_b,), (ost_b[s],))
            dma("pool", out_d[t0 + tb * 128:t0 + (tb + 1) * 128, :], ost[s][:], (ost_b[s],), (), "xin0o")

    P.emit()

    if os.environ.get("KDBG"):
        print("STATS", P.stats)
    es.close()
    return nc


def _consts():
    c = np.zeros((128, NCST), np.float32)
    p = np.arange(128)
    c[:, C_ID:C_ID + 128] = np.eye(128, dtype=np.float32)
    same = (p[:, None] // 64) == (p[None, :] // 64)
    c[:, C_ML:C_ML + 128] = (same & (p[None, :] < p[:, None])).astype(np.float32)
    c[:, C_MU:C_MU + 128] = (same & (p[:, None] < p[None, :])).astype(np.float32)
    c[:, C_MUI:C_MUI + 128] = (same & (p[:, None] <= p[None, :])).astype(np.float32)
    c[:, C_BO:C_BO + 128] = same.astype(np.float32)
    c[:, C_INVF] = (10000.0 ** (-(np.arange(128) % 32).astype(np.float32) / 32.0)).astype(np.float32)
    c[:, C_SS] = np.where((p % 64) < 32, -1.0, 1.0)
    c[:, C_RST:C_RST + 512] = ((np.arange(512) % 64) != 0).astype(np.float32)[None, :]
    return c


def _vt(inp, b):
    def ch(v):
        v = np.asarray(v, np.float32).reshape(-1, 128)
        return v.T
    sm = inp["rwkv_shift_mix"][0]
    cols = [
        ch(inp["c"][b]), ch(inp["b_mod"][0]), ch(inp["ffn1_norm_g"][0]), ch(inp["mix_norm_g"][0]),
        ch(inp["ffn2_norm_g"][0]), ch(inp["final_norm_g"]), ch(inp["q_norm_g"][0]), ch(inp["kv_norm_g"][0]),
        ch(sm[0:512]), ch(sm[512:1024]), ch(sm[1024:1536]), ch(sm[1536:1664]), ch(sm[1664:1792]),
        ch(inp["rwkv_w0"][0]), ch(inp["rwkv_a0"][0]), ch(inp["rwkv_k_k"][0]), ch(inp["rwkv_k_a"][0]),
        ch(inp["rwkv_r_k"][0].reshape(-1)), ch(inp["rwkv_ln_w"][0]), ch(inp["rwkv_ln_b"][0]),
    ]
    v = np.concatenate(cols, axis=1)
    assert v.shape == (128, NV), v.shape
    return np.ascontiguousarray(v, np.float32)


def make_in_maps(inp, S, cores):
    f = lambda a: np.ascontiguousarray(np.asarray(a, np.float32))
    shared = {
        "cst": _consts(),
        "gat": np.ascontiguousarray(np.tile(f(inp["attn_out_norm_g"][0])[None, :], (128, 1))),
        "w_mod": f(inp["w_mod"][0]),
        "w1g": f(inp["ffn1_w_gate"][0]), "w1u": f(inp["ffn1_w_up"][0]), "w1d": f(inp["ffn1_w_down"][0]),
        "win": f(inp["w_in"][0]), "wuq": f(inp["w_uq"][0]), "wukv": f(inp["w_ukv"][0]),
        "w2": f(inp["rwkv_w2"][0]), "a2": f(inp["rwkv_a2"][0]), "g2": f(inp["rwkv_g2"][0]),
        "wout": f(inp["w_out"][0]),
        "w3g": f(inp["ffn2_w_gate"][0]), "w3u": f(inp["ffn2_w_up"][0]), "w3d": f(inp["ffn2_w_down"][0]),
    }
    maps = []
    for b in cores:
        m = dict(shared)
        m["x"] = f(inp["x"][b][:S])
        m["pos"] = np.ascontiguousarray(np.tile(np.asarray(inp["positions"][b][:S], np.int32)[None, :], (64, 1)))
        m["vt"] = _vt(inp, b)
        maps.append(m)
    return maps


def kernel(**inputs):
    S = inputs["x"].shape[1]
    B = inputs["x"].shape[0]
    nc = build_nc(S)
    in_maps = make_in_maps(inputs, S, list(range(B)))
    res = run_bass_kernel_spmd(nc, in_maps, core_ids=list(range(B)))
    return np.stack([np.asarray(r["out"], np.float32) for r in res.results], axis=0)
```

```python
import math
from contextlib import ExitStack

import numpy as np
import concourse.bass as bass
import concourse.mybir as mybir
from concourse.bass_utils import run_bass_kernel_spmd

F32 = mybir.dt.float32
BF16 = mybir.dt.bfloat16
I32 = mybir.dt.int32
AF = mybir.ActivationFunctionType
ALU = mybir.AluOpType
AX = mybir.AxisListType

D = 1024
DFF = 2816
T = 256
NTB = T // 128
NJ = DFF // 128
KAPPA = math.exp(-0.5)
SCALE = 192.0 ** -0.5
PI = math.pi

V_CT, V_BM, V_G1, V_G2, V_G3, V_GF, V_GQ, V_GKV = 0, 8, 80, 88, 96, 104, 112, 115
V_MIXR, V_MIXK, V_MIXV, V_MIXWA, V_MIXG = 117, 121, 125, 129, 130
V_W0, V_A0, V_KK, V_KA, V_RK, V_LNW, V_LNB = 131, 135, 139, 143, 147, 151, 155
NV = 159
C_ID, C_ML, C_MU, C_MUI, C_BO, C_INVF, C_SS, C_RST = 0, 128, 256, 384, 512, 640, 641, 642
NCST = 642 + 512


import os
POOL2DVE = bool(os.environ.get('POOL2DVE'))


class Buf:
    __slots__ = ("name", "w", "wd", "r", "rd")

    def __init__(self, name):
        self.name = name
        self.w = {}
        self.wd = []
        self.r = {}
        self.rd = []


class Op:
    __slots__ = ("idx", "eng", "fn", "deps", "is_dma", "dkey", "dval", "signal", "sigval", "raw")

    def __init__(self, idx, eng, fn, is_dma, dkey, dval):
        self.idx = idx
        self.eng = eng
        self.fn = fn
        self.deps = []
        self.is_dma = is_dma
        self.dkey = dkey
        self.dval = dval
        self.signal = False
        self.sigval = 0


class Prog:
    ENGS = ("pe", "act", "dve", "pool", "sp")

    def __init__(self, nc, es):
        self.nc = nc
        self.es = es
        self.ops = []
        self.eng_ops = {e: [] for e in self.ENGS}
        self.dcount = {}
        self.nbuf = 0

    def buf(self, name=None):
        self.nbuf += 1
        return Buf(name or f"b{self.nbuf}")

    def bufs(self, n, name=None):
        return [self.buf(f"{name}{i}") for i in range(n)]

    def add(self, eng, fn, reads=(), writes=(), dkey=None):
        is_dma = dkey is not None
        if eng == "pool" and not is_dma and POOL2DVE:
            eng = "dve"
        dval = 0
        if is_dma:
            self.dcount[dkey] = self.dcount.get(dkey, 0) + 1
            dval = 16 * self.dcount[dkey]
        op = Op(len(self.ops), eng, fn, is_dma, dkey, dval)
        deps = {}

        def dep(o, raw):
            if o is op:
                return
            if (not o.is_dma) and (not is_dma) and o.eng == eng:
                if eng == "pe":
                    return
            deps[o.idx] = o

        for b in reads:
            for o in b.w.values():
                dep(o, True)
            for o in b.wd:
                dep(o, True)
        for b in writes:
            for o in b.r.values():
                dep(o, False)
            for o in b.rd:
                dep(o, False)
            for o in b.w.values():
                dep(o, False)
            for o in b.wd:
                dep(o, False)
        for b in reads:
            if is_dma:
                b.rd.append(op)
            else:
                b.r[eng] = op
        for b in writes:
            if b.r or b.rd:
                keep_r = None
                b.w = {}
                b.wd = []
                b.r = {}
                b.rd = []
            if is_dma:
                b.wd.append(op)
            else:
                b.w[eng] = op
        op.deps = list(deps.values())
        self.ops.append(op)
        self.eng_ops[eng].append(op)
        return op

    def emit(self):
        nc = self.nc
        es = self.es
        for op in self.ops:
            for d in op.deps:
                if not d.is_dma:
                    d.signal = True
        cnt = {e: 0 for e in self.ENGS}
        for op in self.ops:
            if (not op.is_dma) and op.signal:
                cnt[op.eng] += 1
                op.sigval = cnt[op.eng]
        self.stats = dict(cnt=dict(cnt), nops={e: len(v) for e, v in self.eng_ops.items()}, dmax=max(self.dcount.values()) * 16)
        esem = {e: es.enter_context(nc.semaphore("s_" + e)) for e in self.ENGS}
        dsem = {k: es.enter_context(nc.semaphore("d_" + k)) for k in self.dcount}
        block = es.enter_context(nc.Block())

        def gen(engname):
            def body(e):
                known = {}
                for op in self.eng_ops[engname]:
                    for d in op.deps:
                        if d.is_dma:
                            key, sem, val = "d_" + d.dkey, dsem[d.dkey], d.dval
                        else:
                            key, sem, val = "e_" + d.eng, esem[d.eng], d.sigval
                        if known.get(key, 0) >= val:
                            continue
                        e.wait_ge(sem, val)
                        known[key] = val
                    ins = op.fn(e)
                    if op.is_dma:
                        ins.then_inc(dsem[op.dkey], 16)
                    elif op.signal:
                        ins.then_inc(esem[op.eng], 1)
                if engname == "sp":
                    for k, n in self.dcount.items():
                        e.wait_ge(dsem[k], 16 * n)

            return body

        block.tensor(gen("pe"))
        block.vector(gen("dve"))
        block.scalar(gen("act"))
        block.gpsimd(gen("pool"))
        block.sync(gen("sp"))


def build_nc(S, stages=("ffn1", "mix", "ffn2"), dbg=()):
    NT = S // T
    nc = bass.Bass("TRN2", target_bir_lowering=False)
    es = ExitStack()
    P = Prog(nc, es)

    def din(name, shape, dt=F32):
        return nc.dram_tensor(name, list(shape), dt, kind="ExternalInput").ap()

    x_d = din("x", [S, D])
    pos_d = din("pos", [64, S], I32)
    vt_d = din("vt", [128, NV])
    cst_d = din("cst", [128, NCST])
    gat_d = din("gat", [128, 512])
    wmod_d = din("w_mod", [D, 9 * D])
    wsrc = {
        "w1g": din("w1g", [D, DFF]), "w1u": din("w1u", [D, DFF]), "w1d": din("w1d", [DFF, D]),
        "win": din("win", [D, 2496]), "wuq": din("wuq", [384, 768]), "wukv": din("wukv", [256, 1024]),
        "w2": din("w2", [64, 512]), "a2": din("a2", [64, 512]), "g2": din("g2", [128, 512]),
        "wout": din("wout", [D, D]),
        "w3g": din("w3g", [D, DFF]), "w3u": din("w3u", [D, DFF]), "w3d": din("w3d", [DFF, D]),
    }
    out_d = nc.dram_tensor("out", [S, D], F32, kind="ExternalOutput").ap()
    dbg_d = {}
    for name, shape in dbg:
        dbg_d[name] = nc.dram_tensor("dbg_" + name, list(shape), F32, kind="ExternalOutput").ap()

    TILED = ("w1g", "w1u", "w3g", "w3u")
    wb = {k: nc.dram_tensor(k + "_bf", ([NJ, 128, 8 * 128] if k in TILED else list(v.shape)), BF16).ap()
          for k, v in wsrc.items()}
    wb_buf = {k: P.buf("wb_" + k) for k in wsrc}
    Kc_d = nc.dram_tensor("Kc", [4, 128, S], BF16).ap()
    Vc_d = nc.dram_tensor("Vc", [4, NT, 128, NTB * 129], BF16).ap()
    Kc_buf = [[P.buf(f"Kc{h}_{i}") for i in range(NT)] for h in range(4)]
    Vc_buf = [[P.buf(f"Vc{h}_{i}") for i in range(NT)] for h in range(4)]

    def sb(name, shape, dt=F32):
        return es.enter_context(nc.sbuf_tensor("sb_" + name, list(shape), dt))

    vt = sb("vt", [128, NV]); vt_b = P.buf("vt")
    cst = sb("cst", [128, NCST]); cst_b = P.buf("cst")
    drv = sb("drv", [128, 96]); drv_b = P.buf("drv")
    modT = sb("modT", [128, 72]); mod_b = P.buf("modT")
    cact = sb("cact", [128, 8]); cact_b = P.buf("cact")
    ident_bf = sb("ident_bf", [128, 128], BF16)
    ones_bf = sb("ones_bf", [128, 128], BF16)
    bones_bf = sb("bones_bf", [128, 128], BF16)
    bones_f = sb("bones_f", [128, 128])
    mL4 = sb("mL4", [128, 4, 128], BF16); mU4 = sb("mU4", [128, 4, 128], BF16); mUI4 = sb("mUI4", [128, 4, 128], BF16)
    id4 = sb("id4", [128, 4, 128], BF16)
    gat = sb("gat", [128, 512])
    kconst_b = P.buf("kconst")

    hT = sb("hT", [128, 8, T]); hT_b = P.bufs(8, "hT")
    uT = sb("uT", [128, 8, T], BF16); uT_b = P.bufs(8, "uT")
    sqb = sb("sqb", [128, 8, T], BF16); sqb_b = P.bufs(8, "sqb")
    tmpA = sb("tmpA", [128, T]); tmpA_b = P.buf("tmpA")
    tmpB = [sb(f"tmpB{i}", [128, T]) for i in range(2)]; tmpB_b = P.bufs(2, "tmpB")
    arena = sb("arena", [128, NJ * T], BF16)
    actT = arena[:].rearrange("p (j t) -> p j t", j=NJ); actT_b = P.bufs(NJ, "actT")
    sgt = [sb(f"sgt{i}", [128, T]) for i in range(2)]; sgt_b = P.bufs(2, "sgt")
    xin = [sb(f"xin{i}", [128, D]) for i in range(1)]; xin_b = P.bufs(1, "xin")
    ost = xin; ost_b = xin_b
    wgs = [sb(f"wgs{i}", [128, 8, 128], BF16) for i in range(2)]; wgs_b = P.bufs(2, "wgs")
    wus = [sb(f"wus{i}", [128, 8, 128], BF16) for i in range(2)]; wus_b = P.bufs(2, "wus")
    wds = [sb(f"wds{i}", [128, 4, 512], BF16) for i in range(2)]; wds_b = P.bufs(2, "wds")
    wmods = [xin[0][:].rearrange("p (kc n) -> p kc n", kc=8)]; wmods_b = [xin_b[0]]


    Krc_d = nc.dram_tensor("Krc", [64, S], BF16).ap()
    Krc_buf = [P.buf(f"Krc{i}") for i in range(NT)]
    wuq_sb = sb("wuq_sb", [128, 3, 768], BF16)
    wuq_rs = sb("wuq_rs", [128, 3, 4, 64], BF16)
    wukv_sb = sb("wukv_sb", [128, 2, 1024], BF16)
    wuv_sb = sb("wuv_sb", [128, 2, 4, 128], BF16)
    wkr_sb = sb("wkr_sb", [128, 8, 128], BF16)
    wa_sb = sb("wa_sb", [128, 512], BF16)
    g2_sb = sb("g2_sb", [128, 512], BF16)
    wres_b = P.buf("wres")
    wins = [sb(f"wins{i}", [128, 8, 256], BF16) for i in range(2)]; wins_b = P.bufs(2, "wins")
    w128 = [sb(f"w128_{i}", [128, 8, 128], BF16) for i in range(2)]; w128_b = P.bufs(2, "w128")
    pq = sb("pq", [128, 3, T]); pq_b = P.bufs(3, "pq")
    qn = sb("qn", [128, 3, T], BF16); qn_b = P.bufs(3, "qn")
    pkv = sb("pkv", [128, 2, T]); pkv_b = P.bufs(2, "pkv")
    kvn = sb("kvn", [128, 2, T], BF16); kvn_b = P.bufs(2, "kvn")
    QnT = sb("QnT", [128, 4, T], BF16); QnT_b = P.bufs(4, "QnT")
    QrT = sb("QrT", [128, 4, T], BF16); QrT_b = P.bufs(4, "QrT")
    Kst = sb("Kst", [128, 4, T], BF16); Kst_b = P.buf("Kst")
    Krst = sb("Krst", [128, T], BF16); Krst_b = P.buf("Krst")
    Vst = sb("Vst", [128, 4, NTB, 129], BF16); Vst_b = P.buf("Vst")
    KVG = 4
    Kt = [sb(f"Kt{i}", [128, KVG * T], BF16) for i in range(2)]; Kt_b = P.bufs(2, "Kt")
    Krt = [sb(f"Krt{i}", [128, KVG * T], BF16) for i in range(2)]; Krt_b = P.bufs(2, "Krt")
    Vt = [sb(f"Vt{i}", [128, KVG, NTB * 129], BF16) for i in range(2)]; Vt_b = P.bufs(2, "Vt")
    PT = [sb(f"PT{i}", [128, T], BF16) for i in range(3)]; PT_b = P.bufs(3, "PT")
    posi = sb("posi", [128, T], I32); posi_b = P.buf("posi")
    ang = sb("ang", [128, T]); ang_b = P.buf("ang")
    ang2 = sb("ang2", [128, T]); ang2_b = P.buf("ang2")
    cosT = sb("cosT", [128, T]); sinS = sb("sinS", [128, T]); rope_b = P.buf("rope")
    rtmp = [sb(f"rtmp{i}", [128, T]) for i in range(4)]; rtmp_b = P.bufs(4, "rtmp")
    ya = sb("ya", [128, NTB, 512]); ya_b = P.bufs(NTB, "ya")
    yab = sb("yab", [128, NTB, 512], BF16); yab_b = P.bufs(NTB, "yab")
    yT = sb("yT", [128, 8, T], BF16); yT_b = P.bufs(8, "yT")
    ssq = sb("ssq", [128, 8]); ssq_b = P.buf("ssq")
    rec = sb("rec", [128, 8]); rec_b = P.buf("rec")


    NCH = T // 64
    Hst = sb("Hst", [128, 4, 64]); Hst_b = P.buf("Hst")
    Hb = sb("Hb", [128, 4, 64], BF16); Hb_b = P.buf("Hb")
    carry = sb("carry", [128, 14]); carry_b = P.bufs(14, "carry")
    Bt = [sb(f"Bt{i}", [128, T + 8]) for i in range(2)]; Bt_b = P.bufs(2, "Bt")
    xwxa = sb("xwxa", [128, T]); xwxa_b = P.buf("xwxa")
    xg = sb("xg", [128, T]); xg_b = P.buf("xg")
    thb = sb("thb", [128, T], BF16); xab = sb("xab", [128, T], BF16); sgx = sb("sgx", [128, T], BF16)
    lor_b = P.buf("lor")
    RT = {}
    for _n in ("rS", "kS", "vS", "sg", "Lc", "E1", "E3", "aG", "kk", "rn", "kkn", "bb", "E2", "tk", "kmod", "E4"):
        RT[_n] = (sb("r_" + _n, [128, T]), P.buf("r_" + _n))
    sqk = sb("sqk", [128, T], BF16); sqk_b = P.buf("sqk")
    Ab = sb("Ab", [128, 8, T], BF16); Ab_b = P.bufs(4, "Ab")
    Btb = sb("Btb", [128, 4, T], BF16); Btb_b = P.bufs(4, "Btb")
    Ktb = sb("Ktb", [128, 4, T], BF16); Ktb_b = P.bufs(4, "Ktb")
    Rb = sb("Rb", [128, 8, T], BF16); Rb_b = P.bufs(4, "Rb")
    vb = sqb[:, 0:4, :]; vb_b = sqb_b[0:4]
    Bhb = sqb[:, 4:8, :]; Bhb_b = sqb_b[4:8]
    Khb = sb("Khb", [128, 4, T], BF16); Khb_b = P.bufs(4, "Khb")
    Vtm = sb("Vtm", [128, NTB, 512], BF16); Vtm_b = P.bufs(NTB, "Vtm")
    Bhtm = sb("Bhtm", [128, NTB, 512], BF16); Bhtm_b = P.bufs(NTB, "Bhtm")
    Khtmc = [sb(f"Khtm{i}", [128, NTB, 512], BF16) for i in range(2)]; Khtm_b = P.bufs(NTB, "Khtm")
    bon = arena[:, 0:8 * T].bitcast(F32).rearrange("p (c t) -> p c t", c=4)
    gG = arena[:, 8 * T:16 * T].bitcast(F32).rearrange("p (c t) -> p c t", c=4)
    bon_b = [(actT_b[2 * c], actT_b[2 * c + 1]) for c in range(4)]
    gG_b = [(actT_b[8 + 2 * c], actT_b[8 + 2 * c + 1]) for c in range(4)]
    yR = sb("yR", [128, 4, T]); yR_b = P.buf("yR")
    gL = sb("gL", [128, 4, NCH]); gL_b = P.buf("gL")
    LL = []
    for _i in range(1):
        LL.append({n: (sb(f"{n}{_i}", [128, 8, 128], BF16), P.bufs(2, f"{n}{_i}")) for n in ("TtA", "AkT", "ArbT", "ArkT")})
    MM = [(sb(f"Mm{i}", [128, 4, 128], BF16), P.buf(f"Mm{i}")) for i in range(4)]
    Xb = sb("Xb", [128, 512], BF16); Xb_b = P.buf("Xb")
    Ubc = [sb(f"Ub{i}", [128, 512], BF16) for i in range(2)]; Ub_b = P.buf("Ub")

    pbank = [es.enter_context(nc.psum_tensor(f"pb{i}", [128, 512], F32)) for i in range(8)]
    pbank_b = P.bufs(8, "pb")
    rrA = [0]

    def bankA():
        k = rrA[0] % 4
        rrA[0] += 1
        return pbank[k], pbank_b[k]

    def bankB(i):
        return pbank[4 + i], pbank_b[4 + i]

    def dma(q, out, in_, reads, writes, key):
        return P.add(q, lambda e: e.dma_start(out=out, in_=in_), reads, writes, dkey=key)

    def mm(out, lhsT, rhs, start, stop, reads, writes):
        return P.add("pe", lambda e: e.matmul(out, lhsT=lhsT, rhs=rhs, start=start, stop=stop), reads, writes)

    def tr(out, in_, ident, reads, writes):
        return P.add("pe", lambda e: e.transpose(out, in_, ident), reads, writes)

    def act(out, in_, func, reads, writes, bias=0.0, scale=1.0, accum_out=None):
        if accum_out is None:
            return P.add("act", lambda e: e.activation(out=out, in_=in_, func=func, bias=bias, scale=scale), reads, writes)
        return P.add("act", lambda e: e.activation(out=out, in_=in_, func=func, bias=bias, scale=scale, accum_out=accum_out), reads, writes)

    def tt(eng, out, in0, in1, op, reads, writes):
        return P.add(eng, lambda e: e.tensor_tensor(out=out, in0=in0, in1=in1, op=op), reads, writes)

    def ts(eng, out, in0, s1, s2, op0, op1, reads, writes):
        if op1 is None:
            return P.add(eng, lambda e: e.tensor_scalar(out=out, in0=in0, scalar1=s1, scalar2=None, op0=op0), reads, writes)
        return P.add(eng, lambda e: e.tensor_scalar(out=out, in0=in0, scalar1=s1, scalar2=s2, op0=op0, op1=op1), reads, writes)

    def stt(eng, out, in0, scalar, in1, op0, op1, reads, writes):
        return P.add(eng, lambda e: e.scalar_tensor_tensor(out=out, in0=in0, scalar=scalar, in1=in1, op0=op0, op1=op1), reads, writes)

    def cp(eng, out, in_, reads, writes):
        if eng == "act":
            return P.add(eng, lambda e: e.activation(out=out, in_=in_, func=AF.Identity), reads, writes)
        return P.add(eng, lambda e: e.tensor_copy(out=out, in_=in_), reads, writes)

    def memset(eng, ap, val, writes):
        return P.add(eng, lambda e: e.memset(ap, val), (), writes)

    for k, src in wsrc.items():
        if k in TILED:
            for j in range(NJ):
                dma("pool", wb[k][j].rearrange("p (kc n) -> p kc n", kc=8),
                    src[:, j * 128:(j + 1) * 128].rearrange("(kc p) n -> p kc n", p=128), (), (wb_buf[k],), "wc_" + k)
            continue
        rows = src.shape[0]
        step = 256 if rows >= 256 else rows
        for r0 in range(0, rows, step):
            r1 = min(rows, r0 + step)
            dma("pool", wb[k][r0:r1, :], src[r0:r1, :], (), (wb_buf[k],), "wc_" + k)

    dma("sp", vt[:], vt_d, (), (vt_b,), "c_vt")
    dma("sp", cst[:], cst_d, (), (cst_b,), "c_cst")
    dma("sp", gat[:], gat_d, (), (kconst_b,), "c_gat")

    cp("dve", ident_bf[:], cst[:, C_ID:C_ID + 128], (cst_b,), (kconst_b,))
    memset("dve", ones_bf[:], 1.0, (kconst_b,))
    cp("dve", bones_bf[:], cst[:, C_BO:C_BO + 128], (cst_b,), (kconst_b,))
    ts("dve", bones_f[:], cst[:, C_BO:C_BO + 128], 1.0 / 64.0, None, ALU.mult, None, (cst_b,), (kconst_b,))
    for i in range(4):
        cp("dve", mL4[:, i, :], cst[:, C_ML:C_ML + 128], (cst_b,), (kconst_b,))
        cp("dve", mU4[:, i, :], cst[:, C_MU:C_MU + 128], (cst_b,), (kconst_b,))
        cp("dve", mUI4[:, i, :], cst[:, C_MUI:C_MUI + 128], (cst_b,), (kconst_b,))
        cp("dve", id4[:, i, :], cst[:, C_ID:C_ID + 128], (cst_b,), (kconst_b,))
    ident_f = cst[:, C_ID:C_ID + 128]

    act(cact[:], vt[:, V_CT:V_CT + 8], AF.Silu, (vt_b,), (cact_b,))
    mps, mps_b = bankA()
    for j in range(72):
        s = 0
        dma("sp", wmods[s], wmod_d[:, j * 128:(j + 1) * 128].rearrange("(kc p) n -> p kc n", p=128),
            (), (wmods_b[s],), "xin0")
        for kc in range(8):
            mm(mps[:, j:j + 1], wmods[s][:, kc, :], cact[:, kc:kc + 1],
               kc == 0, kc == 7, (wmods_b[s], cact_b), (mps_b,))
    tt("dve", modT[:], mps[:, 0:72], vt[:, V_BM:V_BM + 72], ALU.add, (mps_b, vt_b), (mod_b,))

    DR_GM1, DR_GM2, DR_GM3, DR_CG1, DR_CG3, DR_GF, DR_GQ, DR_GKV = 0, 8, 16, 24, 32, 40, 48, 51
    DR_OMR, DR_OMK, DR_OMV, DR_OMWA, DR_OMG, DR_OMA, DR_LNW8 = 53, 57, 61, 65, 66, 67, 71
    SH1, SC1, GT1, SH2, SC2, GT2, SH3, SC3, GT3 = [8 * i for i in range(9)]

    def dts(out_c, n, in_ap, s1, s2, op0, op1):
        ts("dve", drv[:, out_c:out_c + n], in_ap, s1, s2, op0, op1, (mod_b, vt_b, drv_b), (drv_b,))

    for (dst, sc, g) in ((DR_GM1, SC1, V_G1), (DR_GM2, SC2, V_G2), (DR_GM3, SC3, V_G3)):
        dts(dst, 8, modT[:, sc:sc + 8], 1.0, 32.0, ALU.add, ALU.mult)
        tt("dve", drv[:, dst:dst + 8], drv[:, dst:dst + 8], vt[:, g:g + 8], ALU.mult, (drv_b, vt_b), (drv_b,))
    dts(DR_CG1, 8, modT[:, GT1:GT1 + 8], 0.5, None, ALU.mult, None)
    dts(DR_CG3, 8, modT[:, GT3:GT3 + 8], 0.5, None, ALU.mult, None)
    dts(DR_GF, 8, vt[:, V_GF:V_GF + 8], 32.0, None, ALU.mult, None)
    dts(DR_GQ, 3, vt[:, V_GQ:V_GQ + 3], math.sqrt(384.0), None, ALU.mult, None)
    dts(DR_GKV, 2, vt[:, V_GKV:V_GKV + 2], 16.0, None, ALU.mult, None)
    dts(DR_OMR, 14, vt[:, V_MIXR:V_MIXR + 14], -1.0, 1.0, ALU.mult, ALU.add)
    dts(DR_OMA, 4, vt[:, V_KA:V_KA + 4], -1.0, 1.0, ALU.mult, ALU.add)
    dts(DR_LNW8, 4, vt[:, V_LNW:V_LNW + 4], 8.0, None, ALU.mult, None)
    DR_NW0, DR_NA0 = 75, 79
    dts(DR_NW0, 4, vt[:, V_W0:V_W0 + 4], -1.0, None, ALU.mult, None)
    dts(DR_NA0, 4, vt[:, V_A0:V_A0 + 4], -1.0, None, ALU.mult, None)
    CONST = (drv_b, mod_b, vt_b, kconst_b, cst_b)
    epst = sb("epst", [128, 16])
    eps_cols = {}

    def eps_ap(val, np_=128):
        if val not in eps_cols:
            k = len(eps_cols)
            eps_cols[val] = k
            memset("dve", epst[:, k:k + 1], float(val), (kconst_b,))
        k = eps_cols[val]
        return epst[0:np_, k:k + 1]

    for _v in (D * 1e-6, 384 * 1e-6, 256 * 1e-6, 1e-30, 64 * 64e-5, -PI, 0.0, 1.0):
        eps_ap(_v)


    def ldw(out, in_, k):
        dma("sp", out, in_, (wb_buf[k],), (wres_b,), "wres")
    ldw(wuq_sb[:], wb["wuq"].rearrange("(kc p) n -> p kc n", p=128), "wuq")
    wuq4 = wb["wuq"].rearrange("(kc p) (h d) -> p kc h d", p=128, d=192)
    for kc in range(3):
        ldw(wuq_rs[:, kc, :, 0:32], wuq4[:, kc, :, 160:192], "wuq")
        ldw(wuq_rs[:, kc, :, 32:64], wuq4[:, kc, :, 128:160], "wuq")
    ldw(wukv_sb[:], wb["wukv"].rearrange("(kc p) n -> p kc n", p=128), "wukv")
    wukv5 = wb["wukv"].rearrange("(kc p) (h two d) -> p kc h two d", p=128, two=2, d=128)
    for kc in range(2):
        ldw(wuv_sb[:, kc, :, :], wukv5[:, kc, :, 1, :], "wukv")
    win3 = wb["win"].rearrange("(kc p) n -> p kc n", p=128)
    ldw(wkr_sb[:, :, 0:64], win3[:, :, 640:704], "win")
    ldw(wkr_sb[:, :, 64:96], win3[:, :, 672:704], "win")
    ldw(wkr_sb[:, :, 96:128], win3[:, :, 640:672], "win")
    ldw(wa_sb[0:64, :], wb["w2"], "w2")
    ldw(wa_sb[64:128, :], wb["a2"], "a2")
    ldw(g2_sb[:], wb["g2"], "g2")
    memset("pool", Vst[:, :, :, 128:129], 1.0, (Vst_b,))

    dbg_n = [0]

    def dump(name, ap, bufs, n):
        if name not in dbg_d:
            return
        k = dbg_n[0]; dbg_n[0] += 1
        tmp = sb(f"dbgtmp{k}", [128, n])
        tb_ = P.buf()
        cp("dve", tmp[:], ap, tuple(bufs), (tb_,))
        dma("pool", dbg_d[name], tmp[:], (tb_,), (), f"dbg{k}")

    memset("pool", Ab[:], 0.0, tuple(Ab_b))
    memset("pool", Rb[:], 0.0, tuple(Rb_b))
    for _i in range(2):
        memset("pool", Khtmc[_i][:], 0.0, tuple(Khtm_b))
        memset("pool", Ubc[_i][:], 0.0, (Ub_b,))
    memset("pool", Hst[:], 0.0, (Hst_b,))
    memset("pool", Hb[:], 0.0, (Hb_b,))
    memset("pool", carry[:], 0.0, tuple(carry_b))

    tmpS = sb("tmpS", [128, T]); tmpS_b = P.buf("tmpS")

    def rsqrt_ps(ps, ps_b, addc, np_=128, n=T, dst=None, dst_b=None):
        dst = tmpA if dst is None else dst
        dst_b = tmpA_b if dst_b is None else dst_b
        act(tmpS[0:np_, 0:n], ps[0:np_, 0:n], AF.Ln, (ps_b,), (tmpS_b,), bias=eps_ap(addc, np_))
        act(dst[0:np_, 0:n], tmpS[0:np_, 0:n], AF.Exp, (tmpS_b,), (dst_b,), scale=-0.5)

    def norm_mod(gm_ap_fn, sh_ap_fn, dst, dst_b, Dn=D, src=None, src_b=None, nchunk=8, eps=1e-6):
        src = hT if src is None else src
        src_b = hT_b if src_b is None else src_b
        for c in range(nchunk):
            act(sqb[:, c, :], src[:, c, :], AF.Square, (src_b[c],), (sqb_b[c],))
        ps, ps_b = bankA()
        for c in range(nchunk):
            mm(ps[:, 0:T], ones_bf[:], sqb[:, c, :], c == 0, c == nchunk - 1, (sqb_b[c], kconst_b), (ps_b,))
        rsqrt_ps(ps, ps_b, Dn * eps)
        for c in range(nchunk):
            if sh_ap_fn is None:
                stt("dve", dst[:, c, :], src[:, c, :], gm_ap_fn(c), tmpA[:], ALU.mult, ALU.mult,
                    (src_b[c], tmpA_b) + CONST, (dst_b[c],))
            else:
                k = c % 2
                tt("dve", tmpB[k][:], src[:, c, :], tmpA[:], ALU.mult, (src_b[c], tmpA_b), (tmpB_b[k],))
                act(dst[:, c, :], tmpB[k][:], AF.Identity, (tmpB_b[k],) + CONST, (dst_b[c],),
                    bias=sh_ap_fn(c), scale=gm_ap_fn(c))

    wslot = {"g": 0, "u": 0, "d": 0}

    def ffn(kg, ku, kd, cg_col):
        for j in range(NJ):
            sg_ = wslot["g"] % 2; wslot["g"] += 1
            su_ = wslot["u"] % 2; wslot["u"] += 1
            dma("sp", wgs[sg_][:], wb[kg][j].rearrange("p (kc n) -> p kc n", kc=8),
                (wb_buf[kg],), (wgs_b[sg_],), f"wgs{sg_}")
            dma("sp", wus[su_][:], wb[ku][j].rearrange("p (kc n) -> p kc n", kc=8),
                (wb_buf[ku],), (wus_b[su_],), f"wus{su_}")
            gps, gps_b = bankA()
            ups, ups_b = bankA()
            for kc in range(8):
                mm(gps[:, 0:T], wgs[sg_][:, kc, :], uT[:, kc, :], kc == 0, kc == 7,
                   (wgs_b[sg_], uT_b[kc]), (gps_b,))
            for kc in range(8):
                mm(ups[:, 0:T], wus[su_][:, kc, :], uT[:, kc, :], kc == 0, kc == 7,
                   (wus_b[su_], uT_b[kc]), (ups_b,))
            k = j % 2
            act(sgt[k][:], gps[:, 0:T], AF.Silu, (gps_b,), (sgt_b[k],))
            tt("dve", actT[:, j, :], sgt[k][:], ups[:, 0:T], ALU.mult, (sgt_b[k], ups_b), (actT_b[j],))
        for half in range(2):
            for jg in range(6):
                nj = 4 if jg < 5 else 2
                sd_ = wslot["d"] % 2; wslot["d"] += 1
                dma("sp", wds[sd_][:, 0:nj, :],
                    wb[kd][jg * 512:jg * 512 + nj * 128, half * 512:(half + 1) * 512].rearrange("(j p) n -> p j n", p=128),
                    (wb_buf[kd],), (wds_b[sd_],), f"wds{sd_}")
                for jj in range(nj):
                    j = 4 * jg + jj
                    for c4 in range(4):
                        bk, bk_b = bankB(c4)
                        mm(bk[:, 0:T], wds[sd_][:, jj, c4 * 128:(c4 + 1) * 128], actT[:, j, :], j == 0, j == NJ - 1,
                           (wds_b[sd_], actT_b[j]), (bk_b,))
            for c4 in range(4):
                c = half * 4 + c4
                bk, bk_b = bankB(c4)
                stt("dve", hT[:, c, :], bk[:, 0:T], drv[:, cg_col + c:cg_col + c + 1], hT[:, c, :], ALU.mult, ALU.add,
                    (bk_b, hT_b[c]) + CONST, (hT_b[c],))


    RWKV_STEPS = 190
    spslot = [0]
    winslot = [0]
    w128slot = [0]
    kvslot = [0]
    ptslot = [0]
    rtslot = [0]

    def load_win(key, c0, ncols):
        k = winslot[0] % 2; winslot[0] += 1
        dma("sp", wins[k][:, :, 0:ncols], wb[key][:, c0:c0 + ncols].rearrange("(kc p) n -> p kc n", p=128),
            (wb_buf[key],), (wins_b[k],), f"wins{k}")
        return wins[k], wins_b[k]

    def load_w128(key, c0):
        k = w128slot[0] % 2; w128slot[0] += 1
        dma("sp", w128[k][:], wb[key][:, c0:c0 + 128].rearrange("(kc p) n -> p kc n", p=128),
            (wb_buf[key],), (w128_b[k],), f"w128_{k}")
        return w128[k], w128_b[k]

    def proj(lhs_fn, w_b, M=128):
        ps, ps_b = bankA()
        for kc in range(8):
            mm(ps[0:M, 0:T], lhs_fn(kc), uT[:, kc, :], kc == 0, kc == 7, (w_b, uT_b[kc]), (ps_b,))
        return ps, ps_b

    def rt():
        k = rtslot[0] % 4; rtslot[0] += 1
        return rtmp[k], rtmp_b[k]

    def rope_combine(psr, psr_b, psrs, psrs_b, dst_ap, dst_bufs):
        t1, t1_b = rt()
        t2, t2_b = rt()
        tt("dve", t1[0:64, :], psr[0:64, 0:T], cosT[0:64, :], ALU.mult, (psr_b, rope_b), (t1_b,))
        tt("dve", t2[0:64, :], psrs[0:64, 0:T], sinS[0:64, :], ALU.mult, (psrs_b, rope_b), (t2_b,))
        tt("pool", dst_ap, t1[0:64, :], t2[0:64, :], ALU.add, (t1_b, t2_b), dst_bufs)

    def mla(it, t0, step=lambda: None):
        dma("sp", posi[0:64, :], pos_d[:, t0:t0 + T], (), (posi_b,), "posi")
        cp("dve", ang[0:64, :], posi[0:64, :], (posi_b,), (ang_b,))
        ts("dve", ang[0:64, :], ang[0:64, :], cst[0:64, C_INVF:C_INVF + 1], None, ALU.mult, None, (ang_b, cst_b), (ang_b,))
        C1 = 6.28125
        C2 = 2 * PI - C1

        def sin_of(dst, shift):
            ts("dve", ang2[0:64, :], ang[0:64, :], shift, 1.0 / (2 * PI), ALU.add, ALU.mult, (ang_b, rope_b), (ang2_b,))
            cp("dve", posi[0:64, :], ang2[0:64, :], (ang2_b,), (posi_b,))
            cp("dve", ang2[0:64, :], posi[0:64, :], (posi_b,), (ang2_b,))
            t1, t1_b = rt()
            ts("dve", t1[0:64, :], ang[0:64, :], shift, None, ALU.add, None, (ang_b,), (t1_b,))
            stt("dve", t1[0:64, :], ang2[0:64, :], -C1, t1[0:64, :], ALU.mult, ALU.add, (ang2_b, t1_b), (t1_b,))
            stt("dve", t1[0:64, :], ang2[0:64, :], -C2, t1[0:64, :], ALU.mult, ALU.add, (ang2_b, t1_b), (t1_b,))
            ts("dve", ang2[0:64, :], t1[0:64, :], PI, None, ALU.is_gt, None, (t1_b,), (ang2_b,))
            stt("dve", t1[0:64, :], ang2[0:64, :], -2 * PI, t1[0:64, :], ALU.mult, ALU.add, (ang2_b, t1_b), (t1_b,))
            ts("dve", ang2[0:64, :], t1[0:64, :], -PI, None, ALU.is_lt, None, (t1_b,), (ang2_b,))
            stt("dve", t1[0:64, :], ang2[0:64, :], 2 * PI, t1[0:64, :], ALU.mult, ALU.add, (ang2_b, t1_b), (t1_b,))
            ts("dve", t1[0:64, :], t1[0:64, :], PI, -PI, ALU.min, ALU.max, (t1_b,), (t1_b,))
            act(dst, t1[0:64, :], AF.Sin, (t1_b,), (rope_b,))

        sin_of(sinS[0:64, :], 0.0)
        ts("dve", sinS[0:64, :], sinS[0:64, :], cst[0:64, C_SS:C_SS + 1], None, ALU.mult, None, (rope_b, cst_b), (rope_b,))
        sin_of(cosT[0:64, :], 0.5 * PI)

        w, w_b = load_win("win", 0, 256)
        for jj in range(2):
            ps, ps_b = proj(lambda kc, jj=jj, w=w: w[:, kc, jj * 128:(jj + 1) * 128], w_b)
            cp("act", pq[:, jj, :], ps[:, 0:T], (ps_b,), (pq_b[jj],))
        w, w_b = load_win("win", 256, 256)
        ps, ps_b = proj(lambda kc, w=w: w[:, kc, 0:128], w_b)
        cp("act", pq[:, 2, :], ps[:, 0:T], (ps_b,), (pq_b[2],))
        ps, ps_b = proj(lambda kc, w=w: w[:, kc, 128:256], w_b)
        cp("act", pkv[:, 0, :], ps[:, 0:T], (ps_b,), (pkv_b[0],))
        w, w_b = load_win("win", 512, 128)
        ps, ps_b = proj(lambda kc, w=w: w[:, kc, 0:128], w_b)
        cp("act", pkv[:, 1, :], ps[:, 0:T], (ps_b,), (pkv_b[1],))
        psr, psr_b = proj(lambda kc: wkr_sb[:, kc, 0:64], wres_b, M=64)
        psrs, psrs_b = proj(lambda kc: wkr_sb[:, kc, 64:128], wres_b, M=64)
        rope_combine(psr, psr_b, psrs, psrs_b, Krst[0:64, :], (Krst_b,))
        dma("pool", Krc_d[:, t0:t0 + T], Krst[0:64, :], (Krst_b,), (Krc_buf[it],), "krst")

        norm_mod(lambda c: drv[:, DR_GQ + c:DR_GQ + c + 1], None, qn, qn_b, Dn=384, src=pq, src_b=pq_b, nchunk=3)
        norm_mod(lambda c: drv[:, DR_GKV + c:DR_GKV + c + 1], None, kvn, kvn_b, Dn=256, src=pkv, src_b=pkv_b, nchunk=2)

        for h in range(4):
            ps, ps_b = bankA()
            for kc in range(3):
                mm(ps[:, 0:T], wuq_sb[:, kc, 192 * h:192 * h + 128], qn[:, kc, :], kc == 0, kc == 2,
                   (wres_b, qn_b[kc]), (ps_b,))
            cp("act", QnT[:, h, :], ps[:, 0:T], (ps_b,), (QnT_b[h],))
            psr, psr_b = bankA()
            for kc in range(3):
                mm(psr[0:64, 0:T], wuq_sb[:, kc, 192 * h + 128:192 * h + 192], qn[:, kc, :], kc == 0, kc == 2,
                   (wres_b, qn_b[kc]), (psr_b,))
            psrs, psrs_b = bankA()
            for kc in range(3):
                mm(psrs[0:64, 0:T], wuq_rs[:, kc, h, :], qn[:, kc, :], kc == 0, kc == 2,
                   (wres_b, qn_b[kc]), (psrs_b,))
            rope_combine(psr, psr_b, psrs, psrs_b, QrT[0:64, h, :], (QrT_b[h],))
        for h in range(4):
            ps, ps_b = bankA()
            for kc in range(2):
                mm(ps[:, 0:T], wukv_sb[:, kc, 256 * h:256 * h + 128], kvn[:, kc, :], kc == 0, kc == 1,
                   (wres_b, kvn_b[kc]), (ps_b,))
            cp("act", Kst[:, h, :], ps[:, 0:T], (ps_b,), (Kst_b,))
        dma("pool", Kc_d[:, :, t0:t0 + T].rearrange("h p t -> p h t"), Kst[:], (Kst_b,),
            tuple(Kc_buf[h][it] for h in range(4)), "kst")
        for tb in range(NTB):
            ps, ps_b = bankA()
            for kc in range(2):
                mm(ps[:, 0:512], kvn[:, kc, tb * 128:(tb + 1) * 128], wuv_sb[:, kc, :, :].rearrange("p h d -> p (h d)"),
                   kc == 0, kc == 1, (wres_b, kvn_b[kc]), (ps_b,))
            cp("dve", Vst[:, :, tb, 0:128], ps[:, 0:512].rearrange("p (h d) -> p h d", h=4), (ps_b,), (Vst_b,))
        for h in range(4):
            dma("pool", Vc_d[h, it], Vst[:, h, :, :].rearrange("p t d -> p (t d)"), (Vst_b,), (Vc_buf[h][it],), f"vst{h}")

        iters = []
        for h in range(4):
            for jg in range(0, it + 1, KVG):
                njt = min(KVG, it + 1 - jg)
                for jj in range(njt):
                    for kb in range(NTB):
                        iters.append(dict(h=h, jg=jg, njt=njt, jj=jj, kb=kb, jt=jg + jj,
                                          newgrp=(jj == 0 and kb == 0)))
        for I in iters:
            I["lasth"] = False
        for n, I in enumerate(iters):
            if n + 1 == len(iters) or iters[n + 1]["h"] != I["h"]:
                I["lasth"] = True
        cur = {}

        def emit_qk(I):
            h = I["h"]
            if I["newgrp"]:
                k = kvslot[0] % 2; kvslot[0] += 1
                jg, njt = I["jg"], I["njt"]
                dma("sp", Kt[k][:, 0:njt * T], Kc_d[h][:, jg * T:(jg + njt) * T],
                    tuple(Kc_buf[h][jg:jg + njt]), (Kt_b[k],), f"kt{k}")
                dma("sp", Krt[k][0:64, 0:njt * T], Krc_d[:, jg * T:(jg + njt) * T],
                    tuple(Krc_buf[jg:jg + njt]), (Krt_b[k],), f"krt{k}")
                dma("sp", Vt[k][:, 0:njt, :], Vc_d[h, jg:jg + njt].rearrange("t p d -> p t d"),
                    tuple(Vc_buf[h][jg:jg + njt]), (Vt_b[k],), f"vt{k}")
                cur["k"] = k
            k = cur["k"]
            I["k"] = k
            diag = I["jt"] == it
            q0 = I["kb"] * 128 if diag else 0
            ko = I["jj"] * T + I["kb"] * 128
            sps, sps_b = bankB(2 + spslot[0] % 2); spslot[0] += 1
            mm(sps[:, q0:T], Kt[k][:, ko:ko + 128], QnT[:, h, q0:T], True, False, (Kt_b[k], QnT_b[h]), (sps_b,))
            mm(sps[:, q0:T], Krt[k][0:64, ko:ko + 128], QrT[0:64, h, q0:T], False, True, (Krt_b[k], QrT_b[h]), (sps_b,))
            I["sps"] = (sps, sps_b); I["diag"] = diag; I["q0"] = q0

        def emit_rest(I):
            h, k, kb, jj, jt, diag, q0 = I["h"], I["k"], I["kb"], I["jj"], I["jt"], I["diag"], I["q0"]
            sps, sps_b = I["sps"]
            p = ptslot[0] % 3; ptslot[0] += 1
            act(PT[p][:, q0:T], sps[:, q0:T], AF.Exp, (sps_b,), (PT_b[p],), scale=SCALE)
            if diag:
                memset("pool", PT[p][64:128, q0:q0 + 64], 0.0, (PT_b[p],))
            for qs in range(q0 // 128, NTB):
                first = (jt == 0 and kb == 0)
                last = (diag and kb == qs)
                ops, ops_b = bankB(qs)
                mm(ops[:, 0:129], PT[p][:, qs * 128:(qs + 1) * 128],
                   Vt[k][:, jj, kb * 129:(kb + 1) * 129], first, last, (PT_b[p], Vt_b[k]), (ops_b,))
            step()
            if I["lasth"]:
                for qs in range(NTB):
                    ops, ops_b = bankB(qs)
                    P.add("dve", lambda e, qs=qs, ops=ops: e.reciprocal(out=rec[:, qs:qs + 1], in_=ops[:, 128:129]),
                          (ops_b,), (rec_b,))
                    ts("dve", ya[:, qs, h * 128:(h + 1) * 128], ops[:, 0:128], rec[:, qs:qs + 1], None,
                       ALU.mult, None, (ops_b, rec_b), (ya_b[qs],))

        emit_qk(iters[0])
        for n, I in enumerate(iters):
            if n + 1 < len(iters):
                emit_qk(iters[n + 1])
            emit_rest(I)

        for qs in range(NTB):
            t1, t1_b = rt()
            t2, t2_b = rt()
            act(t1[:], ya[:, qs, 0:T], AF.Square, (ya_b[qs],), (t1_b,))
            act(t2[:], ya[:, qs, T:2 * T], AF.Square, (ya_b[qs],), (t2_b,))
            tt("dve", t1[:], t1[:], t2[:], ALU.add, (t1_b, t2_b), (t1_b,))
            P.add("dve", lambda e, qs=qs, t1=t1: e.reduce_sum(out=ssq[:, qs:qs + 1], in_=t1[:], axis=AX.X), (t1_b,), (ssq_b,))
        if it == NT - 1:
            dump("cos", cosT[:], (rope_b,), T)
            dump("sin", sinS[:], (rope_b,), T)
            dump("qn0", QnT[:, 0, :], (QnT_b[0],), T)
            dump("qr0", QrT[:, 0, :], (QrT_b[0],), T)
            dump("ya0", ya[:, 0, :], (ya_b[0],), 512)
            dump("kst0", Kst[:, 0, :], (Kst_b,), T)
            dump("krst", Krst[:], (Krst_b,), T)
        ts("dve", ssq[:, 0:NTB], ssq[:, 0:NTB], 1.0 / 512.0, 1e-6, ALU.mult, ALU.add, (ssq_b,), (ssq_b,))
        act(ssq[:, 0:NTB], ssq[:, 0:NTB], AF.Ln, (ssq_b,), (ssq_b,))
        act(ssq[:, 0:NTB], ssq[:, 0:NTB], AF.Exp, (ssq_b,), (ssq_b,), scale=-0.5)
        for qs in range(NTB):
            stt("dve", yab[:, qs, :], ya[:, qs, :], ssq[:, qs:qs + 1], gat[:], ALU.mult, ALU.mult,
                (ya_b[qs], ssq_b, kconst_b), (yab_b[qs],))
        for cch in range(4):
            ps, ps_b = bankA()
            psbf = ps[:].bitcast(BF16)
            for qs in range(NTB):
                tr(psbf[:, qs * 128:(qs + 1) * 128], yab[:, qs, cch * 128:(cch + 1) * 128], ident_bf[:],
                   (yab_b[qs], kconst_b), (ps_b,))
            cp("act", yT[:, cch, :], psbf[:, 0:T], (ps_b,), (yT_b[cch],))


    btslot = [0]

    SSUB = int(os.environ.get("SSUB", "9"))
    RSUB = int(os.environ.get("RSUB", "9"))

    def shift_evac(ps, ps_b, ci, mixcol, omcol, dst, dst_b):
        k = btslot[0] % 2; btslot[0] += 1
        ts("dve", Bt[k][:, 8:T + 8], ps[:, 0:T], vt[:, mixcol:mixcol + 1], None, ALU.mult, None, (ps_b, vt_b), (Bt_b[k],))
        if SSUB <= 1:
            return
        cp("pool", Bt[k][:, 7:8], carry[:, ci:ci + 1], (carry_b[ci],), (Bt_b[k],))
        if SSUB <= 2:
            return
        ts("dve", dst, ps[:, 0:T], drv[:, omcol:omcol + 1], None, ALU.mult, None, (ps_b,) + CONST, (dst_b,))
        if SSUB <= 3:
            return
        tt("pool", dst, dst, Bt[k][:, 7:T + 7], ALU.add, (dst_b, Bt_b[k]), (dst_b,))
        if SSUB <= 4:
            return
        cp("pool", carry[:, ci:ci + 1], Bt[k][:, T + 7:T + 8], (Bt_b[k],), (carry_b[ci],))

    def R_(n):
        return RT[n][0][:], RT[n][1]


    RCUT = int(os.environ.get("RCUT", "9"))

    def rwkv_fill():
        for c in range(4, 8):
            memset("pool", yT[:, c, :], 0.0, (yT_b[c],))

    def rwkv(it, t0):
        w, w_b = load_win("win", 2240, 256)
        ps, ps_b = proj(lambda kc, w=w: w[:, kc, 0:128], w_b)
        if RSUB <= 1:
            cp("act", xwxa[:], ps[:, 0:T], (ps_b,), (xwxa_b,))
            rwkv_fill(); return
        shift_evac(ps, ps_b, 12, V_MIXWA, DR_OMWA, xwxa[:], xwxa_b)
        yield
        if RSUB <= 2:
            rwkv_fill(); return
        ps, ps_b = proj(lambda kc, w=w: w[:, kc, 128:256], w_b)
        shift_evac(ps, ps_b, 13, V_MIXG, DR_OMG, xg[:], xg_b)
        yield
        if RSUB <= 3:
            rwkv_fill(); return
        act(thb[0:64, :], xwxa[0:64, :], AF.Tanh, (xwxa_b,), (lor_b,))
        if RSUB <= 4:
            rwkv_fill(); return
        cp("dve", xab[64:128, :], xwxa[64:128, :], (xwxa_b,), (lor_b,))
        if RSUB <= 5:
            rwkv_fill(); return
        act(xg[:], xg[:], AF.Exp, (xg_b,), (xg_b,), scale=-1.0)
        act(xg[:], xg[:], AF.Ln, (xg_b,) + CONST, (xg_b,), bias=eps_ap(1.0))
        act(sgx[:], xg[:], AF.Exp, (xg_b,), (lor_b,), scale=-1.0)
        yield
        if RCUT <= 1:
            rwkv_fill(); return
        rS, rS_b = R_("rS"); kS, kS_b = R_("kS"); vS, vS_b = R_("vS"); sg, sg_b = R_("sg"); Lc, Lc_b = R_("Lc")
        E1, E1_b = R_("E1"); E3, E3_b = R_("E3"); aG, aG_b = R_("aG"); kk, kk_b = R_("kk"); rn, rn_b = R_("rn")
        kkn, kkn_b = R_("kkn"); bb, bb_b = R_("bb"); E2, E2_b = R_("E2"); tk, tk_b = R_("tk"); kmod, kmod_b = R_("kmod")
        E4, E4_b = R_("E4"); rk, rk_b = R_("bb"); yc, yc_b = R_("kkn"); yn, yn_b = R_("kk")
        for cc in range(4):
            sl = slice(cc * 128, (cc + 1) * 128)
            for (dst, dst_b, c0, ci, mixc, omc) in ((rS, rS_b, 704, cc, V_MIXR + cc, DR_OMR + cc),
                                                   (kS, kS_b, 1216, 4 + cc, V_MIXK + cc, DR_OMK + cc),
                                                   (vS, vS_b, 1728, 8 + cc, V_MIXV + cc, DR_OMV + cc)):
                w, w_b = load_w128("win", c0 + 128 * cc)
                ps, ps_b = proj(lambda kc, w=w: w[:, kc, :], w_b)
                shift_evac(ps, ps_b, ci, mixc, omc, dst, dst_b)
                yield
            ps, ps_b = bankA()
            mm(ps[:, 0:T], wa_sb[0:64, sl], thb[0:64, :], True, True, (wres_b, lor_b), (ps_b,))
            act(sg, ps[:, 0:T], AF.Exp, (ps_b,) + CONST, (sg_b,), bias=drv[:, DR_NW0 + cc:DR_NW0 + cc + 1], scale=-1.0)
            act(sg, sg, AF.Ln, (sg_b,) + CONST, (sg_b,), bias=eps_ap(1.0))
            act(sg, sg, AF.Exp, (sg_b,), (sg_b,), scale=-1.0)
            if os.environ.get("NOSCAN"):
                cp("dve", Lc, sg, (sg_b,), (Lc_b,))
            else:
                P.add("dve", lambda e: e.tensor_tensor_scan(out=Lc, data0=cst[:, C_RST:C_RST + T], data1=sg, initial=0.0,
                                                            op0=ALU.mult, op1=ALU.add), (sg_b, cst_b), (Lc_b,))
            act(E1, Lc, AF.Exp, (Lc_b,), (E1_b,), scale=-KAPPA)
            yield
            cp("pool", gL[:, cc, :], E1.rearrange("p (c t) -> p c t", t=64)[:, :, 63], (E1_b,), (gL_b,))
            tt("dve", Rb[0:64, 2 * cc, :], rS[0:64, :], E1[0:64, :], ALU.mult, (rS_b, E1_b), (Rb_b[cc],))
            tt("dve", Rb[64:128, 2 * cc + 1, :], rS[64:128, :], E1[64:128, :], ALU.mult, (rS_b, E1_b), (Rb_b[cc],))
            tt("pool", E3, Lc, sg, ALU.subtract, (Lc_b, sg_b), (E3_b,))
            act(E3, E3, AF.Exp, (E3_b,), (E3_b,), scale=-KAPPA)
            yield
            ps, ps_b = bankA()
            mm(ps[:, 0:T], wa_sb[64:128, sl], xab[64:128, :], True, True, (wres_b, lor_b), (ps_b,))
            act(aG, ps[:, 0:T], AF.Exp, (ps_b,) + CONST, (aG_b,), bias=drv[:, DR_NA0 + cc:DR_NA0 + cc + 1], scale=-1.0)
            act(aG, aG, AF.Ln, (aG_b,) + CONST, (aG_b,), bias=eps_ap(1.0))
            act(aG, aG, AF.Exp, (aG_b,), (aG_b,), scale=-1.0)
            yield
            ts("dve", kk, kS, vt[:, V_KK + cc:V_KK + cc + 1], None, ALU.mult, None, (kS_b, vt_b), (kk_b,))
            act(sqk[:], kk, AF.Square, (kk_b,), (sqk_b,))
            ps, ps_b = bankA()
            mm(ps[:, 0:T], bones_bf[:], sqk[:], True, True, (sqk_b, kconst_b), (ps_b,))
            rsqrt_ps(ps, ps_b, 1e-30, dst=RT["rn"][0], dst_b=rn_b)
            tt("dve", kkn, kk, rn, ALU.mult, (kk_b, rn_b), (kkn_b,))
            yield
            stt("dve", Ab[0:64, 2 * cc, :], kkn[0:64, :], -1.0, E3[0:64, :], ALU.mult, ALU.mult, (kkn_b, E3_b), (Ab_b[cc],))
            stt("dve", Ab[64:128, 2 * cc + 1, :], kkn[64:128, :], -1.0, E3[64:128, :], ALU.mult, ALU.mult, (kkn_b, E3_b), (Ab_b[cc],))
            tt("pool", bb, kkn, aG, ALU.mult, (kkn_b, aG_b), (bb_b,))
            act(E2, Lc, AF.Exp, (Lc_b,), (E2_b,), scale=KAPPA)
            tt("dve", Btb[:, cc, :], bb, E2, ALU.mult, (bb_b, E2_b), (Btb_b[cc],))
            yield
            ts("dve", tk, aG, vt[:, V_KA + cc:V_KA + cc + 1], drv[:, DR_OMA + cc:DR_OMA + cc + 1], ALU.mult, ALU.add,
               (aG_b,) + CONST, (tk_b,))
            tt("pool", kmod, kS, tk, ALU.mult, (kS_b, tk_b), (kmod_b,))
            tt("dve", Ktb[:, cc, :], kmod, E2, ALU.mult, (kmod_b, E2_b), (Ktb_b[cc],))
            yield
            Lc3 = Lc.rearrange("p (c t) -> p c t", t=64)
            if os.environ.get("NOBC"):
                tt("pool", E4, Lc, Lc, ALU.subtract, (Lc_b,), (E4_b,))
            else:
                tt("pool", E4.rearrange("p (c t) -> p c t", t=64), Lc3[:, :, 63:64].broadcast_to([128, NCH, 64]), Lc3,
                   ALU.subtract, (Lc_b,), (E4_b,))
            act(E4, E4, AF.Exp, (E4_b,), (E4_b,), scale=-KAPPA)
            tt("dve", Bhb[:, cc, :], bb, E4, ALU.mult, (bb_b, E4_b), (Bhb_b[cc],))
            tt("pool", Khb[:, cc, :], kmod, E4, ALU.mult, (kmod_b, E4_b), (Khb_b[cc],))
            cp("act", vb[:, cc, :], vS, (vS_b,), (vb_b[cc],))
            yield
            tt("pool", rk, rS, kmod, ALU.mult, (rS_b, kmod_b), (rk_b,))
            ts("dve", sqk[:], rk, vt[:, V_RK + cc:V_RK + cc + 1], None, ALU.mult, None, (rk_b, vt_b), (sqk_b,))
            ps, ps_b = bankA()
            mm(ps[:, 0:T], bones_bf[:], sqk[:], True, True, (sqk_b, kconst_b), (ps_b,))
            tt("dve", bon[:, cc, :], ps[:, 0:T], vS, ALU.mult, (ps_b, vS_b), bon_b[cc])
            yield
            ps, ps_b = bankA()
            mm(ps[:, 0:T], g2_sb[:, sl], sgx[:], True, True, (wres_b, lor_b), (ps_b,))
            cp("act", gG[:, cc, :], ps[:, 0:T], (ps_b,), gG_b[cc])
            yield

        if RCUT <= 2:
            rwkv_fill(); return
        for tb in range(NTB):
            blk = slice(tb * 128, (tb + 1) * 128)
            for (src, src_b, dst, dst_b) in ((vb, vb_b, Vtm, Vtm_b), (Bhb, Bhb_b, Bhtm, Bhtm_b), (Khb, Khb_b, None, Khtm_b)):
                ps, ps_b = bankA()
                psbf = ps[:].bitcast(BF16)
                for cc in range(4):
                    tr(psbf[:, cc * 128:(cc + 1) * 128], src[:, cc, blk], ident_bf[:], (src_b[cc], kconst_b), (ps_b,))
                if dst is None:
                    cp("act", Khtmc[0][0:64, tb, :], psbf[0:64, 0:512], (ps_b,), (dst_b[tb],))
                    cp("act", Khtmc[1][64:128, tb, :], psbf[64:128, 0:512], (ps_b,), (dst_b[tb],))
                    yield
                else:
                    cp("act", dst[:, tb, :], psbf[:, 0:512], (ps_b,), (dst_b[tb],))
                    yield

        if RCUT <= 3:
            rwkv_fill(); return
        NSUB = int(os.environ.get("NSUB", "99"))
        for tb in range(NTB):
            blk = slice(tb * 128, (tb + 1) * 128)
            L = LL[0]
            TtA, TtA_b = L["TtA"]; AkT, AkT_b = L["AkT"]; ArbT, ArbT_b = L["ArbT"]; ArkT, ArkT_b = L["ArkT"]
            for bi in range(2):
                hs = [4 * bi + i for i in range(4)]

                def opnd(X, h):
                    if X is Ab or X is Rb:
                        return X[:, h, blk]
                    return X[:, h // 2, blk]

                def batch_mm(lfn, rfn, reads):
                    ps, ps_b = bankA()
                    for i, h in enumerate(hs):
                        mm(ps[:, i * 128:(i + 1) * 128], lfn(i, h), rfn(i, h), True, True, reads, (ps_b,))
                    return ps, ps_b

                ab_r = tuple(Ab_b) + tuple(Btb_b) + tuple(Ktb_b) + tuple(Rb_b)
                (M0, M0_b), (M0t, M0t_b), (M1, M1_b), (M1t, M1t_b) = MM
                ps, ps_b = batch_mm(lambda i, h: opnd(Ab, h), lambda i, h: opnd(Btb, h), ab_r)
                tt("dve", M0[:], ps[:].rearrange("p (i s) -> p i s", i=4), mL4[:], ALU.mult, (ps_b, kconst_b), (M0_b,))
                yield
                if NSUB <= 1:
                    continue
                ps, ps_b = batch_mm(lambda i, h: opnd(Btb, h), lambda i, h: opnd(Ab, h), ab_r)
                tt("dve", M0t[:], ps[:].rearrange("p (i s) -> p i s", i=4), mU4[:], ALU.mult, (ps_b, kconst_b), (M0t_b,))
                if NSUB <= 2:
                    continue
                tt("pool", TtA[:, 4 * bi:4 * bi + 4, :], M0t[:], id4[:], ALU.add, (M0t_b, kconst_b), (TtA_b[bi],))
                yield
                if NSUB <= 3:
                    continue
                cur, cur_b, curt, curt_b = M0, M0_b, M0t, M0t_b
                nxt, nxt_b, nxtt, nxtt_b = M1, M1_b, M1t, M1t_b
                for lvl in range(5):
                    ps, ps_b = batch_mm(lambda i, h: curt[:, i, :], lambda i, h: cur[:, i, :], (cur_b, curt_b))
                    cp("act", nxt[:], ps[:].rearrange("p (i s) -> p i s", i=4), (ps_b,), (nxt_b,))
                    yield
                    if lvl < 4:
                        ps, ps_b = batch_mm(lambda i, h: cur[:, i, :], lambda i, h: curt[:, i, :], (cur_b, curt_b))
                        cp("act", nxtt[:], ps[:].rearrange("p (i s) -> p i s", i=4), (ps_b,), (nxtt_b,))
                        yield
                    ps, ps_b = batch_mm(lambda i, h: nxt[:, i, :], lambda i, h: TtA[:, h, :], (nxt_b, TtA_b[bi]))
                    tt("dve", TtA[:, 4 * bi:4 * bi + 4, :], ps[:].rearrange("p (i s) -> p i s", i=4),
                       TtA[:, 4 * bi:4 * bi + 4, :], ALU.add, (ps_b, TtA_b[bi]), (TtA_b[bi],))
                    yield
                    cur, cur_b, curt, curt_b, nxt, nxt_b, nxtt, nxtt_b = nxt, nxt_b, nxtt, nxtt_b, cur, cur_b, curt, curt_b
                if NSUB <= 4:
                    continue
                ps, ps_b = batch_mm(lambda i, h: opnd(Ktb, h), lambda i, h: opnd(Ab, h), ab_r)
                tt("dve", AkT[:, 4 * bi:4 * bi + 4, :], ps[:].rearrange("p (i s) -> p i s", i=4), mU4[:], ALU.mult,
                   (ps_b, kconst_b), (AkT_b[bi],))
                yield
                ps, ps_b = batch_mm(lambda i, h: opnd(Btb, h), lambda i, h: opnd(Rb, h), ab_r)
                tt("dve", ArbT[:, 4 * bi:4 * bi + 4, :], ps[:].rearrange("p (i s) -> p i s", i=4), mUI4[:], ALU.mult,
                   (ps_b, kconst_b), (ArbT_b[bi],))
                yield
                ps, ps_b = batch_mm(lambda i, h: opnd(Ktb, h), lambda i, h: opnd(Rb, h), ab_r)
                tt("dve", ArkT[:, 4 * bi:4 * bi + 4, :], ps[:].rearrange("p (i s) -> p i s", i=4), mUI4[:], ALU.mult,
                   (ps_b, kconst_b), (ArkT_b[bi],))
                yield

            if RCUT <= 4:
                continue
            LLr = tuple(TtA_b) + tuple(AkT_b) + tuple(ArbT_b) + tuple(ArkT_b)
            for half in range(2):
                ch = 2 * tb + half
                tp = 64 * half
                tsl = slice(tp, tp + 64)
                csl = slice(tb * 128 + tp, tb * 128 + tp + 64)
                Ub = Ubc[half]
                Khtm = Khtmc[half]
                ps, ps_b = bankA()
                for h in range(8):
                    pb, cc = 64 * (h % 2), h // 2
                    hs_ = slice(h * 64, (h + 1) * 64)
                    mm(ps[:, hs_], Ab[:, h, blk], Hb[:, cc, :], True, False, (Ab_b[cc], Hb_b), (ps_b,))
                    mm(ps[:, hs_], AkT[:, h, :], Vtm[:, tb, hs_], False, True, LLr + (Vtm_b[tb],), (ps_b,))
                cp("act", Xb[:], ps[:], (ps_b,), (Xb_b,))
                yield
                ps, ps_b = bankA()
                for h in range(8):
                    hs_ = slice(h * 64, (h + 1) * 64)
                    mm(ps[:, hs_], TtA[:, h, :], Xb[:, hs_], True, True, LLr + (Xb_b,), (ps_b,))
                cp("dve", Ub[tsl, :], ps[tsl, :], (ps_b,), (Ub_b,))
                yield
                ps, ps_b = bankA()
                for h in range(8):
                    pb, cc = 64 * (h % 2), h // 2
                    hs_ = slice(h * 64, (h + 1) * 64)
                    o = ps[pb:pb + 64, cc * 64:(cc + 1) * 64]
                    mm(o, Hb[:, cc, :], Rb[:, h, csl], True, False, (Hb_b, Rb_b[cc]), (ps_b,))
                    mm(o, Ub[:, hs_], ArbT[:, h, tsl], False, False, LLr + (Ub_b,), (ps_b,))
                    mm(o, Vtm[:, tb, hs_], ArkT[:, h, tsl], False, True, LLr + (Vtm_b[tb],), (ps_b,))
                cp("act", yR[:, :, csl], ps[:, 0:256].rearrange("p (c t) -> p c t", c=4), (ps_b,), (yR_b,))
                yield
                ps, ps_b = bankA()
                for h in range(8):
                    pb, cc = 64 * (h % 2), h // 2
                    hs_ = slice(h * 64, (h + 1) * 64)
                    o = ps[pb:pb + 64, cc * 64:(cc + 1) * 64]
                    mm(o, Bhtm[:, tb, hs_], Ub[:, hs_], True, False, (Bhtm_b[tb], Ub_b), (ps_b,))
                    mm(o, Khtm[:, tb, hs_], Vtm[:, tb, hs_], False, True, (Khtm_b[tb], Vtm_b[tb]), (ps_b,))
                tt("dve", Hst[:], Hst[:], gL[:, :, ch:ch + 1].broadcast_to([128, 4, 64]), ALU.mult, (Hst_b, gL_b), (Hst_b,))
                tt("dve", Hst[:], Hst[:], ps[:, 0:256].rearrange("p (c v) -> p c v", c=4), ALU.add, (Hst_b, ps_b), (Hst_b,))
                cp("act", Hb[:], Hst[:], (Hst_b,), (Hb_b,))
                yield

        if RCUT <= 5:
            rwkv_fill(); return
        for cc in range(4):
            ps, ps_b = bankA()
            mm(ps[:, 0:T], bones_f[:], yR[:, cc, :], True, True, (yR_b, kconst_b), (ps_b,))
            tt("dve", yc, yR[:, cc, :], ps[:, 0:T], ALU.subtract, (yR_b, ps_b), (yc_b,))
            act(sqk[:], yc, AF.Square, (yc_b,), (sqk_b,))
            ps, ps_b = bankA()
            mm(ps[:, 0:T], bones_bf[:], sqk[:], True, True, (sqk_b, kconst_b), (ps_b,))
            rsqrt_ps(ps, ps_b, 64 * 64e-5, dst=RT["rn"][0], dst_b=rn_b)
            tt("dve", yc, yc, rn, ALU.mult, (yc_b, rn_b), (yc_b,))
            act(yn, yc, AF.Identity, (yc_b,) + CONST, (yn_b,), bias=vt[:, V_LNB + cc:V_LNB + cc + 1],
                scale=drv[:, DR_LNW8 + cc:DR_LNW8 + cc + 1])
            tt("pool", yn, yn, bon[:, cc, :], ALU.add, (yn_b,) + bon_b[cc], (yn_b,))
            tt("dve", yT[:, 4 + cc, :], yn, gG[:, cc, :], ALU.mult, (yn_b,) + gG_b[cc], (yT_b[4 + cc],))
            yield

    def outproj():
        for g in range(4):
            w, w_b = load_win("wout", g * 256, 256)
            for jj in range(2):
                c = 2 * g + jj
                ps, ps_b = bankA()
                for kc in range(8):
                    mm(ps[:, 0:T], w[:, kc, jj * 128:(jj + 1) * 128], yT[:, kc, :], kc == 0, kc == 7,
                       (w_b, yT_b[kc]), (ps_b,))
                stt("dve", hT[:, c, :], ps[:, 0:T], modT[:, GT2 + c:GT2 + c + 1], hT[:, c, :], ALU.mult, ALU.add,
                    (ps_b, hT_b[c]) + CONST, (hT_b[c],))

    xslot = [0]
    oslot = [0]
    for it in range(NT):
        t0 = it * T
        for tb in range(NTB):
            s = 0
            dma("sp", xin[s][:], x_d[t0 + tb * 128:t0 + (tb + 1) * 128, :], (), (xin_b[s],), "xin0")
            for half in range(2):
                ps, ps_b = bankA()
                for c4 in range(4):
                    c = half * 4 + c4
                    tr(ps[:, c4 * 128:(c4 + 1) * 128], xin[s][:, c * 128:(c + 1) * 128], ident_f,
                       (xin_b[s], cst_b), (ps_b,))
                cp("act", hT[:, half * 4:half * 4 + 4, tb * 128:(tb + 1) * 128],
                   ps[:].rearrange("p (c t) -> p c t", c=4), (ps_b,), tuple(hT_b[half * 4:half * 4 + 4]))

        if "ffn1" in stages:
            norm_mod(lambda c: drv[:, DR_GM1 + c:DR_GM1 + c + 1], lambda c: modT[:, SH1 + c:SH1 + c + 1], uT, uT_b)
            ffn("w1g", "w1u", "w1d", DR_CG1)

        if "mix" in stages:
            norm_mod(lambda c: drv[:, DR_GM2 + c:DR_GM2 + c + 1], lambda c: modT[:, SH2 + c:SH2 + c + 1], uT, uT_b)
            if "nomla" in stages:
                for c in range(4):
                    memset("pool", yT[:, c, :], 0.0, (yT_b[c],))
            else:
                gen = None if "norwkv" in stages else rwkv(it, t0)
                nper = max(1, -(-RWKV_STEPS // (8 * (it + 1))))

                def step(gen=gen, nper=nper):
                    if gen is None:
                        return
                    for _ in range(nper):
                        next(gen, None)

                mla(it, t0, step)
                if gen is not None:
                    for _ in gen:
                        pass
            if "norwkv" in stages:
                for c in range(4, 8):
                    memset("pool", yT[:, c, :], 0.0, (yT_b[c],))
            elif "nomla" in stages:
                for _ in rwkv(it, t0):
                    pass
            outproj()

        if "ffn2" in stages:
            norm_mod(lambda c: drv[:, DR_GM3 + c:DR_GM3 + c + 1], lambda c: modT[:, SH3 + c:SH3 + c + 1], uT, uT_b)
            ffn("w3g", "w3u", "w3d", DR_CG3)

        for c in range(8):
            act(sqb[:, c, :], hT[:, c, :], AF.Square, (hT_b[c],), (sqb_b[c],))
        ps, ps_b = bankA()
        for c in range(8):
            mm(ps[:, 0:T], ones_bf[:], sqb[:, c, :], c == 0, c == 7, (sqb_b[c], kconst_b), (ps_b,))
        rsqrt_ps(ps, ps_b, D * 1e-6)
        for c in range(8):
            stt("dve", hT[:, c, :], hT[:, c, :], drv[:, DR_GF + c:DR_GF + c + 1], tmpA[:], ALU.mult, ALU.mult,
                (hT_b[c], tmpA_b) + CONST, (hT_b[c],))
        for tb in range(NTB):
            s = 0
            for half in range(2):
                ps, ps_b = bankA()
                for c4 in range(4):
                    c = half * 4 + c4
                    tr(ps[:, c4 * 128:(c4 + 1) * 128], hT[:, c, tb * 128:(tb + 1) * 128], ident_f,
                       (hT_b[c], cst_b), (ps_b,))
                cp("act" if half == 0 else "dve", ost[s][:, half * 512:(half + 1) * 512], ps[:], (ps_b,), (ost_b[s],))
            dma("pool", out_d[t0 + tb * 128:t0 + (tb + 1) * 128, :], ost[s][:], (ost_b[s],), (), "xin0o")

    P.emit()

    if os.environ.get("KDBG"):
        print("STATS", P.stats)
    es.close()
    return nc


def _consts():
    c = np.zeros((128, NCST), np.float32)
    p = np.arange(128)
    c[:, C_ID:C_ID + 128] = np.eye(128, dtype=np.float32)
    same = (p[:, None] // 64) == (p[None, :] // 64)
    c[:, C_ML:C_ML + 128] = (same & (p[None, :] < p[:, None])).astype(np.float32)
    c[:, C_MU:C_MU + 128] = (same & (p[:, None] < p[None, :])).astype(np.float32)
    c[:, C_MUI:C_MUI + 128] = (same & (p[:, None] <= p[None, :])).astype(np.float32)
    c[:, C_BO:C_BO + 128] = same.astype(np.float32)
    c[:, C_INVF] = (10000.0 ** (-(np.arange(128) % 32).astype(np.float32) / 32.0)).astype(np.float32)
    c[:, C_SS] = np.where((p % 64) < 32, -1.0, 1.0)
    c[:, C_RST:C_RST + 512] = ((np.arange(512) % 64) != 0).astype(np.float32)[None, :]
    return c


def _vt(inp, b):
    def ch(v):
        v = np.asarray(v, np.float32).reshape(-1, 128)
        return v.T
    sm = inp["rwkv_shift_mix"][0]
    cols = [
        ch(inp["c"][b]), ch(inp["b_mod"][0]), ch(inp["ffn1_norm_g"][0]), ch(inp["mix_norm_g"][0]),
        ch(inp["ffn2_norm_g"][0]), ch(inp["final_norm_g"]), ch(inp["q_norm_g"][0]), ch(inp["kv_norm_g"][0]),
        ch(sm[0:512]), ch(sm[512:1024]), ch(sm[1024:1536]), ch(sm[1536:1664]), ch(sm[1664:1792]),
        ch(inp["rwkv_w0"][0]), ch(inp["rwkv_a0"][0]), ch(inp["rwkv_k_k"][0]), ch(inp["rwkv_k_a"][0]),
        ch(inp["rwkv_r_k"][0].reshape(-1)), ch(inp["rwkv_ln_w"][0]), ch(inp["rwkv_ln_b"][0]),
    ]
    v = np.concatenate(cols, axis=1)
    assert v.shape == (128, NV), v.shape
    return np.ascontiguousarray(v, np.float32)


def make_in_maps(inp, S, cores):
    f = lambda a: np.ascontiguousarray(np.asarray(a, np.float32))
    shared = {
        "cst": _consts(),
        "gat": np.ascontiguousarray(np.tile(f(inp["attn_out_norm_g"][0])[None, :], (128, 1))),
        "w_mod": f(inp["w_mod"][0]),
        "w1g": f(inp["ffn1_w_gate"][0]), "w1u": f(inp["ffn1_w_up"][0]), "w1d": f(inp["ffn1_w_down"][0]),
        "win": f(inp["w_in"][0]), "wuq": f(inp["w_uq"][0]), "wukv": f(inp["w_ukv"][0]),
        "w2": f(inp["rwkv_w2"][0]), "a2": f(inp["rwkv_a2"][0]), "g2": f(inp["rwkv_g2"][0]),
        "wout": f(inp["w_out"][0]),
        "w3g": f(inp["ffn2_w_gate"][0]), "w3u": f(inp["ffn2_w_up"][0]), "w3d": f(inp["ffn2_w_down"][0]),
    }
    maps = []
    for b in cores:
        m = dict(shared)
        m["x"] = f(inp["x"][b][:S])
        m["pos"] = np.ascontiguousarray(np.tile(np.asarray(inp["positions"][b][:S], np.int32)[None, :], (64, 1)))
        m["vt"] = _vt(inp, b)
        maps.append(m)
    return maps


def kernel(**inputs):
    S = inputs["x"].shape[1]
    B = inputs["x"].shape[0]
    nc = build_nc(S)
    in_maps = make_in_maps(inputs, S, list(range(B)))
    res = run_bass_kernel_spmd(nc, in_maps, core_ids=list(range(B)))
    return np.stack([np.asarray(r["out"], np.float32) for r in res.results], axis=0)
```
